# Optimizing a Trainium2 kernel written in Bass

```python
import math
import jax, jax.numpy as jnp
from jax import lax
import numpy as np

D_MODEL = 1024
BATCH = 4
SEQ = 4096
DEPTH = 2

GRID_W = 64
HEAD_DIM = 64
BRANCH_WIDTH = 512
N_BRANCHES = 4
FOURIER_GROUPS = 8
FOURIER_GROUP_DIM = BRANCH_WIDTH // FOURIER_GROUPS
GQA_Q_HEADS = 8
GQA_KV_HEADS = 2
MLSTM_HEADS = 4
MLSTM_HEAD_DIM = BRANCH_WIDTH // MLSTM_HEADS
MLSTM_CHUNK = 128
DIFF_HEADS = 4
DIFF_QK_DIM = 64
DIFF_V_DIM = 2 * DIFF_QK_DIM
REL_BUCKETS = 32
REL_MAX_DIST = 128
Q_BLOCK = 128
D_FF = 2816
CONV_W = 3
ROPE_BASE = 10000.0
EPS = 1e-6

A_W = BRANCH_WIDTH
B_Q_W = GQA_Q_HEADS * HEAD_DIM
B_KV_W = GQA_KV_HEADS * HEAD_DIM
C_W = BRANCH_WIDTH
C_GATE_W = 4 * MLSTM_HEADS
D_QK_W = DIFF_HEADS * 2 * DIFF_QK_DIM
D_V_W = DIFF_HEADS * DIFF_V_DIM
GATE_W = N_BRANCHES * D_MODEL
SPLIT_WIDTHS = (A_W, B_Q_W, B_KV_W, B_KV_W, C_W, C_W, C_W, C_W, C_GATE_W, D_QK_W, D_QK_W, D_V_W, GATE_W)
IN_WIDTH = A_W + B_Q_W + 2 * B_KV_W + 4 * C_W + C_GATE_W + 2 * D_QK_W + D_V_W + GATE_W

kernel_name = 'hybrid_fourier_gqa_mlstm_diffattn_encoder'

F32 = jnp.float32


def rms_norm(x, g):
    xf = x.astype(F32)
    y = xf * lax.rsqrt(jnp.mean(xf * xf, axis=-1, keepdims=True) + EPS)
    return (y * g.astype(F32)).astype(x.dtype)


def split_cols(p):
    points = [int(v) for v in np.cumsum(SPLIT_WIDTHS)[:-1]]
    return jnp.split(p, points, axis=-1)


def fourier_mix(a):
    b, s, _ = a.shape
    g = a.reshape(b, s, FOURIER_GROUPS, FOURIER_GROUP_DIM).transpose(0, 2, 1, 3).astype(F32)
    f = jnp.real(jnp.fft.fft2(g, norm='ortho'))
    return f.transpose(0, 2, 1, 3).reshape(b, s, BRANCH_WIDTH).astype(a.dtype)


def axial_rope_tables(seq):
    rows = seq // GRID_W
    row_id = jnp.repeat(jnp.arange(rows, dtype=F32), GRID_W)
    col_id = jnp.tile(jnp.arange(GRID_W, dtype=F32), rows)
    n_pairs = HEAD_DIM // 4
    inv_freq = ROPE_BASE ** (-jnp.arange(n_pairs, dtype=F32) / n_pairs)
    ang = jnp.concatenate([row_id[:, None] * inv_freq, col_id[:, None] * inv_freq], axis=-1)
    return jnp.cos(ang), jnp.sin(ang)


def apply_rope(x, cos, sin):
    xf = x.astype(F32)
    x1, x2 = jnp.split(xf, 2, axis=-1)
    c = cos[None, :, None, :]
    s = sin[None, :, None, :]
    return jnp.concatenate([x1 * c - x2 * s, x1 * s + x2 * c], axis=-1).astype(x.dtype)


def gqa_attention(q, k, v):
    b, s, hq, d = q.shape
    grp = hq // GQA_KV_HEADS
    nb = s // Q_BLOCK
    scale = d ** -0.5
    qb = q.reshape(b, nb, Q_BLOCK, GQA_KV_HEADS, grp, d).transpose(1, 0, 2, 3, 4, 5)

    def one_block(qblk):
        sc = jnp.einsum('blkgd,bskd->bkgls', qblk, k, preferred_element_type=F32) * scale
        p = jax.nn.softmax(sc, axis=-1).astype(v.dtype)
        return jnp.einsum('bkgls,bskd->blkgd', p, v)

    out = lax.map(one_block, qb)
    return out.transpose(1, 0, 2, 3, 4, 5).reshape(b, s, hq * d)


def gqa_branch(q, k, v, qk_g, cos, sin):
    b, s, _ = q.shape
    q = q.reshape(b, s, GQA_Q_HEADS, HEAD_DIM)
    k = k.reshape(b, s, GQA_KV_HEADS, HEAD_DIM)
    v = v.reshape(b, s, GQA_KV_HEADS, HEAD_DIM)
    q = apply_rope(rms_norm(q, qk_g[0]), cos, sin)
    k = apply_rope(rms_norm(k, qk_g[1]), cos, sin)
    return gqa_attention(q, k, v)


def mlstm_scan(q, k, v, i_pre, f_pre):
    b, h, s, d = q.shape
    L = MLSTM_CHUNK
    nc = s // L
    k = k * d ** -0.5
    logf = jax.nn.log_sigmoid(f_pre)

    def chunks(t):
        return jnp.moveaxis(t.reshape(b, h, nc, L, *t.shape[3:]), 2, 0)

    xs = (chunks(q), chunks(k), chunks(v), chunks(i_pre), chunks(logf))
    lower = jnp.tril(jnp.ones((L, L), dtype=bool))

    def step(carry, inp):
        C, n, m = carry
        qt, kt, vt, it, ft = inp
        bcum = jnp.cumsum(ft, axis=-1)
        dmat = bcum[..., :, None] - bcum[..., None, :] + it[..., None, :]
        dmat = jnp.where(lower, dmat, -jnp.inf)
        inter = bcum + m[..., None]
        m_t = jnp.maximum(inter, jnp.max(dmat, axis=-1))
        w_intra = jnp.exp(dmat - m_t[..., None])
        w_inter = jnp.exp(inter - m_t)
        s_qk = jnp.einsum('bhld,bhsd->bhls', qt, kt) * w_intra
        num = w_inter[..., None] * jnp.einsum('bhld,bhde->bhle', qt, C) + jnp.einsum('bhls,bhse->bhle', s_qk, vt)
        den = w_inter * jnp.einsum('bhld,bhd->bhl', qt, n) + jnp.sum(s_qk, axis=-1)
        h_out = num / jnp.maximum(jnp.abs(den), jnp.exp(-m_t))[..., None]
        btot = bcum[..., -1]
        g_s = btot[..., None] - bcum + it
        m_new = jnp.maximum(btot + m, jnp.max(g_s, axis=-1))
        decay = jnp.exp(btot + m - m_new)
        w_s = jnp.exp(g_s - m_new[..., None])
        C_new = decay[..., None, None] * C + jnp.einsum('bhs,bhsd,bhse->bhde', w_s, kt, vt)
        n_new = decay[..., None] * n + jnp.einsum('bhs,bhsd->bhd', w_s, kt)
        return (C_new, n_new, m_new), h_out

    init = (jnp.zeros((b, h, d, d), F32), jnp.zeros((b, h, d), F32), jnp.zeros((b, h), F32))
    _, hs = lax.scan(step, init, xs)
    return jnp.moveaxis(hs, 0, 2).reshape(b, h, s, d)


def mlstm_branch(q, k, v, o, gate_pre, gate_bias, norm_g):
    b, s, _ = q.shape

    def heads(t):
        return t.reshape(b, s, MLSTM_HEADS, MLSTM_HEAD_DIM).transpose(0, 2, 1, 3).astype(F32)

    qh, kh, vh = heads(q), heads(k), heads(v)
    gp = gate_pre.reshape(b, s, 4, MLSTM_HEADS).astype(F32) + gate_bias.astype(F32)
    gp = gp.transpose(2, 0, 3, 1)
    h_fwd = mlstm_scan(qh, kh, vh, gp[0], gp[1])
    rev = lambda t: jnp.flip(t, axis=2)
    h_bwd = rev(mlstm_scan(rev(qh), rev(kh), rev(vh), jnp.flip(gp[2], axis=-1), jnp.flip(gp[3], axis=-1)))
    hsum = (h_fwd + h_bwd).transpose(0, 2, 1, 3)
    hsum = rms_norm(hsum, norm_g.reshape(MLSTM_HEADS, MLSTM_HEAD_DIM))
    return (jax.nn.sigmoid(o.astype(F32)) * hsum.reshape(b, s, BRANCH_WIDTH)).astype(q.dtype)


def rel_bucket(rel):
    half = REL_BUCKETS // 2
    max_exact = half // 2
    ret = jnp.where(rel > 0, half, 0)
    n = jnp.abs(rel)
    nf = jnp.maximum(n, 1).astype(F32)
    large = max_exact + (jnp.log(nf / max_exact) / math.log(REL_MAX_DIST / max_exact) * (half - max_exact)).astype(jnp.int32)
    large = jnp.minimum(large, half - 1)
    return ret + jnp.where(n < max_exact, n, large)


def diff_attention(q, k, v, lam, rel_bias):
    b, s = q.shape[:2]
    nb = s // Q_BLOCK
    scale = DIFF_QK_DIM ** -0.5
    qb = q.reshape(b, nb, Q_BLOCK, DIFF_HEADS, 2, DIFF_QK_DIM).transpose(1, 0, 2, 3, 4, 5)
    starts = jnp.arange(nb, dtype=jnp.int32) * Q_BLOCK
    kpos = jnp.arange(s, dtype=jnp.int32)

    def one_block(args):
        qblk, start = args
        qpos = start + jnp.arange(Q_BLOCK, dtype=jnp.int32)
        bias = rel_bias[rel_bucket(kpos[None, :] - qpos[:, None])].astype(F32).transpose(2, 0, 1)
        sc = jnp.einsum('blhmd,bshmd->bhmls', qblk, k, preferred_element_type=F32) * scale + bias[None, :, None]
        p = jax.nn.softmax(sc, axis=-1)
        w = (p[:, :, 0] - lam * p[:, :, 1]).astype(v.dtype)
        return jnp.einsum('bhls,bshd->blhd', w, v)

    out = lax.map(one_block, (qb, starts))
    return out.transpose(1, 0, 2, 3, 4).reshape(b, s, DIFF_HEADS, DIFF_V_DIM)


def diff_branch(q, k, v, lam_params, sub_g, rel_bias, layer_number):
    b, s, _ = q.shape
    q = q.reshape(b, s, DIFF_HEADS, 2, DIFF_QK_DIM)
    k = k.reshape(b, s, DIFF_HEADS, 2, DIFF_QK_DIM)
    v = v.reshape(b, s, DIFF_HEADS, DIFF_V_DIM)
    lam_init = 0.8 - 0.6 * math.exp(-0.3 * (layer_number - 1))
    lp = lam_params.astype(F32)
    lam = jnp.exp(jnp.sum(lp[0] * lp[1])) - jnp.exp(jnp.sum(lp[2] * lp[3])) + lam_init
    o = diff_attention(q, k, v, lam, rel_bias)
    o = rms_norm(o, sub_g) * (1.0 - lam_init)
    return o.reshape(b, s, BRANCH_WIDTH)


def conv_ffn(v, w_up, conv_w, conv_b, w_down):
    s = v.shape[1]
    up = v @ w_up
    a, lin = jnp.split(up, 2, axis=-1)
    pad = CONV_W // 2
    ap = jnp.pad(a, ((0, 0), (pad, pad), (0, 0)))
    a = sum(ap[:, j:j + s] * conv_w[j] for j in range(CONV_W)) + conv_b
    return (jax.nn.gelu(a) * lin) @ w_down


def setup_inputs(seed: int = 0) -> dict:
    key = jax.random.key(seed)
    ks = jax.random.split(key, 17)
    nrm = lambda k, shape: jax.random.normal(k, shape, F32)
    gate_base = jnp.array([0.0, 3.0, 0.0, 3.0], F32)[None, :, None]
    return {
        'x': nrm(ks[0], (BATCH, SEQ, D_MODEL)),
        'norm_mix_g': 1.0 + 0.02 * nrm(ks[1], (DEPTH, D_MODEL)),
        'w_in': nrm(ks[2], (DEPTH, D_MODEL, IN_WIDTH)) * D_MODEL ** -0.5,
        'mlstm_gate_bias': gate_base + 0.1 * nrm(ks[3], (DEPTH, 4, MLSTM_HEADS)),
        'qk_norm_g': 1.0 + 0.02 * nrm(ks[4], (DEPTH, 2, HEAD_DIM)),
        'mlstm_norm_g': 1.0 + 0.02 * nrm(ks[5], (DEPTH, BRANCH_WIDTH)),
        'diff_lambda': 0.1 * nrm(ks[6], (DEPTH, 4, DIFF_QK_DIM)),
        'diff_norm_g': 1.0 + 0.02 * nrm(ks[7], (DEPTH, DIFF_V_DIM)),
        'rel_bias': 0.5 * nrm(ks[8], (REL_BUCKETS, DIFF_HEADS)),
        'w_branch': nrm(ks[9], (DEPTH, N_BRANCHES, BRANCH_WIDTH, D_MODEL)) * BRANCH_WIDTH ** -0.5,
        'w_out': nrm(ks[10], (DEPTH, D_MODEL, D_MODEL)) * D_MODEL ** -0.5,
        'norm_ffn_g': 1.0 + 0.02 * nrm(ks[11], (DEPTH, D_MODEL)),
        'w_up': nrm(ks[12], (DEPTH, D_MODEL, 2 * D_FF)) * D_MODEL ** -0.5,
        'conv_w': nrm(ks[13], (DEPTH, CONV_W, D_FF)) * CONV_W ** -0.5,
        'conv_b': 0.02 * nrm(ks[14], (DEPTH, D_FF)),
        'w_down': nrm(ks[15], (DEPTH, D_FF, D_MODEL)) * D_FF ** -0.5,
        'final_norm_g': 1.0 + 0.02 * nrm(ks[16], (D_MODEL,)),
    }


def reference(x, norm_mix_g, w_in, mlstm_gate_bias, qk_norm_g, mlstm_norm_g, diff_lambda, diff_norm_g, rel_bias, w_branch, w_out, norm_ffn_g, w_up, conv_w, conv_b, w_down, final_norm_g):
    b, s, _ = x.shape
    cos, sin = axial_rope_tables(s)
    for layer in range(DEPTH):
        u = rms_norm(x, norm_mix_g[layer])
        proj = jnp.einsum('bsd,de->bse', u, w_in[layer])
        (a_in, bq, bk, bv, cq, ck, cv, co, cgate, dq, dk, dv, gates) = split_cols(proj)
        y_a = fourier_mix(a_in)
        y_b = gqa_branch(bq, bk, bv, qk_norm_g[layer], cos, sin)
        y_c = mlstm_branch(cq, ck, cv, co, cgate, mlstm_gate_bias[layer], mlstm_norm_g[layer])
        y_d = diff_branch(dq, dk, dv, diff_lambda[layer], diff_norm_g[layer], rel_bias, layer + 1)
        ys = jnp.stack([y_a, y_b, y_c, y_d], axis=2)
        branch = jnp.einsum('bsnc,ncd->bsnd', ys, w_branch[layer])
        g = jax.nn.sigmoid(gates.reshape(b, s, N_BRANCHES, D_MODEL))
        merged = jnp.sum(g * branch, axis=2)
        x = x + merged @ w_out[layer]
        v = rms_norm(x, norm_ffn_g[layer])
        x = x + conv_ffn(v, w_up[layer], conv_w[layer], conv_b[layer], w_down[layer])
    return rms_norm(x, final_norm_g)
```

```python
import math
import contextlib
import numpy as np
import ml_dtypes
import concourse.bass as bass
import concourse.mybir as mybir
from concourse.bass_utils import run_bass_kernel_spmd

F32 = mybir.dt.float32
BF16 = mybir.dt.bfloat16
AF = mybir.ActivationFunctionType
ALU = mybir.AluOpType
AX = mybir.AxisListType

D = 1024
DEPTH = 2
BATCH = 4
SEQ = 4096
IN_W = 8976
DFF = 2816
EPS = 1e-6
COL = dict(a=0, bq=512, bk=1024, bv=1152, cq=1280, ck=1792, cv=2304, co=2816, cg=3328,
           dq=3344, dk=3856, dv=4368, g=4880)

COMPUTE = ("pe", "act", "dve", "pool")
NDMASEM = 8


class Op:
    __slots__ = ("eng", "fn", "deps", "signal", "ev", "is_dma")

    def __init__(self, eng, fn, is_dma=False):
        self.eng = eng
        self.fn = fn
        self.deps = set()
        self.signal = False
        self.ev = None
        self.is_dma = is_dma


class Prog:
    def __init__(self, nc, same_engine_sync=True):
        self.nc = nc
        self.ops = []
        self.last_w = {}
        self.readers = {}
        self.same_engine_sync = same_engine_sync
        self.engines = {"pe": nc.tensor, "act": nc.scalar, "dve": nc.vector,
                        "pool": nc.gpsimd, "sp": nc.sync}
        self.since_barrier = []
        self.barrier_op = None

    def op(self, eng, fn, r=(), w=(), dma=False):
        o = Op(eng, fn, is_dma=dma)
        idx = len(self.ops)
        if self.barrier_op is not None:
            o.deps.add(self.barrier_op)
        for k in r:
            lw = self.last_w.get(k)
            if lw is not None:
                o.deps.add(lw)
        for k in w:
            lw = self.last_w.get(k)
            if lw is not None:
                o.deps.add(lw)
            for rd in self.readers.get(k, ()):
                o.deps.add(rd)
        for k in r:
            self.readers.setdefault(k, []).append(idx)
        for k in w:
            self.last_w[k] = idx
            self.readers[k] = []
        self.ops.append(o)
        self.since_barrier.append(idx)
        return idx

    def barrier(self, scratch):
        prev = list(self.since_barrier)
        o = Op("dve", lambda e: e.memset(scratch, 0.0))
        if self.barrier_op is not None:
            o.deps.add(self.barrier_op)
        o.deps.update(prev)
        idx = len(self.ops)
        self.ops.append(o)
        self.barrier_op = idx
        self.since_barrier = []
        self.last_w = {}
        self.readers = {}
        return idx

    def pe(self, fn, r=(), w=()):
        return self.op("pe", fn, r, w)

    def act(self, fn, r=(), w=()):
        return self.op("act", fn, r, w)

    def dve(self, fn, r=(), w=()):
        return self.op("dve", fn, r, w)

    def pool(self, fn, r=(), w=()):
        return self.op("pool", fn, r, w)

    def dma(self, q, out, in_, r=(), w=(), **kw):
        return self.op(q, lambda e: e.dma_start(out=out, in_=in_, **kw), r, w, dma=True)

    def emit(self, final_wait_ops=()):
        nc = self.nc
        ops = self.ops
        for i, o in enumerate(ops):
            keep = set()
            for d in o.deps:
                p = ops[d]
                if p.is_dma:
                    keep.add(d)
                    continue
                if p.eng == o.eng and not o.is_dma:
                    if o.eng == "pe":
                        continue
                    if not self.same_engine_sync:
                        continue
                keep.add(d)
            latest = {}
            keep2 = set()
            for d in keep:
                p = ops[d]
                if p.is_dma:
                    keep2.add(d)
                else:
                    if p.eng not in latest or d > latest[p.eng]:
                        latest[p.eng] = d
            keep2.update(latest.values())
            o.deps = keep2
            for d in keep2:
                ops[d].signal = True
        for d in final_wait_ops:
            ops[d].signal = True
        es = contextlib.ExitStack()
        sems = {}
        for e in COMPUTE:
            sems[e] = es.enter_context(nc.semaphore("s_" + e))
        dsems = {}
        for q in ("sp", "pool", "act"):
            dsems[q] = [es.enter_context(nc.semaphore("d_%s%d" % (q, j))) for j in range(NDMASEM)]
        cnt = {e: 0 for e in COMPUTE}
        dcnt = {q: 0 for q in dsems}
        seen = {e: {} for e in self.engines}
        nwait = 0
        plan = {e: [] for e in self.engines}

        def need(engname, ev, waits):
            nonlocal nwait
            sem, val = ev
            s = seen[engname]
            if s.get(id(sem), 0) >= val:
                return
            s[id(sem)] = val
            waits.append((sem, val))
            nwait += 1

        for i, o in enumerate(ops):
            waits = []
            for d in sorted(o.deps):
                need(o.eng, ops[d].ev, waits)
            if o.is_dma:
                q = o.eng
                j = dcnt[q]
                dcnt[q] += 1
                sem = dsems[q][j % NDMASEM]
                if j >= NDMASEM:
                    need(q, (sem, 16 * (j // NDMASEM)), waits)
                o.ev = (sem, 16 * (j // NDMASEM + 1))
                plan[o.eng].append((waits, o, sem, 16))
            else:
                if o.signal:
                    cnt[o.eng] += 1
                    o.ev = (sems[o.eng], cnt[o.eng])
                    plan[o.eng].append((waits, o, sems[o.eng], 1))
                else:
                    plan[o.eng].append((waits, o, None, 0))
        fw = []
        for d in final_wait_ops:
            need("sp", ops[d].ev, fw)
        plan["sp"].append((fw, None, None, 0))

        def run_engine(name, e):
            for waits, o, sem, inc in plan[name]:
                for (ws, wv) in waits:
                    e.wait_ge(ws, wv)
                if o is None:
                    continue
                ins = o.fn(e)
                if sem is not None:
                    ins.then_inc(sem, inc)

        with nc.Block() as block:
            @block.sync
            def _(e):
                run_engine("sp", e)

            @block.tensor
            def _(e):
                run_engine("pe", e)

            @block.scalar
            def _(e):
                run_engine("act", e)

            @block.vector
            def _(e):
                run_engine("dve", e)

            @block.gpsimd
            def _(e):
                run_engine("pool", e)
        self.stats = dict(n_ops=len(ops), n_wait=nwait, cnt=dict(cnt), dcnt=dict(dcnt))
        es.close()
        return self.stats


_UNIQ = [0]


def _uniq(n):
    _UNIQ[0] += 1
    return "%s_%d" % (n, _UNIQ[0])


class Ring:
    def __init__(self, aps, name):
        self.aps = aps
        self.name = name
        self.i = 0

    def next(self):
        j = self.i % len(self.aps)
        self.i += 1
        return self.aps[j], (self.name, j)


def rel_bucket_np(rel):
    half = 16
    max_exact = 8
    ret = np.where(rel > 0, half, 0)
    n = np.abs(rel)
    nf = np.maximum(n, 1).astype(np.float32)
    large = max_exact + (np.log(nf / np.float32(max_exact)) / np.float32(math.log(128 / max_exact))
                         * np.float32(half - max_exact)).astype(np.int32)
    large = np.minimum(large, half - 1)
    return ret + np.where(n < max_exact, n, large)


def host_consts(S):
    c = {}
    k = np.arange(S, dtype=np.int64)
    ks = (k[:, None] * k[None, :]) % S
    ang = (2.0 * np.pi / S) * ks.astype(np.float64)
    c["dftc"] = np.cos(ang).astype(np.float32).astype(ml_dtypes.bfloat16)
    c["dfts"] = (-np.sin(ang)).astype(np.float32).astype(ml_dtypes.bfloat16)
    j = np.arange(64)
    a64 = 2.0 * np.pi * ((j[:, None] * j[None, :]) % 64) / 64.0
    nrm = 1.0 / math.sqrt(S * 64.0)
    bd = np.zeros((2, 128, 128), np.float64)
    for g in range(2):
        bd[0, g * 64:(g + 1) * 64, g * 64:(g + 1) * 64] = np.cos(a64) * nrm
        bd[1, g * 64:(g + 1) * 64, g * 64:(g + 1) * 64] = np.sin(a64) * nrm
    c["bdcs"] = bd.astype(np.float32).astype(ml_dtypes.bfloat16)
    rows = S // 64
    row_id = np.repeat(np.arange(rows, dtype=np.float32), 64)
    col_id = np.tile(np.arange(64, dtype=np.float32), rows)
    inv = (np.float32(10000.0) ** (-np.arange(16, dtype=np.float32) / np.float32(16))).astype(np.float32)
    angr = np.concatenate([row_id[:, None] * inv, col_id[:, None] * inv], axis=-1).astype(np.float32)
    c["ropec"] = np.cos(angr).astype(np.float32)
    c["ropes"] = np.sin(angr).astype(np.float32)
    i = np.arange(128)[:, None]
    m = np.arange(1152)[None, :]
    c["relf"] = (i - m + 512).astype(np.float32)
    return c


def bias_steps():
    rel = np.arange(-700, 701)
    b = rel_bucket_np(rel)
    seq = [int(b[0])]
    thr = []
    for idx in range(1, len(rel)):
        if b[idx] != b[idx - 1]:
            seq.append(int(b[idx]))
            thr.append(int(rel[idx]))
    return seq, thr


class Ctx:
    pass


def build(S, depth, stage="full", taps=()):
    NT = S // 128
    NQC = S // 512
    nc = bass.Bass("TRN2", target_bir_lowering=False)
    P = Prog(nc)
    C = Ctx()
    C.nc, C.P, C.S, C.NT, C.NQC = nc, P, S, NT, NQC

    def din(name, shape, dt=F32):
        return nc.dram_tensor(name, list(shape), dt, kind="ExternalInput").ap()

    I = {}
    I["x"] = din("x", [S, D])
    I["norm_mix_g"] = din("norm_mix_g", [DEPTH, D])
    I["w_in"] = din("w_in", [DEPTH, D, IN_W])
    I["mlstm_gate_bias"] = din("mlstm_gate_bias", [DEPTH, 16])
    I["qk_norm_g"] = din("qk_norm_g", [DEPTH, 128])
    I["mlstm_norm_g"] = din("mlstm_norm_g", [DEPTH, 512])
    I["diff_lambda"] = din("diff_lambda", [DEPTH, 256])
    I["diff_norm_g"] = din("diff_norm_g", [DEPTH, 128])
    I["rel_bias"] = din("rel_bias", [1, 128])
    I["w_branch"] = din("w_branch", [DEPTH, 4, 512, D])
    I["w_out"] = din("w_out", [DEPTH, D, D])
    I["norm_ffn_g"] = din("norm_ffn_g", [DEPTH, D])
    I["w_up"] = din("w_up", [DEPTH, D, 2 * DFF])
    I["conv_w"] = din("conv_w", [DEPTH, 3, DFF])
    I["conv_b"] = din("conv_b", [DEPTH, DFF])
    I["w_down"] = din("w_down", [DEPTH, DFF, D])
    I["final_norm_g"] = din("final_norm_g", [1, D])
    I["dftc"] = din("dftc", [S, S], BF16)
    I["dfts"] = din("dfts", [S, S], BF16)
    I["bdcs"] = din("bdcs", [2, 128, 128], BF16)
    I["ropec"] = din("ropec", [S, 32])
    I["ropes"] = din("ropes", [S, 32])
    I["relf"] = din("relf", [128, 1152])
    C.I = I
    out = nc.dram_tensor("out", [S, D], F32, kind="ExternalOutput").ap()
    C.tap = {}
    for (nm, shp, dt) in taps:
        C.tap[nm] = nc.dram_tensor("tap_" + nm, list(shp), dt, kind="ExternalOutput").ap()

    def dscr(name, shape, dt=BF16):
        return nc.dram_tensor(name, list(shape), dt).ap()

    C.xres = dscr("xres", [S, D], F32)
    C.uT_d = dscr("uT_d", [D, S])
    C.ynT = [dscr("ynT%d" % n, [512, S]) for n in range(4)]

    ges = contextlib.ExitStack()
    C.ges = ges

    def gsb(name, shape, dt):
        return ges.enter_context(nc.sbuf_tensor(name, list(shape), dt))

    C.psb = [ges.enter_context(nc.psum_tensor("psb%d" % i, [128, 1024], F32)) for i in range(3)]
    C.pst = [ges.enter_context(nc.psum_tensor("pst%d" % i, [128, 1024], BF16)) for i in range(2)]
    C.identf = gsb("identf", [128, 128], F32)
    C.ident = gsb("ident", [128, 128], BF16)
    C.maskL = gsb("maskL", [128, 128], F32)
    C.maskU = gsb("maskU", [128, 128], F32)
    C.onesf = gsb("onesf", [128, 128], F32)
    C.maskLb = gsb("maskLb", [128, 128], BF16)
    C.maskUb = gsb("maskUb", [128, 128], BF16)
    C.onesb = gsb("onesb", [128, 128], BF16)
    C.bar = gsb("bar", [128, 1], F32)
    C.rbB = gsb("rbB", [128, 128], F32)
    C.epsb = gsb("epsb", [128, 1], F32)
    C.lnk = gsb("lnk", [128, 1], F32)

    P.pool(lambda e: e.iota(C.identf[:], [[1, 128]], 0, channel_multiplier=-1,
                            allow_small_or_imprecise_dtypes=True), w=["identf"])
    P.dve(lambda e: e.tensor_single_scalar(out=C.ident[:], in_=C.identf[:], scalar=0.0, op=ALU.is_equal),
          r=["identf"], w=["ident"])
    P.dve(lambda e: e.tensor_single_scalar(out=C.maskL[:], in_=C.identf[:], scalar=0.0, op=ALU.is_ge),
          r=["identf"], w=["maskL"])
    P.dve(lambda e: e.tensor_single_scalar(out=C.maskU[:], in_=C.identf[:], scalar=0.0, op=ALU.is_le),
          r=["identf"], w=["maskU"])
    P.dve(lambda e: e.memset(C.onesf[:], 1.0), w=["onesf"])
    P.dve(lambda e: e.memset(C.onesb[:], 1.0), w=["onesb"])
    P.dve(lambda e: e.tensor_copy(out=C.maskLb[:], in_=C.maskL[:]), r=["maskL"], w=["maskLb"])
    P.dve(lambda e: e.tensor_copy(out=C.maskUb[:], in_=C.maskU[:]), r=["maskU"], w=["maskUb"])
    P.dve(lambda e: e.memset(C.epsb[:], EPS), w=["epsb"])
    P.dve(lambda e: e.memset(C.lnk[:], -0.5 * math.log(128.0)), w=["lnk"])
    P.dma("sp", C.rbB[:], I["rel_bias"].partition_broadcast(128), w=["rbB"])
    P.barrier(C.bar[:])

    setup_bias_tables(C)
    fin = []
    done = False
    for layer in range(depth):
        xsrc = I["x"] if layer == 0 else C.xres
        phase_norm(C, xsrc, I["norm_mix_g"][layer:layer + 1, :], C.uT_d)
        if stage == "norm":
            fin.append(copy_dram(C, C.tap["uT"], C.uT_d, [D, S], BF16)); break
        if stage in ("gqa", "full", "merge", "layer"):
            phase_gqa(C, layer)
        if stage == "gqa":
            fin.append(copy_dram(C, C.tap["ybT"], C.ynT[1], [512, S], BF16)); break
        if stage in ("diff", "full", "merge", "layer"):
            phase_diff(C, layer)
        if stage == "diff":
            fin.append(copy_dram(C, C.tap["ydT"], C.ynT[3], [512, S], BF16)); break
        if stage in ("four", "full", "merge", "layer"):
            phase_four(C, layer)
        if stage == "four":
            fin.append(copy_dram(C, C.tap["yaT"], C.ynT[0], [512, S], BF16)); break
        if stage in ("mlstm", "full", "merge", "layer"):
            phase_mlstm(C, layer)
        if stage == "mlstm":
            fin.append(copy_dram(C, C.tap["ycT"], C.ynT[2], [512, S], BF16)); break
        phase_merge(C, layer, xsrc)
        if stage == "merge":
            fin.append(copy_dram(C, C.tap["xmid"], C.xres, [S, D], F32)); break
        phase_norm(C, C.xres, I["norm_ffn_g"][layer:layer + 1, :], C.uT_d)
        phase_ffn(C, layer)
        if stage == "layer":
            fin.append(copy_dram(C, C.tap["xl"], C.xres, [S, D], F32)); break
    if stage == "full":
        fin.extend(phase_out(C, out))
    st = P.emit(final_wait_ops=fin)
    ges.close()
    return nc, st


def copy_dram(C, dst, src, shape, dt):
    P, nc = C.P, C.nc
    rows, cols = shape
    last = None
    with contextlib.ExitStack() as es:
        t = es.enter_context(nc.sbuf_tensor(_uniq("cpy"), [128, cols], dt))
        for r0 in range(0, rows, 128):
            P.dma("sp", t[:], src[r0:r0 + 128, :], w=["cpy"])
            last = P.dma("sp", dst[r0:r0 + 128, :], t[:], r=["cpy"])
        P.barrier(C.bar[:])
    return last


def phase_norm(C, xsrc, g_row, dstT):
    P, nc, S, NT = C.P, C.nc, C.S, C.NT
    with contextlib.ExitStack() as es:
        sb = lambda n, s, d: es.enter_context(nc.sbuf_tensor(_uniq(n), list(s), d))
        gB = sb("n_gB", [128, D], F32)
        xt = [sb("n_xt%d" % i, [128, D], F32) for i in range(4)]
        junk = sb("n_junk", [128, D], F32)
        ub = [sb("n_ub%d" % i, [128, D], BF16) for i in range(4)]
        ssq = [sb("n_ssq%d" % i, [128, 1], F32) for i in range(4)]
        stg = [sb("n_stg%d" % i, [128, 8, 512], BF16) for i in range(2)]
        P.dma("sp", gB[:], g_row.partition_broadcast(128), w=["n_gB"])
        def stage_a(t):
            b = t % 4
            P.dma("sp", xt[b][:], xsrc[t * 128:(t + 1) * 128, :], w=[("n_xt", b)])
            P.act(lambda e, b=b: e.activation(out=junk[:], in_=xt[b][:], func=AF.Square, accum_out=ssq[b][:]),
                  r=[("n_xt", b)], w=["n_junk", ("n_ssq", b)])
            P.act(lambda e, b=b: e.activation(out=ssq[b][:], in_=ssq[b][:], func=AF.Sqrt, scale=1.0 / D, bias=C.epsb[:]),
                  r=[("n_ssq", b)], w=[("n_ssq", b)])
            P.dve(lambda e, b=b: e.reciprocal(out=ssq[b][:], in_=ssq[b][:]), r=[("n_ssq", b)], w=[("n_ssq", b)])
            P.dve(lambda e, b=b: e.scalar_tensor_tensor(out=ub[b][:], in0=xt[b][:], scalar=ssq[b][:, 0:1], in1=gB[:],
                                                        op0=ALU.mult, op1=ALU.mult),
                  r=[("n_xt", b), ("n_ssq", b), "n_gB"], w=[("n_ub", b)])
            pb_ = t % 2
            pt = C.pst[pb_]
            for j in range(8):
                P.pe(lambda e, b=b, j=j, pt=pt: e.transpose(pt[:, j * 128:(j + 1) * 128], ub[b][:, j * 128:(j + 1) * 128], C.ident[:]),
                     r=[("n_ub", b)], w=[("pst", pb_)])

        def stage_b(t):
            pb_ = t % 2
            pt = C.pst[pb_]
            sgi = (t // 4) % 2
            tt = t % 4
            P.act(lambda e, pt=pt, sgi=sgi, tt=tt: e.copy(out=stg[sgi][:, :, tt * 128:(tt + 1) * 128],
                                                          in_=pt[:].rearrange("p (j c) -> p j c", j=8)),
                  r=[("pst", pb_)], w=[("n_stg", sgi)])
            if tt == 3:
                t0 = (t // 4) * 512
                P.dma("pool", dstT[:, t0:t0 + 512].rearrange("(j p) t -> p j t", p=128), stg[sgi][:],
                      r=[("n_stg", sgi)])

        stage_a(0)
        for t in range(NT):
            if t + 1 < NT:
                stage_a(t + 1)
            stage_b(t)
        P.barrier(C.bar[:])


def load_T(C, dst, srcT, ncc, q="sp", key=None):
    C.P.dma(q, dst[:], srcT.rearrange("(j p) t -> p j t", p=128), w=[key])


def load_w(C, dst, wsrc, key, q="pool"):
    C.P.dma(q, dst, wsrc.rearrange("(j p) n -> p j n", p=128), w=[key])


def attention(C, QT, KT, qk_key, Vaug, v_key, dv, scale, ptile, out_cb, bias_fn=None, obufs=None, feat=False):
    P, S, NT, NQC = C.P, C.S, C.NT, C.NQC
    NP = NT // 2
    steps = [(qc, sp) for qc in range(NQC) for sp in range(NP)]
    po = C.psb[2]

    def bias_of(st, qc):
        return bias_fn(st, qc) if bias_fn is not None else None

    def issue_qk(i):
        qc, sp = steps[i]
        buf = i % 2
        ps = C.psb[buf]
        for u in range(2):
            st = 2 * sp + u
            psS = ps[:, u * 512:(u + 1) * 512]
            kS = ("psb", buf, u)
            bias = bias_of(st, qc)
            band = bias is not None and bias[0] == "band"
            P.pe(lambda e, psS=psS, st=st, qc=qc, band=band: e.matmul(
                psS, lhsT=KT[:, st * 128:(st + 1) * 128], rhs=QT[:, qc * 512:(qc + 1) * 512],
                start=True, stop=not band), r=[qk_key], w=[kS])
            if band:
                P.pe(lambda e, psS=psS, bt=bias[1]: e.matmul(psS, lhsT=C.ident[:], rhs=bt, start=False, stop=True),
                     r=[bias[2], "ident"], w=[kS])

    def issue_exp(i):
        qc, sp = steps[i]
        buf = i % 2
        ps = C.psb[buf]
        pt, pk = ptile.next()
        b0 = bias_of(2 * sp, qc)
        b1 = bias_of(2 * sp + 1, qc)
        c0 = b0 if (b0 is not None and b0[0] == "const") else None
        c1 = b1 if (b1 is not None and b1[0] == "const") else None
        same = (c0 is None and c1 is None)
        if same:
            if c0 is None:
                P.act(lambda e, pt=pt, ps=ps: e.activation(out=pt[:, 0:1024], in_=ps[:, 0:1024], func=AF.Exp, scale=scale),
                      r=[("psb", buf, 0), ("psb", buf, 1)], w=[pk])
            else:
                P.act(lambda e, pt=pt, ps=ps, bb=c0[1]: e.activation(out=pt[:, 0:1024], in_=ps[:, 0:1024], func=AF.Exp, scale=scale, bias=bb),
                      r=[("psb", buf, 0), ("psb", buf, 1), c0[2]], w=[pk])
        else:
            for u, cb in enumerate((c0, c1)):
                if cb is None:
                    P.act(lambda e, pt=pt, ps=ps, u=u: e.activation(out=pt[:, u * 512:(u + 1) * 512], in_=ps[:, u * 512:(u + 1) * 512],
                                                                   func=AF.Exp, scale=scale), r=[("psb", buf, u)], w=[pk])
                else:
                    P.act(lambda e, pt=pt, ps=ps, u=u, bb=cb[1]: e.activation(out=pt[:, u * 512:(u + 1) * 512], in_=ps[:, u * 512:(u + 1) * 512],
                                                                             func=AF.Exp, scale=scale, bias=bb),
                          r=[("psb", buf, u), cb[2]], w=[pk])
        return pt, pk

    def issue_pv_feat(i, pt, pk):
        qc, sp = steps[i]
        poT = po[0:dv + 1, 0:512]
        for u in range(2):
            st = 2 * sp + u
            P.pe(lambda e, pt=pt, st=st, u=u, sp=sp: e.matmul(poT, lhsT=Vaug(st), rhs=pt[:, u * 512:(u + 1) * 512],
                                                             start=(sp == 0 and u == 0), stop=(st == NT - 1)),
                 r=[pk, v_key], w=[("psb", 2, 0)])
        if sp == NP - 1:
            out_cb(qc, poT, ("psb", 2, 0))

    def issue_pv(i, pt, pk):
        if feat:
            return issue_pv_feat(i, pt, pk)
        qc, sp = steps[i]
        for u in range(2):
            st = 2 * sp + u
            for qt in range(4):
                o_ap = po[:, qt * 256:qt * 256 + dv + 1]
                P.pe(lambda e, o_ap=o_ap, pt=pt, qt=qt, st=st, u=u, sp=sp: e.matmul(
                    o_ap, lhsT=pt[:, u * 512 + qt * 128:u * 512 + (qt + 1) * 128], rhs=Vaug(st),
                    start=(sp == 0 and u == 0 and qt % 2 == 0), stop=(st == NT - 1), skip_group_check=True),
                    r=[pk, v_key], w=[("psb", 2, qt // 2)])
        if sp == NP - 1:
            ob, ok = obufs.next()
            for bk in range(2):
                P.dve(lambda e, ob=ob, bk=bk: e.tensor_copy(out=ob[:, bk * 512:(bk + 1) * 512], in_=po[:, bk * 512:(bk + 1) * 512]),
                      r=[("psb", 2, bk)], w=[ok])
            for qt in range(4):
                out_cb(qc, qt, ob[:, qt * 256:qt * 256 + dv + 1], ok)

    n = len(steps)
    issue_qk(0)
    for i in range(n):
        if i + 1 < n:
            issue_qk(i + 1)
        pt, pk = issue_exp(i)
        issue_pv(i, pt, pk)


def phase_gqa(C, layer):
    P, nc, S, NT, NQC, I = C.P, C.nc, C.S, C.NT, C.NQC, C.I
    with contextlib.ExitStack() as es:
        sb = lambda n, s, d: es.enter_context(nc.sbuf_tensor(_uniq(n), list(s), d))
        QT = sb("b_QT", [128, 4, S], BF16)
        KT = sb("b_KT", [128, 2, S], BF16)
        V = sb("b_V", [128, NT, 2, 65], BF16)
        ropec = sb("b_rc", [128, NT, 32], F32)
        ropes = sb("b_rs", [128, NT, 32], F32)
        g640 = sb("b_g", [128, 10, 64], F32)
        gq = sb("b_gq", [128, 128], F32)
        with contextlib.ExitStack() as es2:
            sb2 = lambda n, s, d: es2.enter_context(nc.sbuf_tensor(_uniq(n), list(s), d))
            uT = sb2("b_uT", [128, 8, S], BF16)
            wB = sb2("b_w", [128, 8, 768], BF16)
            sq = sb2("b_sq", [128, 10, 64], F32)
            ss = sb2("b_ss", [128, 10], F32)
            qn = sb2("b_qn", [128, 10, 64], F32)
            t1 = sb2("b_t1", [128, 10, 32], F32)
            t2 = sb2("b_t2", [128, 10, 32], F32)
            qr = [sb2("b_qr%d" % i, [128, 12, 64], BF16) for i in range(2)]
            load_T(C, uT, C.uT_d, 8, key="b_uT")
            load_w(C, wB[:], I["w_in"][layer, :, COL["bq"]:COL["bq"] + 768], "b_w")
            P.dma("sp", ropec[:], I["ropec"].rearrange("(t p) c -> p t c", p=128), w=["b_rc"])
            P.dma("sp", ropes[:], I["ropes"].rearrange("(t p) c -> p t c", p=128), w=["b_rs"])
            P.dma("sp", gq[:], I["qk_norm_g"][layer:layer + 1, :].partition_broadcast(128), w=["b_gq"])
            P.dve(lambda e: e.memset(V[:], 1.0), w=["b_V"])
            P.dve(lambda e: e.tensor_copy(out=g640[:, 0:8, :], in_=gq[:, 0:64].unsqueeze(1).to_broadcast([128, 8, 64])),
                  r=["b_gq"], w=["b_g"])
            P.dve(lambda e: e.tensor_copy(out=g640[:, 8:10, :], in_=gq[:, 64:128].unsqueeze(1).to_broadcast([128, 2, 64])),
                  r=["b_gq", "b_g"], w=["b_g"])
            def gq_a(t):
                b = t % 2
                ps = C.psb[b]
                for half, (c0, n) in enumerate(((0, 512), (512, 256))):
                    for j in range(8):
                        P.pe(lambda e, ps=ps, half=half, c0=c0, n=n, j=j, t=t: e.matmul(
                            ps[:, half * 512:half * 512 + n], lhsT=uT[:, j, t * 128:(t + 1) * 128],
                            rhs=wB[:, j, c0:c0 + n], start=(j == 0), stop=(j == 7)),
                            r=["b_uT", "b_w"], w=[("psb", b, half)])
            def gq_b(t):
                b = t % 2
                ps = C.psb[b]
                kq = [("psb", b, 0), ("psb", b, 1)]
                qk_ps = ps[:, 0:640].rearrange("p (h d) -> p h d", d=64)
                P.act(lambda e, qk_ps=qk_ps: e.activation(out=sq[:], in_=qk_ps, func=AF.Square), r=kq, w=["b_sq"])
                P.dve(lambda e: e.tensor_reduce(out=ss[:], in_=sq[:], axis=AX.X, op=ALU.add), r=["b_sq"], w=["b_ss"])
                P.act(lambda e: e.activation(out=ss[:], in_=ss[:], func=AF.Sqrt, scale=1.0 / 64, bias=C.epsb[:]),
                      r=["b_ss"], w=["b_ss"])
                P.dve(lambda e: e.reciprocal(out=ss[:], in_=ss[:]), r=["b_ss"], w=["b_ss"])
                P.dve(lambda e, qk_ps=qk_ps: e.tensor_tensor(out=qn[:], in0=qk_ps, in1=ss[:].unsqueeze(2).to_broadcast([128, 10, 64]),
                                                              op=ALU.mult), r=kq + ["b_ss"], w=["b_qn"])
                P.dve(lambda e: e.tensor_tensor(out=qn[:], in0=qn[:], in1=g640[:], op=ALU.mult), r=["b_qn", "b_g"], w=["b_qn"])
                cb = ropec[:, t, :].unsqueeze(1).to_broadcast([128, 10, 32])
                sbb = ropes[:, t, :].unsqueeze(1).to_broadcast([128, 10, 32])
                x1 = qn[:, :, 0:32]
                x2 = qn[:, :, 32:64]
                q_out = qr[b]
                P.dve(lambda e, cb=cb: e.tensor_tensor(out=t1[:], in0=x1, in1=cb, op=ALU.mult), r=["b_qn", "b_rc"], w=["b_t1"])
                P.dve(lambda e, sbb=sbb: e.tensor_tensor(out=t2[:], in0=x2, in1=sbb, op=ALU.mult), r=["b_qn", "b_rs"], w=["b_t2"])
                P.dve(lambda e, q_out=q_out: e.tensor_tensor(out=q_out[:, 0:10, 0:32], in0=t1[:], in1=t2[:], op=ALU.subtract),
                      r=["b_t1", "b_t2"], w=[("b_qr", b)])
                P.dve(lambda e, sbb=sbb: e.tensor_tensor(out=t1[:], in0=x1, in1=sbb, op=ALU.mult), r=["b_qn", "b_rs", ("b_qr", b)], w=["b_t1"])
                P.dve(lambda e, cb=cb: e.tensor_tensor(out=t2[:], in0=x2, in1=cb, op=ALU.mult), r=["b_qn", "b_rc", ("b_qr", b)], w=["b_t2"])
                P.dve(lambda e, q_out=q_out: e.tensor_tensor(out=q_out[:, 0:10, 32:64], in0=t1[:], in1=t2[:], op=ALU.add),
                      r=["b_t1", "b_t2"], w=[("b_qr", b)])
                P.dve(lambda e, q_out=q_out: e.tensor_copy(out=q_out[:, 10:12, :], in_=q_out[:, 9:10, :].to_broadcast([128, 2, 64])),
                      r=[("b_qr", b)], w=[("b_qr", b)])
                P.dve(lambda e, q_out=q_out: e.tensor_copy(out=q_out[:, 9:10, :], in_=q_out[:, 8:9, :]),
                      r=[("b_qr", b)], w=[("b_qr", b)])
                pt = C.pst[b]
                for j in range(6):
                    P.pe(lambda e, pt=pt, j=j, q_out=q_out: e.transpose(
                        pt[:, j * 128:(j + 1) * 128], q_out[:, 2 * j:2 * j + 2, :].rearrange("p a d -> p (a d)"), C.ident[:]),
                        r=[("b_qr", b)], w=[("pst", b)])
                P.act(lambda e, pt=pt, t=t: e.copy(out=QT[:, :, t * 128:(t + 1) * 128],
                                                   in_=pt[:, 0:512].rearrange("p (j c) -> p j c", j=4)),
                      r=[("pst", b)], w=["b_QT"])
                P.act(lambda e, pt=pt, t=t: e.copy(out=KT[:, :, t * 128:(t + 1) * 128],
                                                   in_=pt[:, 512:768].rearrange("p (j c) -> p j c", j=2)),
                      r=[("pst", b)], w=["b_KT"])
                P.act(lambda e, ps=ps, t=t: e.copy(out=V[:, t, :, 0:64], in_=ps[:, 640:768].rearrange("p (g d) -> p g d", g=2)),
                      r=kq, w=["b_V"] + kq)
            gq_a(0)
            for t in range(NT):
                if t + 1 < NT:
                    gq_a(t + 1)
                gq_b(t)
            P.barrier(C.bar[:])
        pts = [sb("b_pt%d" % i, [128, 1024], BF16) for i in range(3)]
        ring = Ring([p[:] for p in pts], "b_pt")
        obs = [sb("b_ob%d" % i, [128, 1024], F32) for i in range(2)]
        obufs = Ring([p[:] for p in obs], "b_ob")
        rec = sb("b_rec", [128, 1], F32)
        stg = sb("b_stg", [128, 4, 512], BF16)
        rrow = sb("b_rrow", [128, 512], F32)
        rt = sb("b_rt", [128, 512], F32)
        rhi = sb("b_rhi", [128, 512], BF16)
        rlo = sb("b_rlo", [128, 512], BF16)
        bcs = sb("b_bcs", [128, 512], F32)
        ystg = [sb("b_ystg%d" % i, [128, 512], BF16) for i in range(2)]
        poss = [sb("b_pos%d" % i, [128, 512], F32) for i in range(2)]
        kst = 0
        for h in range(8):
            g = h // 4
            base = 64 * (h % 2)
            QTh = QT[base:base + 64, h // 2, :]
            KTh = KT[base:base + 64, g, :]

            def out_cb(qc, poT, ps_key, h=h, base=base):
                nonlocal kst
                yb = kst % 2
                kst += 1
                r64 = slice(64, 65)
                pos = poss[yb]
                P.dve(lambda e, pos=pos: e.tensor_copy(out=pos[0:65, :], in_=poT), r=[ps_key], w=[("b_pos", yb)])
                P.dve(lambda e, pos=pos: e.reciprocal(out=rrow[r64, :], in_=pos[64:65, :]), r=[("b_pos", yb)], w=["b_rrow"])
                P.dve(lambda e: e.tensor_copy(out=rhi[r64, :], in_=rrow[r64, :]), r=["b_rrow"], w=["b_rhi"])
                P.dve(lambda e: e.tensor_copy(out=rt[r64, :], in_=rhi[r64, :]), r=["b_rhi"], w=["b_rt"])
                P.dve(lambda e: e.tensor_tensor(out=rt[r64, :], in0=rrow[r64, :], in1=rt[r64, :], op=ALU.subtract), r=["b_rrow", "b_rt"], w=["b_rt"])
                P.dve(lambda e: e.tensor_copy(out=rlo[r64, :], in_=rt[r64, :]), r=["b_rt"], w=["b_rlo"])
                bc = C.psb[2][0:64, 512:1024]
                P.pe(lambda e: e.matmul(bc, lhsT=C.onesb[64:65, 0:64], rhs=rhi[r64, :], start=True, stop=False), r=["b_rhi", "onesb"], w=[("psb", 2, 1)])
                P.pe(lambda e: e.matmul(bc, lhsT=C.onesb[64:65, 0:64], rhs=rlo[r64, :], start=False, stop=True), r=["b_rlo", "onesb"], w=[("psb", 2, 1)])
                P.dve(lambda e: e.tensor_copy(out=bcs[0:64, :], in_=bc), r=[("psb", 2, 1)], w=["b_bcs"])
                P.dve(lambda e, yb=yb, pos=pos: e.tensor_tensor(out=ystg[yb][0:64, :], in0=pos[0:64, :], in1=bcs[0:64, :], op=ALU.mult),
                      r=[("b_pos", yb), "b_bcs"], w=[("b_ystg", yb)])
                P.dma("pool", C.ynT[1][h * 64:(h + 1) * 64, qc * 512:(qc + 1) * 512], ystg[yb][0:64, :], r=[("b_ystg", yb)])
            attention(C, QTh, KTh, "b_QT", lambda st, g=g: V[:, st, g, :], "b_V", 64, 0.125, ring, out_cb, obufs=obufs, feat=True)
        P.barrier(C.bar[:])


def lam_init_of(layer):
    return 0.8 - 0.6 * math.exp(-0.3 * layer)


def setup_bias_tables(C):
    P, nc = C.P, C.nc
    seq, thr = bias_steps()
    C.W2b = [C.ges.enter_context(nc.sbuf_tensor("W2b%d" % h, [128, 1152], BF16)) for h in range(4)]
    with contextlib.ExitStack() as es:
        sb = lambda n, s, d: es.enter_context(nc.sbuf_tensor(_uniq(n), list(s), d))
        relf = sb("s_relf", [128, 1152], F32)
        acc = sb("s_acc", [128, 1152], F32)
        tmp = sb("s_tmp", [128, 1152], F32)
        dB = sb("s_dB", [128, len(thr), 4], F32)
        P.dma("sp", relf[:], C.I["relf"], w=["s_relf"])
        rb3 = C.rbB[:].rearrange("p (b h) -> p b h", h=4)
        for k in range(len(thr)):
            P.dve(lambda e, k=k: e.tensor_tensor(out=dB[:, k, :], in0=rb3[:, seq[k + 1], :], in1=rb3[:, seq[k], :], op=ALU.subtract),
                  r=["rbB"], w=["s_dB"])
        for h in range(4):
            P.dve(lambda e, h=h: e.tensor_scalar(out=acc[:], in0=relf[:], scalar1=0.0, scalar2=C.rbB[:, seq[0] * 4 + h:seq[0] * 4 + h + 1],
                                                 op0=ALU.mult, op1=ALU.add), r=["s_relf", "rbB"], w=["s_acc"])
            for k in range(len(thr)):
                P.dve(lambda e, k=k, h=h: e.tensor_scalar(out=tmp[:], in0=relf[:], scalar1=float(thr[k]), scalar2=dB[:, k, h:h + 1],
                                                          op0=ALU.is_ge, op1=ALU.mult), r=["s_relf", "s_dB"], w=["s_tmp"])
                P.dve(lambda e: e.tensor_tensor(out=acc[:], in0=acc[:], in1=tmp[:], op=ALU.add), r=["s_acc", "s_tmp"], w=["s_acc"])
            P.act(lambda e, h=h: e.activation(out=C.W2b[h][:], in_=acc[:], func=AF.Copy, scale=8.0), r=["s_acc"], w=[("W2b", h)])
        P.barrier(C.bar[:])


def phase_diff(C, layer):
    P, nc, S, NT, NQC, I = C.P, C.nc, C.S, C.NT, C.NQC, C.I
    li = lam_init_of(layer)
    with contextlib.ExitStack() as es:
        sb = lambda n, s, d: es.enter_context(nc.sbuf_tensor(_uniq(n), list(s), d))
        QT = sb("d_QT", [128, 4, S], BF16)
        KT = sb("d_KT", [128, 4, S], BF16)
        V = sb("d_V", [128, NT, 4, 129], BF16)
        with contextlib.ExitStack() as es2:
            sb2 = lambda n, s, d: es2.enter_context(nc.sbuf_tensor(_uniq(n), list(s), d))
            uT = sb2("d_uT", [128, 8, S], BF16)
            wqk = sb2("d_wqk", [128, 8, 1024], BF16)
            load_T(C, uT, C.uT_d, 8, key="d_uT")
            load_w(C, wqk[:], I["w_in"][layer, :, COL["dq"]:COL["dq"] + 1024], "d_wqk")
            P.dve(lambda e: e.memset(V[:], 1.0), w=["d_V"])
            k = 0
            for blk in range(8):
                dst = QT if blk < 4 else KT
                for tc in range(NQC):
                    bi = k % 4
                    k += 1
                    ps = C.psb[bi // 2][:, (bi % 2) * 512:(bi % 2) * 512 + 512]
                    pk = ("psb", bi // 2, bi % 2)
                    for j in range(8):
                        P.pe(lambda e, ps=ps, j=j, blk=blk, tc=tc: e.matmul(
                            ps, lhsT=wqk[:, j, blk * 128:(blk + 1) * 128], rhs=uT[:, j, tc * 512:(tc + 1) * 512],
                            start=(j == 0), stop=(j == 7)), r=["d_uT", "d_wqk"], w=[pk])
                    eng = P.act if k % 2 == 0 else P.dve
                    if k % 2 == 0:
                        P.act(lambda e, ps=ps, dst=dst, blk=blk, tc=tc: e.copy(out=dst[:, blk % 4, tc * 512:(tc + 1) * 512], in_=ps),
                              r=[pk], w=["d_QK"])
                    else:
                        P.dve(lambda e, ps=ps, dst=dst, blk=blk, tc=tc: e.tensor_copy(out=dst[:, blk % 4, tc * 512:(tc + 1) * 512], in_=ps),
                              r=[pk], w=["d_QK"])
            wv = wqk[:, :, 0:512]
            load_w(C, wv, I["w_in"][layer, :, COL["dv"]:COL["dv"] + 512], "d_wqk")
            for t in range(NT):
                bi = t % 4
                ps = C.psb[bi // 2][:, (bi % 2) * 512:(bi % 2) * 512 + 512]
                pk = ("psb", bi // 2, bi % 2)
                for j in range(8):
                    P.pe(lambda e, ps=ps, j=j, t=t: e.matmul(ps, lhsT=uT[:, j, t * 128:(t + 1) * 128], rhs=wv[:, j, :],
                                                            start=(j == 0), stop=(j == 7)), r=["d_uT", "d_wqk"], w=[pk])
                P.act(lambda e, ps=ps, t=t: e.copy(out=V[:, t, :, 0:128], in_=ps.rearrange("p (h d) -> p h d", h=4)),
                      r=[pk], w=["d_V"])
            P.barrier(C.bar[:])
        lpB = sb("d_lp", [128, 256], F32)
        lpr = sb("d_lpr", [128, 128], F32)
        s12 = sb("d_s12", [128, 2], F32)
        nlam = sb("d_nlam", [128, 1], F32)
        gsub = sb("d_gsub", [128, 128], F32)
        P.dma("sp", lpB[:], I["diff_lambda"][layer:layer + 1, :].partition_broadcast(128), w=["d_lp"])
        P.dma("sp", gsub[:], I["diff_norm_g"][layer:layer + 1, :].partition_broadcast(128), w=["d_gsub"])
        lp4 = lpB[:].rearrange("p (a b d) -> p a b d", a=2, b=2)
        P.dve(lambda e: e.tensor_tensor(out=lpr[:].rearrange("p (a d) -> p a d", a=2), in0=lp4[:, :, 0, :], in1=lp4[:, :, 1, :], op=ALU.mult),
              r=["d_lp"], w=["d_lpr"])
        P.dve(lambda e: e.tensor_reduce(out=s12[:], in_=lpr[:].rearrange("p (a d) -> p a d", a=2), axis=AX.X, op=ALU.add),
              r=["d_lpr"], w=["d_s12"])
        P.act(lambda e: e.activation(out=s12[:], in_=s12[:], func=AF.Exp), r=["d_s12"], w=["d_s12"])
        P.dve(lambda e: e.tensor_tensor(out=nlam[:], in0=s12[:, 1:2], in1=s12[:, 0:1], op=ALU.subtract), r=["d_s12"], w=["d_nlam"])
        P.dve(lambda e: e.tensor_scalar_add(out=nlam[:], in0=nlam[:], scalar1=-li), r=["d_nlam"], w=["d_nlam"])
        P.dve(lambda e: e.tensor_scalar_mul(out=gsub[:], in0=gsub[:], scalar1=1.0 - li), r=["d_gsub"], w=["d_gsub"])
        pts = [sb("d_pt%d" % i, [128, 1024], BF16) for i in range(3)]
        ring = Ring([p[:] for p in pts], "d_pt")
        obs = [sb("d_ob%d" % i, [128, 1024], F32) for i in range(2)]
        obufs = Ring([p[:] for p in obs], "d_ob")
        o1n = sb("d_o1n", [128, NT, 128], F32)
        ytok = sb("d_ytok", [128, NT, 512], BF16)
        rec = sb("d_rec", [128, 1], F32)
        osb = sb("d_osb", [128, 128], F32)
        junk = sb("d_junk", [128, 128], F32)
        ssN = sb("d_ssN", [128, NT], F32)
        stg = sb("d_stg", [128, 4, 512], BF16)
        for h in range(4):
            def bias_fn(st, qc, h=h):
                dlt = st - 4 * qc
                if -1 <= dlt <= 4:
                    return ("band", C.W2b[h][:, 512 - 128 * dlt:1024 - 128 * dlt], ("W2b", h))
                if dlt > 4:
                    return ("const", C.rbB[:, 31 * 4 + h:31 * 4 + h + 1], "rbB", 31)
                return ("const", C.rbB[:, 15 * 4 + h:15 * 4 + h + 1], "rbB", 15)
            for m in range(2):
                base = 64 * m
                QTh = QT[base:base + 64, h, :]
                KTh = KT[base:base + 64, h, :]
                if m == 0:
                    def out_cb(qc, qt, ps_ap, ps_key, h=h):
                        t = qc * 4 + qt
                        P.dve(lambda e: e.reciprocal(out=rec[:], in_=ps_ap[:, 128:129]), r=[ps_key], w=["d_rec"])
                        P.dve(lambda e: e.tensor_scalar(out=o1n[:, t, :], in0=ps_ap[:, 0:128], scalar1=rec[:, 0:1], scalar2=None, op0=ALU.mult),
                              r=[ps_key, "d_rec"], w=["d_o1n"])
                else:
                    def out_cb(qc, qt, ps_ap, ps_key, h=h):
                        t = qc * 4 + qt
                        P.dve(lambda e: e.reciprocal(out=rec[:], in_=ps_ap[:, 128:129]), r=[ps_key], w=["d_rec"])
                        P.dve(lambda e: e.tensor_tensor(out=rec[:], in0=rec[:], in1=nlam[:], op=ALU.mult), r=["d_rec", "d_nlam"], w=["d_rec"])
                        P.dve(lambda e: e.scalar_tensor_tensor(out=o1n[:, t, :], in0=ps_ap[:, 0:128], scalar=rec[:, 0:1], in1=o1n[:, t, :],
                                                               op0=ALU.mult, op1=ALU.add), r=[ps_key, "d_rec", "d_o1n"], w=["d_o1n"])
                attention(C, QTh, KTh, "d_QK", lambda st, h=h: V[:, st, h, :], "d_V", 128, 0.125, ring, out_cb, bias_fn=bias_fn, obufs=obufs)
            for t in range(NT):
                P.act(lambda e, t=t: e.activation(out=junk[:], in_=o1n[:, t, :], func=AF.Square, accum_out=ssN[:, t:t + 1]),
                      r=["d_o1n"], w=["d_junk", "d_ssN"])
            P.act(lambda e: e.activation(out=ssN[:], in_=ssN[:], func=AF.Sqrt, scale=1.0 / 128, bias=C.epsb[:]), r=["d_ssN"], w=["d_ssN"])
            P.dve(lambda e: e.reciprocal(out=ssN[:], in_=ssN[:]), r=["d_ssN"], w=["d_ssN"])
            P.dve(lambda e: e.tensor_tensor(out=o1n[:], in0=o1n[:], in1=ssN[:].unsqueeze(2).to_broadcast([128, NT, 128]), op=ALU.mult),
                  r=["d_o1n", "d_ssN"], w=["d_o1n"])
            P.dve(lambda e, h=h: e.tensor_tensor(out=ytok[:, :, h * 128:(h + 1) * 128], in0=o1n[:],
                                                 in1=gsub[:].unsqueeze(1).to_broadcast([128, NT, 128]), op=ALU.mult),
                  r=["d_o1n", "d_gsub"], w=["d_ytok"])
        store_T(C, ytok, C.ynT[3], 4, "d_ytok", stg, "d_stg")
        P.barrier(C.bar[:])


def store_T(C, ytok, dstT, ncc, ykey, stg, skey):
    P, NT = C.P, C.NT
    for t in range(NT):
        b = t % 2
        pt = C.pst[b]
        for j in range(ncc):
            P.pe(lambda e, pt=pt, j=j, t=t: e.transpose(pt[:, j * 128:(j + 1) * 128], ytok[:, t, j * 128:(j + 1) * 128], C.ident[:]),
                 r=[ykey], w=[("pst", b)])
        tt = t % 4
        P.act(lambda e, pt=pt, tt=tt: e.copy(out=stg[:, 0:ncc, tt * 128:(tt + 1) * 128],
                                             in_=pt[:, 0:ncc * 128].rearrange("p (j c) -> p j c", j=ncc)),
              r=[("pst", b)], w=[skey])
        if tt == 3:
            t0 = (t // 4) * 512
            P.dma("pool", dstT[:, t0:t0 + 512].rearrange("(j p) t -> p j t", p=128), stg[:, 0:ncc, :], r=[skey])


def phase_four(C, layer):
    P, nc, S, NT, NQC, I = C.P, C.nc, C.S, C.NT, C.NQC, C.I
    KC = 256
    with contextlib.ExitStack() as es:
        sb = lambda n, s, d: es.enter_context(nc.sbuf_tensor(_uniq(n), list(s), d))
        GCS = sb("a_GCS", [128, NT, 1024], BF16)
        with contextlib.ExitStack() as es2:
            sb2 = lambda n, s, d: es2.enter_context(nc.sbuf_tensor(_uniq(n), list(s), d))
            uT = sb2("a_uT", [128, 8, S], BF16)
            wA = sb2("a_w", [128, 8, 512], BF16)
            bd = sb2("a_bd", [128, 2, 128], BF16)
            aT = sb2("a_aT", [128, 4, S], BF16)
            load_T(C, uT, C.uT_d, 8, key="a_uT")
            load_w(C, wA[:], I["w_in"][layer, :, 0:512], "a_w")
            P.dma("sp", bd[:], I["bdcs"].rearrange("a p c -> p a c"), w=["a_bd"])
            k = 0
            for blk in range(4):
                for tc in range(NQC):
                    bi = k % 4
                    k += 1
                    ps = C.psb[bi // 2][:, (bi % 2) * 512:(bi % 2) * 512 + 512]
                    pk = ("psb", bi // 2, bi % 2)
                    for j in range(8):
                        P.pe(lambda e, ps=ps, j=j, blk=blk, tc=tc: e.matmul(
                            ps, lhsT=wA[:, j, blk * 128:(blk + 1) * 128], rhs=uT[:, j, tc * 512:(tc + 1) * 512],
                            start=(j == 0), stop=(j == 7)), r=["a_uT", "a_w"], w=[pk])
                    if k % 2 == 0:
                        P.act(lambda e, ps=ps, blk=blk, tc=tc: e.copy(out=aT[:, blk, tc * 512:(tc + 1) * 512], in_=ps), r=[pk], w=["a_aT"])
                    else:
                        P.dve(lambda e, ps=ps, blk=blk, tc=tc: e.tensor_copy(out=aT[:, blk, tc * 512:(tc + 1) * 512], in_=ps), r=[pk], w=["a_aT"])
            for st in range(NT):
                b = st % 2
                ps = C.psb[b]
                for cs in range(2):
                    for cc in range(4):
                        P.pe(lambda e, ps=ps, cs=cs, cc=cc, st=st: e.matmul(
                            ps[:, cs * 512 + cc * 128:cs * 512 + (cc + 1) * 128], lhsT=aT[:, cc, st * 128:(st + 1) * 128],
                            rhs=bd[:, cs, :], start=True, stop=True, skip_group_check=True), r=["a_aT", "a_bd"], w=[("psb", b, cs)])
                if st % 2 == 0:
                    P.act(lambda e, ps=ps, st=st: e.copy(out=GCS[:, st, :], in_=ps[:]), r=[("psb", b, 0), ("psb", b, 1)], w=["a_GCS"])
                else:
                    P.dve(lambda e, ps=ps, st=st: e.tensor_copy(out=GCS[:, st, :], in_=ps[:]), r=[("psb", b, 0), ("psb", b, 1)], w=["a_GCS"])
            P.barrier(C.bar[:])
        DC = [sb("a_DC%d" % i, [128, NT, KC], BF16) for i in range(2)]
        DS = [sb("a_DS%d" % i, [128, NT, KC], BF16) for i in range(2)]
        stg = [sb("a_stg%d" % i, [128, 4, KC], BF16) for i in range(2)]
        for kc in range(S // KC):
            b = kc % 2
            P.dma("sp", DC[b][:], I["dftc"][:, kc * KC:(kc + 1) * KC].rearrange("(t p) k -> p t k", p=128), w=[("a_DC", b)])
            P.dma("sp", DS[b][:], I["dfts"][:, kc * KC:(kc + 1) * KC].rearrange("(t p) k -> p t k", p=128), w=[("a_DS", b)])
            for cc in range(4):
                bi = cc
                ps = C.psb[bi // 2][:, (bi % 2) * 512:(bi % 2) * 512 + KC]
                pk = ("psb", bi // 2, bi % 2)
                for st in range(NT):
                    P.pe(lambda e, ps=ps, st=st, cc=cc, b=b: e.matmul(ps, lhsT=GCS[:, st, cc * 128:(cc + 1) * 128], rhs=DC[b][:, st, :],
                                                                     start=(st == 0), stop=False), r=["a_GCS", ("a_DC", b)], w=[pk])
                for st in range(NT):
                    P.pe(lambda e, ps=ps, st=st, cc=cc, b=b: e.matmul(ps, lhsT=GCS[:, st, 512 + cc * 128:512 + (cc + 1) * 128], rhs=DS[b][:, st, :],
                                                                     start=False, stop=(st == NT - 1)), r=["a_GCS", ("a_DS", b)], w=[pk])
                if cc % 2 == 0:
                    P.act(lambda e, ps=ps, cc=cc, b=b: e.copy(out=stg[b][:, cc, :], in_=ps), r=[pk], w=[("a_stg", b)])
                else:
                    P.dve(lambda e, ps=ps, cc=cc, b=b: e.tensor_copy(out=stg[b][:, cc, :], in_=ps), r=[pk], w=[("a_stg", b)])
            P.dma("pool", C.ynT[0][:, kc * KC:(kc + 1) * KC].rearrange("(j p) t -> p j t", p=128), stg[b][:], r=[("a_stg", b)])
        P.barrier(C.bar[:])


def phase_mlstm(C, layer):
    P, nc, S, NT, NQC, I = C.P, C.nc, C.S, C.NT, C.NQC, C.I
    X4 = NT * 4
    with contextlib.ExitStack() as es:
        sb = lambda n, s, d: es.enter_context(nc.sbuf_tensor(_uniq(n), list(s), d))
        uT = sb("c_uT", [128, 8, S], BF16)
        G = sb("c_G", [128, NT, 16], F32)
        wg = sb("c_wg", [128, 8, 16], BF16)
        biasB = sb("c_bias", [128, 16], F32)
        gC = sb("c_gC", [128, 512], F32)
        eq = [sb("c_eq%d" % d, [128, NT, 4], F32) for d in range(2)]
        ek = [sb("c_ek%d" % d, [128, NT, 4], F32) for d in range(2)]
        ekd = [sb("c_ekd%d" % d, [128, NT, 4], F32) for d in range(2)]
        dec = [sb("c_dec%d" % d, [128, NT, 4], F32) for d in range(2)]
        load_T(C, uT, C.uT_d, 8, key="c_uT")
        load_w(C, wg[:], I["w_in"][layer, :, COL["cg"]:COL["cg"] + 16], "c_wg")
        P.dma("sp", biasB[:], I["mlstm_gate_bias"][layer:layer + 1, :].partition_broadcast(128), w=["c_bias"])
        P.dma("sp", gC[:], I["mlstm_norm_g"][layer:layer + 1, :].partition_broadcast(128), w=["c_gC"])
        psg = C.psb[0][:, 0:NT * 16]
        for t in range(NT):
            for j in range(8):
                P.pe(lambda e, t=t, j=j: e.matmul(psg[:, t * 16:(t + 1) * 16], lhsT=uT[:, j, t * 128:(t + 1) * 128], rhs=wg[:, j, :],
                                                 start=(t == 0 and j == 0), stop=(j == 7), skip_group_check=True),
                     r=["c_uT", "c_wg"], w=[("psb", 0, 0)])
        P.dve(lambda e: e.tensor_tensor(out=G[:], in0=psg.rearrange("p (t k) -> p t k", k=16),
                                        in1=biasB[:].unsqueeze(1).to_broadcast([128, NT, 16]), op=ALU.add),
              r=[("psb", 0, 0), "c_bias"], w=["c_G"])
        with contextlib.ExitStack() as es2:
            sb2 = lambda n, s, d: es2.enter_context(nc.sbuf_tensor(_uniq(n), list(s), d))
            af = sb2("c_af", [128, NT, 4], F32)
            lf = sb2("c_lf", [128, NT, 4], F32)
            mn = sb2("c_mn", [128, NT, 4], F32)
            tm = sb2("c_tm", [128, NT, 4], F32)
            lfh = [sb2("c_lfh%d" % i, [128, NT, 4], BF16) for i in range(3)]
            for d in range(2):
                Fd = G[:, :, 4 + 8 * d:8 + 8 * d]
                Id = G[:, :, 8 * d:8 * d + 4]
                P.act(lambda e, Fd=Fd: e.activation(out=af[:], in_=Fd, func=AF.Abs), r=["c_G"], w=["c_af"])
                P.act(lambda e: e.activation(out=af[:], in_=af[:], func=AF.Exp, scale=-1.0), r=["c_af"], w=["c_af"])
                P.act(lambda e: e.activation(out=af[:], in_=af[:], func=AF.Ln, bias=C.onesf[:, 0:1]), r=["c_af"], w=["c_af"])
                P.dve(lambda e, Fd=Fd: e.tensor_single_scalar(out=mn[:], in_=Fd, scalar=0.0, op=ALU.min), r=["c_G"], w=["c_mn"])
                P.dve(lambda e: e.tensor_tensor(out=lf[:], in0=mn[:], in1=af[:], op=ALU.subtract), r=["c_mn", "c_af"], w=["c_lf"])
                tri = C.maskLb if d == 0 else C.maskUb
                pb = C.psb[1][:, 0:X4]
                pt_ = C.psb[1][:, 512:512 + X4]
                for sp in range(3):
                    P.dve(lambda e, sp=sp: e.tensor_copy(out=lfh[sp][:], in_=lf[:]), r=["c_lf"], w=[("c_lfh", sp)])
                    if sp < 2:
                        P.dve(lambda e, sp=sp: e.tensor_copy(out=mn[:], in_=lfh[sp][:]), r=[("c_lfh", sp)], w=["c_mn"])
                        P.dve(lambda e: e.tensor_tensor(out=lf[:], in0=lf[:], in1=mn[:], op=ALU.subtract), r=["c_lf", "c_mn"], w=["c_lf"])
                for sp in range(3):
                    l2 = lfh[sp][:].rearrange("p t h -> p (t h)")
                    P.pe(lambda e, tri=tri, pb=pb, l2=l2, sp=sp: e.matmul(pb, lhsT=tri[:], rhs=l2, start=(sp == 0), stop=(sp == 2)),
                         r=[("c_lfh", sp)], w=[("psb", 1, 0)])
                for sp in range(3):
                    l2 = lfh[sp][:].rearrange("p t h -> p (t h)")
                    P.pe(lambda e, pt_=pt_, l2=l2, sp=sp: e.matmul(pt_, lhsT=C.onesb[:], rhs=l2, start=(sp == 0), stop=(sp == 2)),
                         r=[("c_lfh", sp)], w=[("psb", 1, 1)])
                pb3 = pb.rearrange("p (t h) -> p t h", h=4)
                pt3 = pt_.rearrange("p (t h) -> p t h", h=4)
                P.act(lambda e, d=d, pb3=pb3: e.activation(out=eq[d][:], in_=pb3, func=AF.Exp), r=[("psb", 1, 0)], w=[("c_eq", d)])
                P.dve(lambda e, Id=Id, pb3=pb3: e.tensor_tensor(out=tm[:], in0=Id, in1=pb3, op=ALU.subtract), r=["c_G", ("psb", 1, 0)], w=["c_tm", ("psb", 1, 0)])
                P.act(lambda e, d=d: e.activation(out=ek[d][:], in_=tm[:], func=AF.Exp, bias=C.lnk[:]), r=["c_tm"], w=[("c_ek", d)])
                P.act(lambda e, d=d, pt3=pt3: e.activation(out=dec[d][:], in_=pt3, func=AF.Exp), r=[("psb", 1, 1)], w=[("c_dec", d)])
                P.dve(lambda e, d=d: e.tensor_tensor(out=ekd[d][:], in0=ek[d][:], in1=dec[d][:], op=ALU.mult),
                      r=[("c_ek", d), ("c_dec", d)], w=[("c_ekd", d)])
            P.barrier(C.bar[:])
        import os
        MSTOP = int(os.environ.get("MSTOP", "9"))
        if MSTOP == 1:
            return
        wh = sb("c_wh", [128, 8, 512], BF16)
        T4 = sb("c_T4", [128, 4, S], BF16)
        KD = sb("c_KD", [128, NT, 2, 128], BF16)
        V = sb("c_V", [128, NT, 129], BF16)
        SO = sb("c_SO", [128, NT, 128], BF16)
        HS = sb("c_HS", [128, NT, 128], F32)
        sc4 = [sb("c_sc4%d" % i, [128, 4, 128], BF16) for i in range(2)]
        Cst = [sb("c_Cst%d" % d, [128, 129], F32) for d in range(2)]
        Cbf = [sb("c_Cbf%d" % d, [128, 129], BF16) for d in range(2)]
        STm = [sb("c_STm%d" % d, [128, 128], BF16) for d in range(2)]
        den = [sb("c_den%d" % d, [128, 1], F32) for d in range(2)]
        jk = sb("c_jk", [128, 128], F32)
        ssb = sb("c_ss", [128, NT], F32)
        ytk = sb("c_ytk", [128, NT, 128], BF16)
        P.dve(lambda e: e.memset(V[:], 1.0), w=["c_V"])
        for h in range(4):
            for i, nm in enumerate(("cq", "ck", "cv", "co")):
                P.dma("pool", wh[:, :, i * 128:(i + 1) * 128],
                      I["w_in"][layer, :, COL[nm] + h * 128:COL[nm] + (h + 1) * 128].rearrange("(j p) n -> p j n", p=128), w=[("c_wh", i)])
            def ml_a(t, h=h):
                b = t % 2
                ps = C.psb[b][:, 0:512]
                pk = ("psb", b, 0)
                for j in range(8):
                    P.pe(lambda e, ps=ps, j=j, t=t: e.matmul(ps, lhsT=uT[:, j, t * 128:(t + 1) * 128], rhs=wh[:, j, :],
                                                            start=(j == 0), stop=(j == 7)), r=["c_uT"] + [("c_wh", i4) for i4 in range(4)], w=[pk])
            def ml_b(t, h=h):
                b = t % 2
                ps = C.psb[b][:, 0:512]
                pk = ("psb", b, 0)
                s4 = sc4[b]
                MSKIP = os.environ.get("MSKIP", "")
                for wi, (src, sc) in enumerate(((0, eq[0]), (0, eq[1]), (1, ek[0]), (1, ek[1]))):
                    if "a" in MSKIP:
                        break
                    P.dve(lambda e, ps=ps, s4=s4, wi=wi, src=src, sc=sc, t=t, h=h: e.tensor_scalar(
                        out=s4[:, wi, :], in0=ps[:, src * 128:(src + 1) * 128], scalar1=sc[:, t, h:h + 1], scalar2=None, op0=ALU.mult),
                        r=[pk, ("c_eq", 0), ("c_eq", 1), ("c_ek", 0), ("c_ek", 1)], w=[("c_sc4", b)])
                for d in range(2):
                    if "b" in MSKIP:
                        break
                    P.dve(lambda e, ps=ps, d=d, t=t, h=h: e.tensor_scalar(
                        out=KD[:, t, d, :], in0=ps[:, 128:256], scalar1=ekd[d][:, t, h:h + 1], scalar2=None, op0=ALU.mult),
                        r=[pk, ("c_ekd", d)], w=["c_KD"])
                if "c" not in MSKIP:
                    P.act(lambda e, ps=ps, t=t: e.copy(out=V[:, t, 0:128], in_=ps[:, 256:384]), r=[pk], w=["c_V", pk])
                if "d" not in MSKIP:
                    P.act(lambda e, ps=ps, t=t: e.activation(out=SO[:, t, :], in_=ps[:, 384:512], func=AF.Sigmoid), r=[pk], w=["c_SO", pk])
                pt = C.pst[b]
                if "e" in MSKIP:
                    return
                for wi in range(4):
                    P.pe(lambda e, pt=pt, wi=wi, s4=s4: e.transpose(pt[:, wi * 128:(wi + 1) * 128], s4[:, wi, :], C.ident[:]),
                         r=[("c_sc4", b)], w=[("pst", b)])
                P.act(lambda e, pt=pt, t=t: e.copy(out=T4[:, :, t * 128:(t + 1) * 128], in_=pt[:, 0:512].rearrange("p (j c) -> p j c", j=4)),
                      r=[("pst", b)], w=["c_T4"])
            ml_a(0)
            for t in range(NT):
                if t + 1 < NT:
                    ml_a(t + 1)
                ml_b(t)
            if MSTOP == 2:
                P.barrier(C.bar[:])
                return
            for d in range(2):
                P.dve(lambda e, d=d: e.memset(Cst[d][:], 0.0), w=[("c_Cst", d)])
                P.dve(lambda e, d=d: e.memset(Cbf[d][:], 0.0), w=[("c_Cbf", d)])
            written = set()
            def chain_vars(step, d):
                c = step if d == 0 else NT - 1 - step
                return dict(c=c, mask=(C.maskL if d == 0 else C.maskU),
                            qsT=T4[:, d, c * 128:(c + 1) * 128], ksT=T4[:, 2 + d, c * 128:(c + 1) * 128],
                            psA=C.psb[0][:, d * 512:d * 512 + 128], psB=C.psb[1][:, d * 512:d * 512 + 129],
                            psC=C.psb[2][:, d * 512:d * 512 + 129], kA=("psb", 0, d), kB=("psb", 1, d), kC=("psb", 2, d))

            def chain_pe1(step, h=h):
                for d in range(2):
                    v = chain_vars(step, d)
                    P.pe(lambda e, v=v: e.matmul(v["psA"], lhsT=v["ksT"], rhs=v["qsT"], start=True, stop=True), r=["c_T4"], w=[v["kA"]])
                    P.pe(lambda e, v=v, d=d: e.matmul(v["psC"], lhsT=KD[:, v["c"], d, :], rhs=V[:, v["c"], :], start=True, stop=True),
                         r=["c_KD", "c_V"], w=[v["kC"]])

            def chain_rest(step, h=h):
                for d in range(2):
                    v = chain_vars(step, d)
                    P.dve(lambda e, v=v, d=d: e.tensor_tensor(out=STm[d][:], in0=v["psA"], in1=v["mask"][:], op=ALU.mult),
                          r=[v["kA"]], w=[("c_STm", d)])
                for d in range(2):
                    v = chain_vars(step, d)
                    P.pe(lambda e, v=v, d=d: e.matmul(v["psB"], lhsT=v["qsT"], rhs=Cbf[d][:], start=True, stop=False),
                         r=["c_T4", ("c_Cbf", d)], w=[v["kB"]])
                    P.pe(lambda e, v=v, d=d: e.matmul(v["psB"], lhsT=STm[d][:], rhs=V[:, v["c"], :], start=False, stop=True),
                         r=[("c_STm", d), "c_V"], w=[v["kB"]])
                for d in range(2):
                    v = chain_vars(step, d)
                    c = v["c"]
                    psB, psC, kB, kC = v["psB"], v["psC"], v["kB"], v["kC"]
                    P.dve(lambda e, psC=psC, d=d, c=c, h=h: e.scalar_tensor_tensor(out=Cst[d][:], in0=Cst[d][:], scalar=dec[d][:, c, h:h + 1],
                                                                               in1=psC, op0=ALU.mult, op1=ALU.add),
                          r=[kC, ("c_Cst", d), ("c_dec", d)], w=[("c_Cst", d)])
                    P.act(lambda e, d=d: e.copy(out=Cbf[d][:], in_=Cst[d][:]), r=[("c_Cst", d), kB], w=[("c_Cbf", d)])
                    P.dve(lambda e, psB=psB, d=d: e.tensor_scalar_max(out=den[d][:], in0=psB[:, 128:129], scalar1=1.0), r=[kB], w=[("c_den", d)])
                    P.dve(lambda e, psB=psB, d=d: e.scalar_tensor_tensor(out=den[d][:], in0=psB[:, 128:129], scalar=-1.0, in1=den[d][:],
                                                                         op0=ALU.mult, op1=ALU.max), r=[kB, ("c_den", d)], w=[("c_den", d)])
                    P.dve(lambda e, d=d: e.reciprocal(out=den[d][:], in_=den[d][:]), r=[("c_den", d)], w=[("c_den", d)])
                    if c in written:
                        P.dve(lambda e, psB=psB, d=d, c=c: e.scalar_tensor_tensor(out=HS[:, c, :], in0=psB[:, 0:128], scalar=den[d][:, 0:1],
                                                                                 in1=HS[:, c, :], op0=ALU.mult, op1=ALU.add),
                              r=[kB, ("c_den", d), "c_HS"], w=["c_HS"])
                    else:
                        written.add(c)
                        P.dve(lambda e, psB=psB, d=d, c=c: e.tensor_scalar(out=HS[:, c, :], in0=psB[:, 0:128], scalar1=den[d][:, 0:1],
                                                                          scalar2=None, op0=ALU.mult), r=[kB, ("c_den", d)], w=["c_HS"])

            chain_pe1(0)
            for step in range(NT):
                chain_rest(step)
                if step + 1 < NT:
                    chain_pe1(step + 1)
            if MSTOP == 3:
                P.barrier(C.bar[:])
                return
            for t in range(NT):
                P.act(lambda e, t=t: e.activation(out=jk[:], in_=HS[:, t, :], func=AF.Square, accum_out=ssb[:, t:t + 1]),
                      r=["c_HS"], w=["c_jk", "c_ss"])
            P.act(lambda e: e.activation(out=ssb[:], in_=ssb[:], func=AF.Sqrt, scale=1.0 / 128, bias=C.epsb[:]), r=["c_ss"], w=["c_ss"])
            P.dve(lambda e: e.reciprocal(out=ssb[:], in_=ssb[:]), r=["c_ss"], w=["c_ss"])
            P.dve(lambda e: e.tensor_tensor(out=HS[:], in0=HS[:], in1=ssb[:].unsqueeze(2).to_broadcast([128, NT, 128]), op=ALU.mult),
                  r=["c_HS", "c_ss"], w=["c_HS"])
            P.dve(lambda e, h=h: e.tensor_tensor(out=HS[:], in0=HS[:], in1=gC[:, h * 128:(h + 1) * 128].unsqueeze(1).to_broadcast([128, NT, 128]),
                                                 op=ALU.mult), r=["c_HS", "c_gC"], w=["c_HS"])
            P.dve(lambda e: e.tensor_tensor(out=ytk[:], in0=HS[:], in1=SO[:], op=ALU.mult), r=["c_HS", "c_SO"], w=["c_ytk"])
            stgT = SO[:].rearrange("p t d -> p (t d)")
            for t in range(NT):
                b = (t // 8) % 2
                pt = C.pst[b]
                P.pe(lambda e, pt=pt, t=t: e.transpose(pt[:, (t % 8) * 128:(t % 8 + 1) * 128], ytk[:, t, :], C.ident[:]),
                     r=["c_ytk"], w=[("pst", b)])
                if t % 8 == 7 or t == NT - 1:
                    n8 = t % 8 + 1
                    t0 = (t // 8) * 8
                    P.act(lambda e, pt=pt, n8=n8, t0=t0: e.copy(out=stgT[:, t0 * 128:(t0 + n8) * 128], in_=pt[:, 0:n8 * 128]),
                          r=[("pst", b)], w=["c_SO"])
            P.dma("pool", C.ynT[2][h * 128:(h + 1) * 128, :], stgT, r=["c_SO"])
        P.barrier(C.bar[:])


def phase_merge(C, layer, xsrc):
    P, nc, S, NT, NQC, I = C.P, C.nc, C.S, C.NT, C.NQC, C.I
    with contextlib.ExitStack() as es:
        sb = lambda n, s, d: es.enter_context(nc.sbuf_tensor(_uniq(n), list(s), d))
        Wg = sb("m_Wg", [128, 8, 4096], BF16)
        Wbr = sb("m_Wbr", [128, 16, 1024], BF16)
        Wo = sb("m_Wo", [128, 8, 1024], BF16)
        uTc = [sb("m_uT%d" % i, [128, 8, 512], BF16) for i in range(1)] * 2
        yc = [sb("m_y%d" % i, [128, 16, 512], BF16) for i in range(1)] * 2
        mT = sb("m_mT", [128, 8, 512], BF16)
        sig = [sb("m_sig%d" % i, [128, 512], F32) for i in range(2)]
        acc = sb("m_acc", [128, 512], F32)
        tmp = sb("m_tmp", [128, 512], F32)
        xt = [sb("m_xt%d" % i, [128, 1024], F32) for i in range(2)]
        for n in range(4):
            P.dma("pool", Wg[:, :, n * 1024:(n + 1) * 1024],
                  I["w_in"][layer, :, COL["g"] + n * 1024:COL["g"] + (n + 1) * 1024].rearrange("(j p) n -> p j n", p=128), w=[("m_Wg", n)])
            P.dma("pool", Wbr[:, n * 4:(n + 1) * 4, :], I["w_branch"][layer, n].rearrange("(j p) n -> p j n", p=128), w=[("m_Wbr", n)])
        P.dma("pool", Wo[:], I["w_out"][layer].rearrange("(j p) n -> p j n", p=128), w=["m_Wo"])
        k = 0
        for tc in range(NQC):
            b = 0
            P.dma("sp", uTc[b][:], C.uT_d[:, tc * 512:(tc + 1) * 512].rearrange("(j p) t -> p j t", p=128), w=[("m_uT", b)])
            for n in range(4):
                P.dma("sp", yc[b][:, n * 4:(n + 1) * 4, :], C.ynT[n][:, tc * 512:(tc + 1) * 512].rearrange("(j p) t -> p j t", p=128),
                      w=[("m_y", b, n)])
            for j in range(8):
                for n in range(4):
                    pi = k % 2
                    k += 1
                    psG = C.psb[pi][:, 0:512]
                    psR = C.psb[pi][:, 512:1024]
                    kG, kR = ("psb", pi, 0), ("psb", pi, 1)
                    for dj in range(8):
                        P.pe(lambda e, psG=psG, dj=dj, n=n, j=j, b=b: e.matmul(
                            psG, lhsT=Wg[:, dj, n * 1024 + j * 128:n * 1024 + (j + 1) * 128], rhs=uTc[b][:, dj, :],
                            start=(dj == 0), stop=(dj == 7)), r=[("m_Wg", n), ("m_uT", b)], w=[kG])
                    for cc in range(4):
                        P.pe(lambda e, psR=psR, cc=cc, n=n, j=j, b=b: e.matmul(
                            psR, lhsT=Wbr[:, n * 4 + cc, j * 128:(j + 1) * 128], rhs=yc[b][:, n * 4 + cc, :],
                            start=(cc == 0), stop=(cc == 3)), r=[("m_Wbr", n), ("m_y", b, n)], w=[kR])
                    sg = sig[pi]
                    P.act(lambda e, sg=sg, psG=psG: e.activation(out=sg[:], in_=psG, func=AF.Sigmoid), r=[kG], w=[("m_sig", pi)])
                    if n == 0:
                        P.dve(lambda e, sg=sg, psR=psR: e.tensor_tensor(out=acc[:], in0=sg[:], in1=psR, op=ALU.mult),
                              r=[("m_sig", pi), kR], w=["m_acc"])
                    elif n < 3:
                        P.dve(lambda e, sg=sg, psR=psR: e.tensor_tensor(out=tmp[:], in0=sg[:], in1=psR, op=ALU.mult),
                              r=[("m_sig", pi), kR], w=["m_tmp"])
                        P.dve(lambda e: e.tensor_tensor(out=acc[:], in0=acc[:], in1=tmp[:], op=ALU.add), r=["m_acc", "m_tmp"], w=["m_acc"])
                    else:
                        P.dve(lambda e, sg=sg, psR=psR: e.tensor_tensor(out=tmp[:], in0=sg[:], in1=psR, op=ALU.mult),
                              r=[("m_sig", pi), kR], w=["m_tmp"])
                        P.dve(lambda e, j=j: e.tensor_tensor(out=mT[:, j, :], in0=acc[:], in1=tmp[:], op=ALU.add),
                              r=["m_acc", "m_tmp"], w=["m_mT"])
            for tt in range(4):
                t = tc * 4 + tt
                xb = t % 2
                P.dma("sp", xt[xb][:], xsrc[t * 128:(t + 1) * 128, :], w=[("m_xt", xb)])
                ps = C.psb[2]
                for half in range(2):
                    for dj in range(8):
                        P.pe(lambda e, ps=ps, half=half, dj=dj, tt=tt: e.matmul(
                            ps[:, half * 512:(half + 1) * 512], lhsT=mT[:, dj, tt * 128:(tt + 1) * 128], rhs=Wo[:, dj, half * 512:(half + 1) * 512],
                            start=(dj == 0), stop=(dj == 7)), r=["m_mT", "m_Wo"], w=[("psb", 2, half)])
                P.dve(lambda e, ps=ps, xb=xb: e.tensor_tensor(out=xt[xb][:], in0=xt[xb][:], in1=ps[:], op=ALU.add),
                      r=[("m_xt", xb), ("psb", 2, 0), ("psb", 2, 1)], w=[("m_xt", xb)])
                P.dma("pool", C.xres[t * 128:(t + 1) * 128, :], xt[xb][:], r=[("m_xt", xb)])
        P.barrier(C.bar[:])


def phase_ffn(C, layer):
    P, nc, S, NT, NQC, I = C.P, C.nc, C.S, C.NT, C.NQC, C.I
    TB = min(1024, S)
    NCC = DFF // 128
    NH = NCC // 2
    NB = S // TB
    with contextlib.ExitStack() as es:
        sb = lambda n, s, d: es.enter_context(nc.sbuf_tensor(_uniq(n), list(s), d))
        WuA = sb("f_WuA", [128, 8, NH * 128], BF16)
        WuL = sb("f_WuL", [128, 8, NH * 128], BF16)
        Wd = sb("f_Wd", [128, NH, 1024], BF16)
        cw = sb("f_cw", [128, NCC, 3], F32)
        cb = sb("f_cb", [128, NCC], F32)
        hT = sb("f_hT", [128, NH, TB], BF16)
        vTc = [sb("f_vT%d" % i, [128, 8, TB + 2], BF16) for i in range(2)]
        aS = [sb("f_aS%d" % i, [128, TB + 2], F32) for i in range(2)]
        t1 = [sb("f_t1%d" % i, [128, TB], F32) for i in range(2)]
        z2 = [sb("f_z2%d" % i, [128, TB], F32) for i in range(2)]
        sg = [sb("f_sg%d" % i, [128, TB], F32) for i in range(2)]
        xt = [sb("f_xt%d" % i, [128, 1024], F32) for i in range(2)]
        for j in range(3):
            P.dma("sp", cw[:, :, j:j + 1], I["conv_w"][layer, j:j + 1, :].rearrange("o (c p) -> p c o", p=128), w=["f_cw"],
                  allow_slow_non_contiguous=True)
        P.dma("sp", cb[:].unsqueeze(2), I["conv_b"][layer:layer + 1, :].rearrange("o (c p) -> p c o", p=128), w=["f_cb"],
              allow_slow_non_contiguous=True)
        k = 0
        kv = 0
        for hp in range(2):
            c_lo = hp * NH
            for q4 in range(0, NH * 128, 512):
                n = min(512, NH * 128 - q4)
                P.dma("pool", WuA[:, :, q4:q4 + n],
                      I["w_up"][layer, :, c_lo * 128 + q4:c_lo * 128 + q4 + n].rearrange("(j p) n -> p j n", p=128), w=[("f_WuA", q4 // 512)])
                P.dma("pool", WuL[:, :, q4:q4 + n],
                      I["w_up"][layer, :, DFF + c_lo * 128 + q4:DFF + c_lo * 128 + q4 + n].rearrange("(j p) n -> p j n", p=128), w=[("f_WuL", q4 // 512)])
            P.dma("pool", Wd[:], I["w_down"][layer, c_lo * 128:(c_lo + NH) * 128, :].rearrange("(j p) n -> p j n", p=128), w=["f_Wd"])
            for blk in range(NB):
                t0 = blk * TB
                vb = kv % 2
                kv += 1
                vt = vTc[vb]
                lo = max(t0 - 1, 0)
                hi = min(t0 + TB + 1, S)
                c0 = lo - (t0 - 1)
                if t0 == 0:
                    P.dve(lambda e, vt=vt: e.memset(vt[:, :, 0:1], 0.0), w=[("f_vT", vb)])
                if t0 + TB == S:
                    P.dve(lambda e, vt=vt: e.memset(vt[:, :, TB + 1:TB + 2], 0.0), w=[("f_vT", vb)])
                P.dma("sp", vt[:, :, c0:c0 + (hi - lo)], C.uT_d[:, lo:hi].rearrange("(j p) t -> p j t", p=128), w=[("f_vT", vb)])
                pieces = [(0, 512), (512, 512), (1024, 2)] if TB == 1024 else [(0, 512), (512, 2)]
                def stage1(ci, b):
                    cc = c_lo + ci
                    a = aS[b]
                    for pi, (p0, n) in enumerate(pieces):
                        bi = pi % 3
                        ps = C.psb[bi // 2][:, (bi % 2) * 512:(bi % 2) * 512 + n]
                        pk = ("psb", bi // 2, bi % 2)
                        for j in range(8):
                            P.pe(lambda e, ps=ps, j=j, p0=p0, n=n, ci=ci, vt=vt: e.matmul(
                                ps, lhsT=WuA[:, j, ci * 128:(ci + 1) * 128], rhs=vt[:, j, p0:p0 + n],
                                start=(j == 0), stop=(j == 7)), r=[("f_WuA", ci // 4), ("f_vT", vb)], w=[pk])
                        P.act(lambda e, ps=ps, a=a, p0=p0, n=n: e.copy(out=a[:, p0:p0 + n], in_=ps), r=[pk], w=[("f_aS", b)])
                    T1, Z2 = t1[b], z2[b]
                    P.dve(lambda e, a=a, cc=cc, T1=T1: e.tensor_scalar(out=T1[:], in0=a[:, 1:TB + 1], scalar1=cw[:, cc, 1:2], scalar2=cb[:, cc:cc + 1],
                                                                       op0=ALU.mult, op1=ALU.add), r=[("f_aS", b), "f_cw", "f_cb"], w=[("f_t1", b)])
                    P.dve(lambda e, a=a, cc=cc, T1=T1: e.scalar_tensor_tensor(out=T1[:], in0=a[:, 0:TB], scalar=cw[:, cc, 0:1], in1=T1[:],
                                                                              op0=ALU.mult, op1=ALU.add), r=[("f_aS", b), "f_cw", ("f_t1", b)], w=[("f_t1", b)])
                    P.dve(lambda e, a=a, cc=cc, T1=T1: e.scalar_tensor_tensor(out=T1[:], in0=a[:, 2:TB + 2], scalar=cw[:, cc, 2:3], in1=T1[:],
                                                                              op0=ALU.mult, op1=ALU.add), r=[("f_aS", b), "f_cw", ("f_t1", b)], w=[("f_t1", b)])
                    P.pool(lambda e, T1=T1, Z2=Z2: e.tensor_tensor(out=Z2[:], in0=T1[:], in1=T1[:], op=ALU.mult), r=[("f_t1", b)], w=[("f_z2", b)])
                    P.pool(lambda e, Z2=Z2: e.tensor_scalar(out=Z2[:], in0=Z2[:], scalar1=0.044715, scalar2=1.0, op0=ALU.mult, op1=ALU.add),
                           r=[("f_z2", b)], w=[("f_z2", b)])
                    P.pool(lambda e, T1=T1, Z2=Z2: e.tensor_tensor(out=Z2[:], in0=Z2[:], in1=T1[:], op=ALU.mult), r=[("f_z2", b), ("f_t1", b)], w=[("f_z2", b)])

                def stage2(ci, b):
                    T1, Z2, SG = t1[b], z2[b], sg[b]
                    P.act(lambda e, Z2=Z2, SG=SG: e.activation(out=SG[:], in_=Z2[:], func=AF.Sigmoid, scale=1.5957691216057308),
                          r=[("f_z2", b)], w=[("f_sg", b)])
                    P.dve(lambda e, T1=T1, SG=SG: e.tensor_tensor(out=SG[:], in0=SG[:], in1=T1[:], op=ALU.mult), r=[("f_sg", b), ("f_t1", b)], w=[("f_sg", b)])
                    for li in range(TB // 512):
                        bi = 3 + (li % 2)
                        ps = C.psb[bi // 2][:, (bi % 2) * 512:(bi % 2) * 512 + 512]
                        pk = ("psb", bi // 2, bi % 2)
                        for j in range(8):
                            P.pe(lambda e, ps=ps, j=j, li=li, ci=ci, vt=vt: e.matmul(
                                ps, lhsT=WuL[:, j, ci * 128:(ci + 1) * 128], rhs=vt[:, j, 1 + li * 512:1 + (li + 1) * 512],
                                start=(j == 0), stop=(j == 7)), r=[("f_WuL", ci // 4), ("f_vT", vb)], w=[pk])
                        P.dve(lambda e, ps=ps, li=li, ci=ci, SG=SG: e.tensor_tensor(out=hT[:, ci, li * 512:(li + 1) * 512], in0=SG[:, li * 512:(li + 1) * 512],
                                                                                    in1=ps, op=ALU.mult), r=[("f_sg", b), pk], w=["f_hT"])

                bs = []
                for ci in range(NH):
                    bs.append(k % 2)
                    k += 1
                stage1(0, bs[0])
                for ci in range(NH):
                    if ci + 1 < NH:
                        stage1(ci + 1, bs[ci + 1])
                    stage2(ci, bs[ci])
                for tt in range(TB // 128):
                    t = t0 // 128 + tt
                    xb = t % 2
                    P.dma("sp", xt[xb][:], C.xres[t * 128:(t + 1) * 128, :], w=[("f_xt", xb)])
                    ps = C.psb[2]
                    for half in range(2):
                        for ci in range(NH):
                            P.pe(lambda e, ps=ps, half=half, ci=ci, tt=tt: e.matmul(
                                ps[:, half * 512:(half + 1) * 512], lhsT=hT[:, ci, tt * 128:(tt + 1) * 128], rhs=Wd[:, ci, half * 512:(half + 1) * 512],
                                start=(ci == 0), stop=(ci == NH - 1)), r=["f_hT", "f_Wd"], w=[("psb", 2, half)])
                    P.dve(lambda e, ps=ps, xb=xb: e.tensor_tensor(out=xt[xb][:], in0=xt[xb][:], in1=ps[:], op=ALU.add),
                          r=[("f_xt", xb), ("psb", 2, 0), ("psb", 2, 1)], w=[("f_xt", xb)])
                    P.dma("pool", C.xres[t * 128:(t + 1) * 128, :], xt[xb][:], r=[("f_xt", xb)])
        P.barrier(C.bar[:])


def phase_out(C, dst):
    P, nc, S, NT, I = C.P, C.nc, C.S, C.NT, C.I
    last = []
    with contextlib.ExitStack() as es:
        sb = lambda n, s, d: es.enter_context(nc.sbuf_tensor(_uniq(n), list(s), d))
        gB = sb("o_gB", [128, D], F32)
        xt = [sb("o_xt%d" % i, [128, D], F32) for i in range(2)]
        yo = [sb("o_y%d" % i, [128, D], F32) for i in range(2)]
        junk = sb("o_junk", [128, D], F32)
        ssq = [sb("o_ssq%d" % i, [128, 1], F32) for i in range(2)]
        P.dma("sp", gB[:], I["final_norm_g"].partition_broadcast(128), w=["o_gB"])
        for t in range(NT):
            b = t % 2
            P.dma("sp", xt[b][:], C.xres[t * 128:(t + 1) * 128, :], w=[("o_xt", b)])
            P.act(lambda e, b=b: e.activation(out=junk[:], in_=xt[b][:], func=AF.Square, accum_out=ssq[b][:]),
                  r=[("o_xt", b)], w=["o_junk", ("o_ssq", b)])
            P.act(lambda e, b=b: e.activation(out=ssq[b][:], in_=ssq[b][:], func=AF.Sqrt, scale=1.0 / D, bias=C.epsb[:]),
                  r=[("o_ssq", b)], w=[("o_ssq", b)])
            P.dve(lambda e, b=b: e.reciprocal(out=ssq[b][:], in_=ssq[b][:]), r=[("o_ssq", b)], w=[("o_ssq", b)])
            P.dve(lambda e, b=b: e.scalar_tensor_tensor(out=yo[b][:], in0=xt[b][:], scalar=ssq[b][:, 0:1], in1=gB[:],
                                                        op0=ALU.mult, op1=ALU.mult),
                  r=[("o_xt", b), ("o_ssq", b), "o_gB"], w=[("o_y", b)])
            last.append(P.dma("pool", dst[t * 128:(t + 1) * 128, :], yo[b][:], r=[("o_y", b)]))
        P.barrier(C.bar[:])
    return last


_CACHE = {}


def kernel(**inputs):
    S = SEQ
    inp = {k: np.asarray(v) for k, v in inputs.items()}
    nb = inp["x"].shape[0]
    nc, st = build(S, DEPTH, stage="full")
    consts = host_consts(S)
    in_maps = [make_in_map(inp, inp["x"][b], S, consts) for b in range(nb)]
    res = run_bass_kernel_spmd(nc, in_maps, core_ids=list(range(nb)))
    out = np.stack([np.asarray(r["out"], dtype=np.float32) for r in res.results], axis=0)
    return out


def make_in_map(inp, x, S, consts=None):
    c = consts if consts is not None else host_consts(S)
    f = lambda a: np.ascontiguousarray(np.asarray(a, dtype=np.float32))
    m = {
        "x": f(x),
        "norm_mix_g": f(inp["norm_mix_g"]),
        "w_in": f(inp["w_in"]),
        "mlstm_gate_bias": f(inp["mlstm_gate_bias"]).reshape(DEPTH, 16),
        "qk_norm_g": f(inp["qk_norm_g"]).reshape(DEPTH, 128),
        "mlstm_norm_g": f(inp["mlstm_norm_g"]),
        "diff_lambda": f(inp["diff_lambda"]).reshape(DEPTH, 256),
        "diff_norm_g": f(inp["diff_norm_g"]),
        "rel_bias": f(inp["rel_bias"]).reshape(1, 128),
        "w_branch": f(inp["w_branch"]),
        "w_out": f(inp["w_out"]),
        "norm_ffn_g": f(inp["norm_ffn_g"]),
        "w_up": f(inp["w_up"]),
        "conv_w": f(inp["conv_w"]),
        "conv_b": f(inp["conv_b"]),
        "w_down": f(inp["w_down"]),
        "final_norm_g": f(inp["final_norm_g"]).reshape(1, D),
    }
    m.update(c)
    return m
```

```python
import math
import contextlib
import numpy as np
import ml_dtypes
import concourse.bass as bass
import concourse.mybir as mybir
from concourse.bass_utils import run_bass_kernel_spmd

F32 = mybir.dt.float32
BF16 = mybir.dt.bfloat16
AF = mybir.ActivationFunctionType
ALU = mybir.AluOpType
AX = mybir.AxisListType

D = 1024
DEPTH = 2
BATCH = 4
SEQ = 4096
IN_W = 8976
DFF = 2816
EPS = 1e-6
COL = dict(a=0, bq=512, bk=1024, bv=1152, cq=1280, ck=1792, cv=2304, co=2816, cg=3328,
           dq=3344, dk=3856, dv=4368, g=4880)

COMPUTE = ("pe", "act", "dve", "pool")
NDMASEM = 8


class Op:
    __slots__ = ("eng", "fn", "deps", "signal", "ev", "is_dma")

    def __init__(self, eng, fn, is_dma=False):
        self.eng = eng
        self.fn = fn
        self.deps = set()
        self.signal = False
        self.ev = None
        self.is_dma = is_dma


class Prog:
    def __init__(self, nc, same_engine_sync=True):
        self.nc = nc
        self.ops = []
        self.last_w = {}
        self.readers = {}
        self.same_engine_sync = same_engine_sync
        self.engines = {"pe": nc.tensor, "act": nc.scalar, "dve": nc.vector,
                        "pool": nc.gpsimd, "sp": nc.sync}
        self.since_barrier = []
        self.barrier_op = None

    def op(self, eng, fn, r=(), w=(), dma=False):
        o = Op(eng, fn, is_dma=dma)
        idx = len(self.ops)
        if self.barrier_op is not None:
            o.deps.add(self.barrier_op)
        for k in r:
            lw = self.last_w.get(k)
            if lw is not None:
                o.deps.add(lw)
        for k in w:
            lw = self.last_w.get(k)
            if lw is not None:
                o.deps.add(lw)
            for rd in self.readers.get(k, ()):
                o.deps.add(rd)
        for k in r:
            self.readers.setdefault(k, []).append(idx)
        for k in w:
            self.last_w[k] = idx
            self.readers[k] = []
        self.ops.append(o)
        self.since_barrier.append(idx)
        return idx

    def barrier(self, scratch):
        prev = list(self.since_barrier)
        o = Op("dve", lambda e: e.memset(scratch, 0.0))
        if self.barrier_op is not None:
            o.deps.add(self.barrier_op)
        o.deps.update(prev)
        idx = len(self.ops)
        self.ops.append(o)
        self.barrier_op = idx
        self.since_barrier = []
        self.last_w = {}
        self.readers = {}
        return idx

    def pe(self, fn, r=(), w=()):
        return self.op("pe", fn, r, w)

    def act(self, fn, r=(), w=()):
        return self.op("act", fn, r, w)

    def dve(self, fn, r=(), w=()):
        return self.op("dve", fn, r, w)

    def pool(self, fn, r=(), w=()):
        return self.op("pool", fn, r, w)

    def dma(self, q, out, in_, r=(), w=(), **kw):
        return self.op(q, lambda e: e.dma_start(out=out, in_=in_, **kw), r, w, dma=True)

    def emit(self, final_wait_ops=()):
        nc = self.nc
        ops = self.ops
        for i, o in enumerate(ops):
            keep = set()
            for d in o.deps:
                p = ops[d]
                if p.is_dma:
                    keep.add(d)
                    continue
                if p.eng == o.eng and not o.is_dma:
                    if o.eng == "pe":
                        continue
                    if not self.same_engine_sync:
                        continue
                keep.add(d)
            latest = {}
            keep2 = set()
            for d in keep:
                p = ops[d]
                if p.is_dma:
                    keep2.add(d)
                else:
                    if p.eng not in latest or d > latest[p.eng]:
                        latest[p.eng] = d
            keep2.update(latest.values())
            o.deps = keep2
            for d in keep2:
                ops[d].signal = True
        for d in final_wait_ops:
            ops[d].signal = True
        es = contextlib.ExitStack()
        sems = {}
        for e in COMPUTE:
            sems[e] = es.enter_context(nc.semaphore("s_" + e))
        dsems = {}
        for q in ("sp", "pool", "act"):
            dsems[q] = [es.enter_context(nc.semaphore("d_%s%d" % (q, j))) for j in range(NDMASEM)]
        cnt = {e: 0 for e in COMPUTE}
        dcnt = {q: 0 for q in dsems}
        seen = {e: {} for e in self.engines}
        nwait = 0
        plan = {e: [] for e in self.engines}

        def need(engname, ev, waits):
            nonlocal nwait
            sem, val = ev
            s = seen[engname]
            if s.get(id(sem), 0) >= val:
                return
            s[id(sem)] = val
            waits.append((sem, val))
            nwait += 1

        for i, o in enumerate(ops):
            waits = []
            for d in sorted(o.deps):
                need(o.eng, ops[d].ev, waits)
            if o.is_dma:
                q = o.eng
                j = dcnt[q]
                dcnt[q] += 1
                sem = dsems[q][j % NDMASEM]
                if j >= NDMASEM:
                    need(q, (sem, 16 * (j // NDMASEM)), waits)
                o.ev = (sem, 16 * (j // NDMASEM + 1))
                plan[o.eng].append((waits, o, sem, 16))
            else:
                if o.signal:
                    cnt[o.eng] += 1
                    o.ev = (sems[o.eng], cnt[o.eng])
                    plan[o.eng].append((waits, o, sems[o.eng], 1))
                else:
                    plan[o.eng].append((waits, o, None, 0))
        fw = []
        for d in final_wait_ops:
            need("sp", ops[d].ev, fw)
        plan["sp"].append((fw, None, None, 0))

        def run_engine(name, e):
            for waits, o, sem, inc in plan[name]:
                for (ws, wv) in waits:
                    e.wait_ge(ws, wv)
                if o is None:
                    continue
                ins = o.fn(e)
                if sem is not None:
                    ins.then_inc(sem, inc)

        with nc.Block() as block:
            @block.sync
            def _(e):
                run_engine("sp", e)

            @block.tensor
            def _(e):
                run_engine("pe", e)

            @block.scalar
            def _(e):
                run_engine("act", e)

            @block.vector
            def _(e):
                run_engine("dve", e)

            @block.gpsimd
            def _(e):
                run_engine("pool", e)
        self.stats = dict(n_ops=len(ops), n_wait=nwait, cnt=dict(cnt), dcnt=dict(dcnt))
        es.close()
        return self.stats


_UNIQ = [0]


def _uniq(n):
    _UNIQ[0] += 1
    return "%s_%d" % (n, _UNIQ[0])


class Ring:
    def __init__(self, aps, name):
        self.aps = aps
        self.name = name
        self.i = 0

    def next(self):
        j = self.i % len(self.aps)
        self.i += 1
        return self.aps[j], (self.name, j)


def rel_bucket_np(rel):
    half = 16
    max_exact = 8
    ret = np.where(rel > 0, half, 0)
    n = np.abs(rel)
    nf = np.maximum(n, 1).astype(np.float32)
    large = max_exact + (np.log(nf / np.float32(max_exact)) / np.float32(math.log(128 / max_exact))
                         * np.float32(half - max_exact)).astype(np.int32)
    large = np.minimum(large, half - 1)
    return ret + np.where(n < max_exact, n, large)


def host_consts(S):
    c = {}
    k = np.arange(S, dtype=np.int64)
    ks = (k[:, None] * k[None, :]) % S
    ang = (2.0 * np.pi / S) * ks.astype(np.float64)
    c["dftc"] = np.cos(ang).astype(np.float32).astype(ml_dtypes.bfloat16)
    c["dfts"] = (-np.sin(ang)).astype(np.float32).astype(ml_dtypes.bfloat16)
    j = np.arange(64)
    a64 = 2.0 * np.pi * ((j[:, None] * j[None, :]) % 64) / 64.0
    nrm = 1.0 / math.sqrt(S * 64.0)
    bd = np.zeros((2, 128, 128), np.float64)
    for g in range(2):
        bd[0, g * 64:(g + 1) * 64, g * 64:(g + 1) * 64] = np.cos(a64) * nrm
        bd[1, g * 64:(g + 1) * 64, g * 64:(g + 1) * 64] = np.sin(a64) * nrm
    c["bdcs"] = bd.astype(np.float32).astype(ml_dtypes.bfloat16)
    rows = S // 64
    row_id = np.repeat(np.arange(rows, dtype=np.float32), 64)
    col_id = np.tile(np.arange(64, dtype=np.float32), rows)
    inv = (np.float32(10000.0) ** (-np.arange(16, dtype=np.float32) / np.float32(16))).astype(np.float32)
    angr = np.concatenate([row_id[:, None] * inv, col_id[:, None] * inv], axis=-1).astype(np.float32)
    c["ropec"] = np.cos(angr).astype(np.float32)
    c["ropes"] = np.sin(angr).astype(np.float32)
    i = np.arange(128)[:, None]
    m = np.arange(1152)[None, :]
    c["relf"] = (i - m + 512).astype(np.float32)
    return c


def bias_steps():
    rel = np.arange(-700, 701)
    b = rel_bucket_np(rel)
    seq = [int(b[0])]
    thr = []
    for idx in range(1, len(rel)):
        if b[idx] != b[idx - 1]:
            seq.append(int(b[idx]))
            thr.append(int(rel[idx]))
    return seq, thr


class Ctx:
    pass


def build(S, depth, stage="full", taps=()):
    NT = S // 128
    NQC = S // 512
    nc = bass.Bass("TRN2", target_bir_lowering=False)
    P = Prog(nc)
    C = Ctx()
    C.nc, C.P, C.S, C.NT, C.NQC = nc, P, S, NT, NQC

    def din(name, shape, dt=F32):
        return nc.dram_tensor(name, list(shape), dt, kind="ExternalInput").ap()

    I = {}
    I["x"] = din("x", [S, D])
    I["norm_mix_g"] = din("norm_mix_g", [DEPTH, D])
    I["w_in"] = din("w_in", [DEPTH, D, IN_W])
    I["mlstm_gate_bias"] = din("mlstm_gate_bias", [DEPTH, 16])
    I["qk_norm_g"] = din("qk_norm_g", [DEPTH, 128])
    I["mlstm_norm_g"] = din("mlstm_norm_g", [DEPTH, 512])
    I["diff_lambda"] = din("diff_lambda", [DEPTH, 256])
    I["diff_norm_g"] = din("diff_norm_g", [DEPTH, 128])
    I["rel_bias"] = din("rel_bias", [1, 128])
    I["w_branch"] = din("w_branch", [DEPTH, 4, 512, D])
    I["w_out"] = din("w_out", [DEPTH, D, D])
    I["norm_ffn_g"] = din("norm_ffn_g", [DEPTH, D])
    I["w_up"] = din("w_up", [DEPTH, D, 2 * DFF])
    I["conv_w"] = din("conv_w", [DEPTH, 3, DFF])
    I["conv_b"] = din("conv_b", [DEPTH, DFF])
    I["w_down"] = din("w_down", [DEPTH, DFF, D])
    I["final_norm_g"] = din("final_norm_g", [1, D])
    I["dftc"] = din("dftc", [S, S], BF16)
    I["dfts"] = din("dfts", [S, S], BF16)
    I["bdcs"] = din("bdcs", [2, 128, 128], BF16)
    I["ropec"] = din("ropec", [S, 32])
    I["ropes"] = din("ropes", [S, 32])
    I["relf"] = din("relf", [128, 1152])
    C.I = I
    out = nc.dram_tensor("out", [S, D], F32, kind="ExternalOutput").ap()
    C.tap = {}
    for (nm, shp, dt) in taps:
        C.tap[nm] = nc.dram_tensor("tap_" + nm, list(shp), dt, kind="ExternalOutput").ap()

    def dscr(name, shape, dt=BF16):
        return nc.dram_tensor(name, list(shape), dt).ap()

    C.xres = dscr("xres", [S, D], F32)
    C.uT_d = dscr("uT_d", [D, S])
    C.ynT = [dscr("ynT%d" % n, [512, S]) for n in range(4)]

    ges = contextlib.ExitStack()
    C.ges = ges

    def gsb(name, shape, dt):
        return ges.enter_context(nc.sbuf_tensor(name, list(shape), dt))

    C.psb = [ges.enter_context(nc.psum_tensor("psb%d" % i, [128, 1024], F32)) for i in range(3)]
    C.pst = [ges.enter_context(nc.psum_tensor("pst%d" % i, [128, 1024], BF16)) for i in range(2)]
    C.identf = gsb("identf", [128, 128], F32)
    C.ident = gsb("ident", [128, 128], BF16)
    C.maskL = gsb("maskL", [128, 128], F32)
    C.maskU = gsb("maskU", [128, 128], F32)
    C.onesf = gsb("onesf", [128, 128], F32)
    C.maskLb = gsb("maskLb", [128, 128], BF16)
    C.maskUb = gsb("maskUb", [128, 128], BF16)
    C.onesb = gsb("onesb", [128, 128], BF16)
    C.bar = gsb("bar", [128, 1], F32)
    C.rbB = gsb("rbB", [128, 128], F32)
    C.epsb = gsb("epsb", [128, 1], F32)
    C.lnk = gsb("lnk", [128, 1], F32)

    P.pool(lambda e: e.iota(C.identf[:], [[1, 128]], 0, channel_multiplier=-1,
                            allow_small_or_imprecise_dtypes=True), w=["identf"])
    P.dve(lambda e: e.tensor_single_scalar(out=C.ident[:], in_=C.identf[:], scalar=0.0, op=ALU.is_equal),
          r=["identf"], w=["ident"])
    P.dve(lambda e: e.tensor_single_scalar(out=C.maskL[:], in_=C.identf[:], scalar=0.0, op=ALU.is_ge),
          r=["identf"], w=["maskL"])
    P.dve(lambda e: e.tensor_single_scalar(out=C.maskU[:], in_=C.identf[:], scalar=0.0, op=ALU.is_le),
          r=["identf"], w=["maskU"])
    P.dve(lambda e: e.memset(C.onesf[:], 1.0), w=["onesf"])
    P.dve(lambda e: e.memset(C.onesb[:], 1.0), w=["onesb"])
    P.dve(lambda e: e.tensor_copy(out=C.maskLb[:], in_=C.maskL[:]), r=["maskL"], w=["maskLb"])
    P.dve(lambda e: e.tensor_copy(out=C.maskUb[:], in_=C.maskU[:]), r=["maskU"], w=["maskUb"])
    P.dve(lambda e: e.memset(C.epsb[:], EPS), w=["epsb"])
    P.dve(lambda e: e.memset(C.lnk[:], -0.5 * math.log(128.0)), w=["lnk"])
    P.dma("sp", C.rbB[:], I["rel_bias"].partition_broadcast(128), w=["rbB"])
    P.barrier(C.bar[:])

    setup_bias_tables(C)
    fin = []
    done = False
    for layer in range(depth):
        xsrc = I["x"] if layer == 0 else C.xres
        phase_norm(C, xsrc, I["norm_mix_g"][layer:layer + 1, :], C.uT_d)
        if stage == "norm":
            fin.append(copy_dram(C, C.tap["uT"], C.uT_d, [D, S], BF16)); break
        if stage in ("gqa", "full", "merge", "layer"):
            phase_gqa(C, layer)
        if stage == "gqa":
            fin.append(copy_dram(C, C.tap["ybT"], C.ynT[1], [512, S], BF16)); break
        if stage in ("diff", "full", "merge", "layer"):
            phase_diff(C, layer)
        if stage == "diff":
            fin.append(copy_dram(C, C.tap["ydT"], C.ynT[3], [512, S], BF16)); break
        if stage in ("four", "full", "merge", "layer"):
            phase_four(C, layer)
        if stage == "four":
            fin.append(copy_dram(C, C.tap["yaT"], C.ynT[0], [512, S], BF16)); break
        if stage in ("mlstm", "full", "merge", "layer"):
            phase_mlstm(C, layer)
        if stage == "mlstm":
            fin.append(copy_dram(C, C.tap["ycT"], C.ynT[2], [512, S], BF16)); break
        phase_merge(C, layer, xsrc)
        if stage == "merge":
            fin.append(copy_dram(C, C.tap["xmid"], C.xres, [S, D], F32)); break
        phase_norm(C, C.xres, I["norm_ffn_g"][layer:layer + 1, :], C.uT_d)
        phase_ffn(C, layer)
        if stage == "layer":
            fin.append(copy_dram(C, C.tap["xl"], C.xres, [S, D], F32)); break
    if stage == "full":
        fin.extend(phase_out(C, out))
    st = P.emit(final_wait_ops=fin)
    ges.close()
    return nc, st


def copy_dram(C, dst, src, shape, dt):
    P, nc = C.P, C.nc
    rows, cols = shape
    last = None
    with contextlib.ExitStack() as es:
        t = es.enter_context(nc.sbuf_tensor(_uniq("cpy"), [128, cols], dt))
        for r0 in range(0, rows, 128):
            P.dma("sp", t[:], src[r0:r0 + 128, :], w=["cpy"])
            last = P.dma("sp", dst[r0:r0 + 128, :], t[:], r=["cpy"])
        P.barrier(C.bar[:])
    return last


def phase_norm(C, xsrc, g_row, dstT):
    P, nc, S, NT = C.P, C.nc, C.S, C.NT
    with contextlib.ExitStack() as es:
        sb = lambda n, s, d: es.enter_context(nc.sbuf_tensor(_uniq(n), list(s), d))
        gB = sb("n_gB", [128, D], F32)
        xt = [sb("n_xt%d" % i, [128, D], F32) for i in range(4)]
        junk = sb("n_junk", [128, D], F32)
        ub = [sb("n_ub%d" % i, [128, D], BF16) for i in range(4)]
        ssq = [sb("n_ssq%d" % i, [128, 1], F32) for i in range(4)]
        stg = [sb("n_stg%d" % i, [128, 8, 512], BF16) for i in range(2)]
        P.dma("sp", gB[:], g_row.partition_broadcast(128), w=["n_gB"])
        def stage_a(t):
            b = t % 4
            P.dma("sp", xt[b][:], xsrc[t * 128:(t + 1) * 128, :], w=[("n_xt", b)])
            P.act(lambda e, b=b: e.activation(out=junk[:], in_=xt[b][:], func=AF.Square, accum_out=ssq[b][:]),
                  r=[("n_xt", b)], w=["n_junk", ("n_ssq", b)])
            P.act(lambda e, b=b: e.activation(out=ssq[b][:], in_=ssq[b][:], func=AF.Sqrt, scale=1.0 / D, bias=C.epsb[:]),
                  r=[("n_ssq", b)], w=[("n_ssq", b)])
            P.dve(lambda e, b=b: e.reciprocal(out=ssq[b][:], in_=ssq[b][:]), r=[("n_ssq", b)], w=[("n_ssq", b)])
            P.dve(lambda e, b=b: e.scalar_tensor_tensor(out=ub[b][:], in0=xt[b][:], scalar=ssq[b][:, 0:1], in1=gB[:],
                                                        op0=ALU.mult, op1=ALU.mult),
                  r=[("n_xt", b), ("n_ssq", b), "n_gB"], w=[("n_ub", b)])
            pb_ = t % 2
            pt = C.pst[pb_]
            for j in range(8):
                P.pe(lambda e, b=b, j=j, pt=pt: e.transpose(pt[:, j * 128:(j + 1) * 128], ub[b][:, j * 128:(j + 1) * 128], C.ident[:]),
                     r=[("n_ub", b)], w=[("pst", pb_)])

        def stage_b(t):
            pb_ = t % 2
            pt = C.pst[pb_]
            sgi = (t // 4) % 2
            tt = t % 4
            P.act(lambda e, pt=pt, sgi=sgi, tt=tt: e.copy(out=stg[sgi][:, :, tt * 128:(tt + 1) * 128],
                                                          in_=pt[:].rearrange("p (j c) -> p j c", j=8)),
                  r=[("pst", pb_)], w=[("n_stg", sgi)])
            if tt == 3:
                t0 = (t // 4) * 512
                P.dma("pool", dstT[:, t0:t0 + 512].rearrange("(j p) t -> p j t", p=128), stg[sgi][:],
                      r=[("n_stg", sgi)])

        stage_a(0)
        for t in range(NT):
            if t + 1 < NT:
                stage_a(t + 1)
            stage_b(t)
        P.barrier(C.bar[:])


def load_T(C, dst, srcT, ncc, q="sp", key=None):
    C.P.dma(q, dst[:], srcT.rearrange("(j p) t -> p j t", p=128), w=[key])


def load_w(C, dst, wsrc, key, q="pool"):
    C.P.dma(q, dst, wsrc.rearrange("(j p) n -> p j n", p=128), w=[key])


def attention(C, QT, KT, qk_key, Vaug, v_key, dv, scale, ptile, out_cb, bias_fn=None, obufs=None, feat=False):
    P, S, NT, NQC = C.P, C.S, C.NT, C.NQC
    NP = NT // 2
    steps = [(qc, sp) for qc in range(NQC) for sp in range(NP)]
    po = C.psb[2]

    def bias_of(st, qc):
        return bias_fn(st, qc) if bias_fn is not None else None

    def issue_qk(i):
        qc, sp = steps[i]
        buf = i % 2
        ps = C.psb[buf]
        for u in range(2):
            st = 2 * sp + u
            psS = ps[:, u * 512:(u + 1) * 512]
            kS = ("psb", buf, u)
            bias = bias_of(st, qc)
            band = bias is not None and bias[0] == "band"
            P.pe(lambda e, psS=psS, st=st, qc=qc, band=band: e.matmul(
                psS, lhsT=KT[:, st * 128:(st + 1) * 128], rhs=QT[:, qc * 512:(qc + 1) * 512],
                start=True, stop=not band), r=[qk_key], w=[kS])
            if band:
                P.pe(lambda e, psS=psS, bt=bias[1]: e.matmul(psS, lhsT=C.ident[:], rhs=bt, start=False, stop=True),
                     r=[bias[2], "ident"], w=[kS])

    def issue_exp(i):
        qc, sp = steps[i]
        buf = i % 2
        ps = C.psb[buf]
        pt, pk = ptile.next()
        b0 = bias_of(2 * sp, qc)
        b1 = bias_of(2 * sp + 1, qc)
        c0 = b0 if (b0 is not None and b0[0] == "const") else None
        c1 = b1 if (b1 is not None and b1[0] == "const") else None
        same = (c0 is None and c1 is None)
        if same:
            if c0 is None:
                P.act(lambda e, pt=pt, ps=ps: e.activation(out=pt[:, 0:1024], in_=ps[:, 0:1024], func=AF.Exp, scale=scale),
                      r=[("psb", buf, 0), ("psb", buf, 1)], w=[pk])
            else:
                P.act(lambda e, pt=pt, ps=ps, bb=c0[1]: e.activation(out=pt[:, 0:1024], in_=ps[:, 0:1024], func=AF.Exp, scale=scale, bias=bb),
                      r=[("psb", buf, 0), ("psb", buf, 1), c0[2]], w=[pk])
        else:
            for u, cb in enumerate((c0, c1)):
                if cb is None:
                    P.act(lambda e, pt=pt, ps=ps, u=u: e.activation(out=pt[:, u * 512:(u + 1) * 512], in_=ps[:, u * 512:(u + 1) * 512],
                                                                   func=AF.Exp, scale=scale), r=[("psb", buf, u)], w=[pk])
                else:
                    P.act(lambda e, pt=pt, ps=ps, u=u, bb=cb[1]: e.activation(out=pt[:, u * 512:(u + 1) * 512], in_=ps[:, u * 512:(u + 1) * 512],
                                                                             func=AF.Exp, scale=scale, bias=bb),
                          r=[("psb", buf, u), cb[2]], w=[pk])
        return pt, pk

    def issue_pv_feat(i, pt, pk):
        qc, sp = steps[i]
        poT = po[0:dv + 1, 0:512]
        for u in range(2):
            st = 2 * sp + u
            P.pe(lambda e, pt=pt, st=st, u=u, sp=sp: e.matmul(poT, lhsT=Vaug(st), rhs=pt[:, u * 512:(u + 1) * 512],
                                                             start=(sp == 0 and u == 0), stop=(st == NT - 1)),
                 r=[pk, v_key], w=[("psb", 2, 0)])
        if sp == NP - 1:
            late = out_cb(qc, poT, ("psb", 2, 0))
            if late is not None:
                deferred.append([2, late])

    def issue_pv(i, pt, pk):
        if feat:
            return issue_pv_feat(i, pt, pk)
        qc, sp = steps[i]
        for u in range(2):
            st = 2 * sp + u
            for qt in range(4):
                o_ap = po[:, qt * 256:qt * 256 + dv + 1]
                P.pe(lambda e, o_ap=o_ap, pt=pt, qt=qt, st=st, u=u, sp=sp: e.matmul(
                    o_ap, lhsT=pt[:, u * 512 + qt * 128:u * 512 + (qt + 1) * 128], rhs=Vaug(st),
                    start=(sp == 0 and u == 0 and qt % 2 == 0), stop=(st == NT - 1), skip_group_check=True),
                    r=[pk, v_key], w=[("psb", 2, qt // 2)])
        if sp == NP - 1:
            ob, ok = obufs.next()
            for bk in range(2):
                P.dve(lambda e, ob=ob, bk=bk: e.tensor_copy(out=ob[:, bk * 512:(bk + 1) * 512], in_=po[:, bk * 512:(bk + 1) * 512]),
                      r=[("psb", 2, bk)], w=[ok])
            for qt in range(4):
                out_cb(qc, qt, ob[:, qt * 256:qt * 256 + dv + 1], ok)

    n = len(steps)
    deferred = []
    issue_qk(0)
    for i in range(n):
        if i + 1 < n:
            issue_qk(i + 1)
        pt, pk = issue_exp(i)
        issue_pv(i, pt, pk)
        for dfr in list(deferred):
            dfr[0] -= 1
            if dfr[0] < 0:
                dfr[1]()
                deferred.remove(dfr)
    for dfr in deferred:
        dfr[1]()


def phase_gqa(C, layer):
    P, nc, S, NT, NQC, I = C.P, C.nc, C.S, C.NT, C.NQC, C.I
    with contextlib.ExitStack() as es:
        sb = lambda n, s, d: es.enter_context(nc.sbuf_tensor(_uniq(n), list(s), d))
        QT = sb("b_QT", [128, 4, S], BF16)
        KT = sb("b_KT", [128, 2, S], BF16)
        V = sb("b_V", [128, NT, 2, 65], BF16)
        ropec = sb("b_rc", [128, NT, 32], F32)
        ropes = sb("b_rs", [128, NT, 32], F32)
        g640 = sb("b_g", [128, 10, 64], F32)
        gq = sb("b_gq", [128, 128], F32)
        with contextlib.ExitStack() as es2:
            sb2 = lambda n, s, d: es2.enter_context(nc.sbuf_tensor(_uniq(n), list(s), d))
            uT = sb2("b_uT", [128, 8, S], BF16)
            wB = sb2("b_w", [128, 8, 768], BF16)
            sq = sb2("b_sq", [128, 10, 64], F32)
            ss = sb2("b_ss", [128, 10], F32)
            qn = sb2("b_qn", [128, 10, 64], F32)
            t1 = sb2("b_t1", [128, 10, 32], F32)
            t2 = sb2("b_t2", [128, 10, 32], F32)
            qr = [sb2("b_qr%d" % i, [128, 12, 64], BF16) for i in range(2)]
            load_T(C, uT, C.uT_d, 8, key="b_uT")
            load_w(C, wB[:], I["w_in"][layer, :, COL["bq"]:COL["bq"] + 768], "b_w")
            P.dma("sp", ropec[:], I["ropec"].rearrange("(t p) c -> p t c", p=128), w=["b_rc"])
            P.dma("sp", ropes[:], I["ropes"].rearrange("(t p) c -> p t c", p=128), w=["b_rs"])
            P.dma("sp", gq[:], I["qk_norm_g"][layer:layer + 1, :].partition_broadcast(128), w=["b_gq"])
            P.dve(lambda e: e.memset(V[:], 1.0), w=["b_V"])
            P.dve(lambda e: e.tensor_copy(out=g640[:, 0:8, :], in_=gq[:, 0:64].unsqueeze(1).to_broadcast([128, 8, 64])),
                  r=["b_gq"], w=["b_g"])
            P.dve(lambda e: e.tensor_copy(out=g640[:, 8:10, :], in_=gq[:, 64:128].unsqueeze(1).to_broadcast([128, 2, 64])),
                  r=["b_gq", "b_g"], w=["b_g"])
            def gq_a(t):
                b = t % 2
                ps = C.psb[b]
                for half, (c0, n) in enumerate(((0, 512), (512, 256))):
                    for j in range(8):
                        P.pe(lambda e, ps=ps, half=half, c0=c0, n=n, j=j, t=t: e.matmul(
                            ps[:, half * 512:half * 512 + n], lhsT=uT[:, j, t * 128:(t + 1) * 128],
                            rhs=wB[:, j, c0:c0 + n], start=(j == 0), stop=(j == 7)),
                            r=["b_uT", "b_w"], w=[("psb", b, half)])
            def gq_b(t):
                b = t % 2
                ps = C.psb[b]
                kq = [("psb", b, 0), ("psb", b, 1)]
                qk_ps = ps[:, 0:640].rearrange("p (h d) -> p h d", d=64)
                P.act(lambda e, qk_ps=qk_ps: e.activation(out=sq[:], in_=qk_ps, func=AF.Square), r=kq, w=["b_sq"])
                P.dve(lambda e: e.tensor_reduce(out=ss[:], in_=sq[:], axis=AX.X, op=ALU.add), r=["b_sq"], w=["b_ss"])
                P.act(lambda e: e.activation(out=ss[:], in_=ss[:], func=AF.Sqrt, scale=1.0 / 64, bias=C.epsb[:]),
                      r=["b_ss"], w=["b_ss"])
                P.dve(lambda e: e.reciprocal(out=ss[:], in_=ss[:]), r=["b_ss"], w=["b_ss"])
                P.dve(lambda e, qk_ps=qk_ps: e.tensor_tensor(out=qn[:], in0=qk_ps, in1=ss[:].unsqueeze(2).to_broadcast([128, 10, 64]),
                                                              op=ALU.mult), r=kq + ["b_ss"], w=["b_qn"])
                P.dve(lambda e: e.tensor_tensor(out=qn[:], in0=qn[:], in1=g640[:], op=ALU.mult), r=["b_qn", "b_g"], w=["b_qn"])
                cb = ropec[:, t, :].unsqueeze(1).to_broadcast([128, 10, 32])
                sbb = ropes[:, t, :].unsqueeze(1).to_broadcast([128, 10, 32])
                x1 = qn[:, :, 0:32]
                x2 = qn[:, :, 32:64]
                q_out = qr[b]
                P.dve(lambda e, cb=cb: e.tensor_tensor(out=t1[:], in0=x1, in1=cb, op=ALU.mult), r=["b_qn", "b_rc"], w=["b_t1"])
                P.dve(lambda e, sbb=sbb: e.tensor_tensor(out=t2[:], in0=x2, in1=sbb, op=ALU.mult), r=["b_qn", "b_rs"], w=["b_t2"])
                P.dve(lambda e, q_out=q_out: e.tensor_tensor(out=q_out[:, 0:10, 0:32], in0=t1[:], in1=t2[:], op=ALU.subtract),
                      r=["b_t1", "b_t2"], w=[("b_qr", b)])
                P.dve(lambda e, sbb=sbb: e.tensor_tensor(out=t1[:], in0=x1, in1=sbb, op=ALU.mult), r=["b_qn", "b_rs", ("b_qr", b)], w=["b_t1"])
                P.dve(lambda e, cb=cb: e.tensor_tensor(out=t2[:], in0=x2, in1=cb, op=ALU.mult), r=["b_qn", "b_rc", ("b_qr", b)], w=["b_t2"])
                P.dve(lambda e, q_out=q_out: e.tensor_tensor(out=q_out[:, 0:10, 32:64], in0=t1[:], in1=t2[:], op=ALU.add),
                      r=["b_t1", "b_t2"], w=[("b_qr", b)])
                P.dve(lambda e, q_out=q_out: e.tensor_copy(out=q_out[:, 10:12, :], in_=q_out[:, 9:10, :].to_broadcast([128, 2, 64])),
                      r=[("b_qr", b)], w=[("b_qr", b)])
                P.dve(lambda e, q_out=q_out: e.tensor_copy(out=q_out[:, 9:10, :], in_=q_out[:, 8:9, :]),
                      r=[("b_qr", b)], w=[("b_qr", b)])
                pt = C.pst[b]
                for j in range(6):
                    P.pe(lambda e, pt=pt, j=j, q_out=q_out: e.transpose(
                        pt[:, j * 128:(j + 1) * 128], q_out[:, 2 * j:2 * j + 2, :].rearrange("p a d -> p (a d)"), C.ident[:]),
                        r=[("b_qr", b)], w=[("pst", b)])
                P.act(lambda e, pt=pt, t=t: e.copy(out=QT[:, :, t * 128:(t + 1) * 128],
                                                   in_=pt[:, 0:512].rearrange("p (j c) -> p j c", j=4)),
                      r=[("pst", b)], w=["b_QT"])
                P.act(lambda e, pt=pt, t=t: e.copy(out=KT[:, :, t * 128:(t + 1) * 128],
                                                   in_=pt[:, 512:768].rearrange("p (j c) -> p j c", j=2)),
                      r=[("pst", b)], w=["b_KT"])
                P.act(lambda e, ps=ps, t=t: e.copy(out=V[:, t, :, 0:64], in_=ps[:, 640:768].rearrange("p (g d) -> p g d", g=2)),
                      r=kq, w=["b_V"] + kq)
            gq_a(0)
            for t in range(NT):
                if t + 1 < NT:
                    gq_a(t + 1)
                gq_b(t)
            P.barrier(C.bar[:])
        pts = [sb("b_pt%d" % i, [128, 1024], BF16) for i in range(3)]
        ring = Ring([p[:] for p in pts], "b_pt")
        obs = [sb("b_ob%d" % i, [128, 1024], F32) for i in range(2)]
        obufs = Ring([p[:] for p in obs], "b_ob")
        rec = sb("b_rec", [128, 1], F32)
        stg = sb("b_stg", [128, 4, 512], BF16)
        rrow = sb("b_rrow", [128, 512], F32)
        rt = sb("b_rt", [128, 512], F32)
        rhi = sb("b_rhi", [128, 512], BF16)
        rlo = sb("b_rlo", [128, 512], BF16)
        bcs = sb("b_bcs", [128, 512], F32)
        ystg = [sb("b_ystg%d" % i, [128, 512], BF16) for i in range(2)]
        poss = [sb("b_pos%d" % i, [128, 512], F32) for i in range(2)]
        kst = 0
        for h in range(8):
            g = h // 4
            base = 64 * (h % 2)
            QTh = QT[base:base + 64, h // 2, :]
            KTh = KT[base:base + 64, g, :]

            def out_cb(qc, poT, ps_key, h=h, base=base):
                nonlocal kst
                yb = kst % 2
                kst += 1
                r64 = slice(64, 65)
                pos = poss[yb]
                P.dve(lambda e, pos=pos: e.tensor_copy(out=pos[0:65, :], in_=poT), r=[ps_key], w=[("b_pos", yb)])
                P.dve(lambda e, pos=pos: e.reciprocal(out=rrow[r64, :], in_=pos[64:65, :]), r=[("b_pos", yb)], w=["b_rrow"])
                P.dve(lambda e: e.tensor_copy(out=rhi[r64, :], in_=rrow[r64, :]), r=["b_rrow"], w=["b_rhi"])
                P.dve(lambda e: e.tensor_copy(out=rt[r64, :], in_=rhi[r64, :]), r=["b_rhi"], w=["b_rt"])
                P.dve(lambda e: e.tensor_tensor(out=rt[r64, :], in0=rrow[r64, :], in1=rt[r64, :], op=ALU.subtract), r=["b_rrow", "b_rt"], w=["b_rt"])
                P.dve(lambda e: e.tensor_copy(out=rlo[r64, :], in_=rt[r64, :]), r=["b_rt"], w=["b_rlo"])
                bc = C.psb[2][0:64, 512:1024]

                def late():
                    late_part(qc, h, yb, pos, bc, r64)
                return late

            def late_part(qc, h, yb, pos, bc, r64):
                P.pe(lambda e: e.matmul(bc, lhsT=C.onesb[64:65, 0:64], rhs=rhi[r64, :], start=True, stop=False), r=["b_rhi", "onesb"], w=[("psb", 2, 1)])
                P.pe(lambda e: e.matmul(bc, lhsT=C.onesb[64:65, 0:64], rhs=rlo[r64, :], start=False, stop=True), r=["b_rlo", "onesb"], w=[("psb", 2, 1)])
                P.dve(lambda e: e.tensor_copy(out=bcs[0:64, :], in_=bc), r=[("psb", 2, 1)], w=["b_bcs"])
                P.dve(lambda e, yb=yb, pos=pos: e.tensor_tensor(out=ystg[yb][0:64, :], in0=pos[0:64, :], in1=bcs[0:64, :], op=ALU.mult),
                      r=[("b_pos", yb), "b_bcs"], w=[("b_ystg", yb)])
                P.dma("pool", C.ynT[1][h * 64:(h + 1) * 64, qc * 512:(qc + 1) * 512], ystg[yb][0:64, :], r=[("b_ystg", yb)])
            attention(C, QTh, KTh, "b_QT", lambda st, g=g: V[:, st, g, :], "b_V", 64, 0.125, ring, out_cb, obufs=obufs, feat=True)
        P.barrier(C.bar[:])


def lam_init_of(layer):
    return 0.8 - 0.6 * math.exp(-0.3 * layer)


def setup_bias_tables(C):
    P, nc = C.P, C.nc
    seq, thr = bias_steps()
    C.W2b = [C.ges.enter_context(nc.sbuf_tensor("W2b%d" % h, [128, 1152], BF16)) for h in range(4)]
    with contextlib.ExitStack() as es:
        sb = lambda n, s, d: es.enter_context(nc.sbuf_tensor(_uniq(n), list(s), d))
        relf = sb("s_relf", [128, 1152], F32)
        acc = sb("s_acc", [128, 1152], F32)
        tmp = sb("s_tmp", [128, 1152], F32)
        dB = sb("s_dB", [128, len(thr), 4], F32)
        P.dma("sp", relf[:], C.I["relf"], w=["s_relf"])
        rb3 = C.rbB[:].rearrange("p (b h) -> p b h", h=4)
        for k in range(len(thr)):
            P.dve(lambda e, k=k: e.tensor_tensor(out=dB[:, k, :], in0=rb3[:, seq[k + 1], :], in1=rb3[:, seq[k], :], op=ALU.subtract),
                  r=["rbB"], w=["s_dB"])
        for h in range(4):
            P.dve(lambda e, h=h: e.tensor_scalar(out=acc[:], in0=relf[:], scalar1=0.0, scalar2=C.rbB[:, seq[0] * 4 + h:seq[0] * 4 + h + 1],
                                                 op0=ALU.mult, op1=ALU.add), r=["s_relf", "rbB"], w=["s_acc"])
            for k in range(len(thr)):
                P.dve(lambda e, k=k, h=h: e.tensor_scalar(out=tmp[:], in0=relf[:], scalar1=float(thr[k]), scalar2=dB[:, k, h:h + 1],
                                                          op0=ALU.is_ge, op1=ALU.mult), r=["s_relf", "s_dB"], w=["s_tmp"])
                P.dve(lambda e: e.tensor_tensor(out=acc[:], in0=acc[:], in1=tmp[:], op=ALU.add), r=["s_acc", "s_tmp"], w=["s_acc"])
            P.act(lambda e, h=h: e.activation(out=C.W2b[h][:], in_=acc[:], func=AF.Copy, scale=8.0), r=["s_acc"], w=[("W2b", h)])
        P.barrier(C.bar[:])


def phase_diff(C, layer):
    P, nc, S, NT, NQC, I = C.P, C.nc, C.S, C.NT, C.NQC, C.I
    li = lam_init_of(layer)
    with contextlib.ExitStack() as es:
        sb = lambda n, s, d: es.enter_context(nc.sbuf_tensor(_uniq(n), list(s), d))
        QT = sb("d_QT", [128, 4, S], BF16)
        KT = sb("d_KT", [128, 4, S], BF16)
        V = sb("d_V", [128, NT, 4, 129], BF16)
        with contextlib.ExitStack() as es2:
            sb2 = lambda n, s, d: es2.enter_context(nc.sbuf_tensor(_uniq(n), list(s), d))
            uT = sb2("d_uT", [128, 8, S], BF16)
            wqk = sb2("d_wqk", [128, 8, 1024], BF16)
            load_T(C, uT, C.uT_d, 8, key="d_uT")
            load_w(C, wqk[:], I["w_in"][layer, :, COL["dq"]:COL["dq"] + 1024], "d_wqk")
            P.dve(lambda e: e.memset(V[:], 1.0), w=["d_V"])
            k = 0
            for blk in range(8):
                dst = QT if blk < 4 else KT
                for tc in range(NQC):
                    bi = k % 4
                    k += 1
                    ps = C.psb[bi // 2][:, (bi % 2) * 512:(bi % 2) * 512 + 512]
                    pk = ("psb", bi // 2, bi % 2)
                    for j in range(8):
                        P.pe(lambda e, ps=ps, j=j, blk=blk, tc=tc: e.matmul(
                            ps, lhsT=wqk[:, j, blk * 128:(blk + 1) * 128], rhs=uT[:, j, tc * 512:(tc + 1) * 512],
                            start=(j == 0), stop=(j == 7)), r=["d_uT", "d_wqk"], w=[pk])
                    eng = P.act if k % 2 == 0 else P.dve
                    if k % 2 == 0:
                        P.act(lambda e, ps=ps, dst=dst, blk=blk, tc=tc: e.copy(out=dst[:, blk % 4, tc * 512:(tc + 1) * 512], in_=ps),
                              r=[pk], w=["d_QK"])
                    else:
                        P.dve(lambda e, ps=ps, dst=dst, blk=blk, tc=tc: e.tensor_copy(out=dst[:, blk % 4, tc * 512:(tc + 1) * 512], in_=ps),
                              r=[pk], w=["d_QK"])
            wv = wqk[:, :, 0:512]
            load_w(C, wv, I["w_in"][layer, :, COL["dv"]:COL["dv"] + 512], "d_wqk")
            for t in range(NT):
                bi = t % 4
                ps = C.psb[bi // 2][:, (bi % 2) * 512:(bi % 2) * 512 + 512]
                pk = ("psb", bi // 2, bi % 2)
                for j in range(8):
                    P.pe(lambda e, ps=ps, j=j, t=t: e.matmul(ps, lhsT=uT[:, j, t * 128:(t + 1) * 128], rhs=wv[:, j, :],
                                                            start=(j == 0), stop=(j == 7)), r=["d_uT", "d_wqk"], w=[pk])
                P.act(lambda e, ps=ps, t=t: e.copy(out=V[:, t, :, 0:128], in_=ps.rearrange("p (h d) -> p h d", h=4)),
                      r=[pk], w=["d_V"])
            P.barrier(C.bar[:])
        lpB = sb("d_lp", [128, 256], F32)
        lpr = sb("d_lpr", [128, 128], F32)
        s12 = sb("d_s12", [128, 2], F32)
        nlam = sb("d_nlam", [128, 1], F32)
        gsub = sb("d_gsub", [128, 128], F32)
        P.dma("sp", lpB[:], I["diff_lambda"][layer:layer + 1, :].partition_broadcast(128), w=["d_lp"])
        P.dma("sp", gsub[:], I["diff_norm_g"][layer:layer + 1, :].partition_broadcast(128), w=["d_gsub"])
        lp4 = lpB[:].rearrange("p (a b d) -> p a b d", a=2, b=2)
        P.dve(lambda e: e.tensor_tensor(out=lpr[:].rearrange("p (a d) -> p a d", a=2), in0=lp4[:, :, 0, :], in1=lp4[:, :, 1, :], op=ALU.mult),
              r=["d_lp"], w=["d_lpr"])
        P.dve(lambda e: e.tensor_reduce(out=s12[:], in_=lpr[:].rearrange("p (a d) -> p a d", a=2), axis=AX.X, op=ALU.add),
              r=["d_lpr"], w=["d_s12"])
        P.act(lambda e: e.activation(out=s12[:], in_=s12[:], func=AF.Exp), r=["d_s12"], w=["d_s12"])
        P.dve(lambda e: e.tensor_tensor(out=nlam[:], in0=s12[:, 1:2], in1=s12[:, 0:1], op=ALU.subtract), r=["d_s12"], w=["d_nlam"])
        P.dve(lambda e: e.tensor_scalar_add(out=nlam[:], in0=nlam[:], scalar1=-li), r=["d_nlam"], w=["d_nlam"])
        P.dve(lambda e: e.tensor_scalar_mul(out=gsub[:], in0=gsub[:], scalar1=1.0 - li), r=["d_gsub"], w=["d_gsub"])
        pts = [sb("d_pt%d" % i, [128, 1024], BF16) for i in range(3)]
        ring = Ring([p[:] for p in pts], "d_pt")
        obs = [sb("d_ob%d" % i, [128, 1024], F32) for i in range(2)]
        obufs = Ring([p[:] for p in obs], "d_ob")
        o1n = sb("d_o1n", [128, NT, 128], F32)
        ytok = sb("d_ytok", [128, NT, 512], BF16)
        rec = sb("d_rec", [128, 1], F32)
        osb = sb("d_osb", [128, 128], F32)
        junk = sb("d_junk", [128, 128], F32)
        ssN = sb("d_ssN", [128, NT], F32)
        stg = sb("d_stg", [128, 4, 512], BF16)
        for h in range(4):
            def bias_fn(st, qc, h=h):
                dlt = st - 4 * qc
                if -1 <= dlt <= 4:
                    return ("band", C.W2b[h][:, 512 - 128 * dlt:1024 - 128 * dlt], ("W2b", h))
                if dlt > 4:
                    return ("const", C.rbB[:, 31 * 4 + h:31 * 4 + h + 1], "rbB", 31)
                return ("const", C.rbB[:, 15 * 4 + h:15 * 4 + h + 1], "rbB", 15)
            for m in range(2):
                base = 64 * m
                QTh = QT[base:base + 64, h, :]
                KTh = KT[base:base + 64, h, :]
                if m == 0:
                    def out_cb(qc, qt, ps_ap, ps_key, h=h):
                        t = qc * 4 + qt
                        P.dve(lambda e: e.reciprocal(out=rec[:], in_=ps_ap[:, 128:129]), r=[ps_key], w=["d_rec"])
                        P.dve(lambda e: e.tensor_scalar(out=o1n[:, t, :], in0=ps_ap[:, 0:128], scalar1=rec[:, 0:1], scalar2=None, op0=ALU.mult),
                              r=[ps_key, "d_rec"], w=["d_o1n"])
                else:
                    def out_cb(qc, qt, ps_ap, ps_key, h=h):
                        t = qc * 4 + qt
                        P.dve(lambda e: e.reciprocal(out=rec[:], in_=ps_ap[:, 128:129]), r=[ps_key], w=["d_rec"])
                        P.dve(lambda e: e.tensor_tensor(out=rec[:], in0=rec[:], in1=nlam[:], op=ALU.mult), r=["d_rec", "d_nlam"], w=["d_rec"])
                        P.dve(lambda e: e.scalar_tensor_tensor(out=o1n[:, t, :], in0=ps_ap[:, 0:128], scalar=rec[:, 0:1], in1=o1n[:, t, :],
                                                               op0=ALU.mult, op1=ALU.add), r=[ps_key, "d_rec", "d_o1n"], w=["d_o1n"])
                attention(C, QTh, KTh, "d_QK", lambda st, h=h: V[:, st, h, :], "d_V", 128, 0.125, ring, out_cb, bias_fn=bias_fn, obufs=obufs)
            for t in range(NT):
                P.act(lambda e, t=t: e.activation(out=junk[:], in_=o1n[:, t, :], func=AF.Square, accum_out=ssN[:, t:t + 1]),
                      r=["d_o1n"], w=["d_junk", "d_ssN"])
            P.act(lambda e: e.activation(out=ssN[:], in_=ssN[:], func=AF.Sqrt, scale=1.0 / 128, bias=C.epsb[:]), r=["d_ssN"], w=["d_ssN"])
            P.dve(lambda e: e.reciprocal(out=ssN[:], in_=ssN[:]), r=["d_ssN"], w=["d_ssN"])
            P.dve(lambda e: e.tensor_tensor(out=o1n[:], in0=o1n[:], in1=ssN[:].unsqueeze(2).to_broadcast([128, NT, 128]), op=ALU.mult),
                  r=["d_o1n", "d_ssN"], w=["d_o1n"])
            P.dve(lambda e, h=h: e.tensor_tensor(out=ytok[:, :, h * 128:(h + 1) * 128], in0=o1n[:],
                                                 in1=gsub[:].unsqueeze(1).to_broadcast([128, NT, 128]), op=ALU.mult),
                  r=["d_o1n", "d_gsub"], w=["d_ytok"])
        store_T(C, ytok, C.ynT[3], 4, "d_ytok", stg, "d_stg")
        P.barrier(C.bar[:])


def store_T(C, ytok, dstT, ncc, ykey, stg, skey):
    P, NT = C.P, C.NT
    for t in range(NT):
        b = t % 2
        pt = C.pst[b]
        for j in range(ncc):
            P.pe(lambda e, pt=pt, j=j, t=t: e.transpose(pt[:, j * 128:(j + 1) * 128], ytok[:, t, j * 128:(j + 1) * 128], C.ident[:]),
                 r=[ykey], w=[("pst", b)])
        tt = t % 4
        P.act(lambda e, pt=pt, tt=tt: e.copy(out=stg[:, 0:ncc, tt * 128:(tt + 1) * 128],
                                             in_=pt[:, 0:ncc * 128].rearrange("p (j c) -> p j c", j=ncc)),
              r=[("pst", b)], w=[skey])
        if tt == 3:
            t0 = (t // 4) * 512
            P.dma("pool", dstT[:, t0:t0 + 512].rearrange("(j p) t -> p j t", p=128), stg[:, 0:ncc, :], r=[skey])


def phase_four(C, layer):
    P, nc, S, NT, NQC, I = C.P, C.nc, C.S, C.NT, C.NQC, C.I
    KC = 256
    with contextlib.ExitStack() as es:
        sb = lambda n, s, d: es.enter_context(nc.sbuf_tensor(_uniq(n), list(s), d))
        GCS = sb("a_GCS", [128, NT, 1024], BF16)
        with contextlib.ExitStack() as es2:
            sb2 = lambda n, s, d: es2.enter_context(nc.sbuf_tensor(_uniq(n), list(s), d))
            uT = sb2("a_uT", [128, 8, S], BF16)
            wA = sb2("a_w", [128, 8, 512], BF16)
            bd = sb2("a_bd", [128, 2, 128], BF16)
            aT = sb2("a_aT", [128, 4, S], BF16)
            load_T(C, uT, C.uT_d, 8, key="a_uT")
            load_w(C, wA[:], I["w_in"][layer, :, 0:512], "a_w")
            P.dma("sp", bd[:], I["bdcs"].rearrange("a p c -> p a c"), w=["a_bd"])
            k = 0
            for blk in range(4):
                for tc in range(NQC):
                    bi = k % 4
                    k += 1
                    ps = C.psb[bi // 2][:, (bi % 2) * 512:(bi % 2) * 512 + 512]
                    pk = ("psb", bi // 2, bi % 2)
                    for j in range(8):
                        P.pe(lambda e, ps=ps, j=j, blk=blk, tc=tc: e.matmul(
                            ps, lhsT=wA[:, j, blk * 128:(blk + 1) * 128], rhs=uT[:, j, tc * 512:(tc + 1) * 512],
                            start=(j == 0), stop=(j == 7)), r=["a_uT", "a_w"], w=[pk])
                    if k % 2 == 0:
                        P.act(lambda e, ps=ps, blk=blk, tc=tc: e.copy(out=aT[:, blk, tc * 512:(tc + 1) * 512], in_=ps), r=[pk], w=["a_aT"])
                    else:
                        P.dve(lambda e, ps=ps, blk=blk, tc=tc: e.tensor_copy(out=aT[:, blk, tc * 512:(tc + 1) * 512], in_=ps), r=[pk], w=["a_aT"])
            for st in range(NT):
                b = st % 2
                ps = C.psb[b]
                for cs in range(2):
                    for cc in range(4):
                        P.pe(lambda e, ps=ps, cs=cs, cc=cc, st=st: e.matmul(
                            ps[:, cs * 512 + cc * 128:cs * 512 + (cc + 1) * 128], lhsT=aT[:, cc, st * 128:(st + 1) * 128],
                            rhs=bd[:, cs, :], start=True, stop=True, skip_group_check=True), r=["a_aT", "a_bd"], w=[("psb", b, cs)])
                if st % 2 == 0:
                    P.act(lambda e, ps=ps, st=st: e.copy(out=GCS[:, st, :], in_=ps[:]), r=[("psb", b, 0), ("psb", b, 1)], w=["a_GCS"])
                else:
                    P.dve(lambda e, ps=ps, st=st: e.tensor_copy(out=GCS[:, st, :], in_=ps[:]), r=[("psb", b, 0), ("psb", b, 1)], w=["a_GCS"])
            P.barrier(C.bar[:])
        DC = [sb("a_DC%d" % i, [128, NT, KC], BF16) for i in range(2)]
        DS = [sb("a_DS%d" % i, [128, NT, KC], BF16) for i in range(2)]
        stg = [sb("a_stg%d" % i, [128, 4, KC], BF16) for i in range(2)]
        for kc in range(S // KC):
            b = kc % 2
            P.dma("sp", DC[b][:], I["dftc"][:, kc * KC:(kc + 1) * KC].rearrange("(t p) k -> p t k", p=128), w=[("a_DC", b)])
            P.dma("sp", DS[b][:], I["dfts"][:, kc * KC:(kc + 1) * KC].rearrange("(t p) k -> p t k", p=128), w=[("a_DS", b)])
            for cc in range(4):
                bi = cc
                ps = C.psb[bi // 2][:, (bi % 2) * 512:(bi % 2) * 512 + KC]
                pk = ("psb", bi // 2, bi % 2)
                for st in range(NT):
                    P.pe(lambda e, ps=ps, st=st, cc=cc, b=b: e.matmul(ps, lhsT=GCS[:, st, cc * 128:(cc + 1) * 128], rhs=DC[b][:, st, :],
                                                                     start=(st == 0), stop=False), r=["a_GCS", ("a_DC", b)], w=[pk])
                for st in range(NT):
                    P.pe(lambda e, ps=ps, st=st, cc=cc, b=b: e.matmul(ps, lhsT=GCS[:, st, 512 + cc * 128:512 + (cc + 1) * 128], rhs=DS[b][:, st, :],
                                                                     start=False, stop=(st == NT - 1)), r=["a_GCS", ("a_DS", b)], w=[pk])
                if cc % 2 == 0:
                    P.act(lambda e, ps=ps, cc=cc, b=b: e.copy(out=stg[b][:, cc, :], in_=ps), r=[pk], w=[("a_stg", b)])
                else:
                    P.dve(lambda e, ps=ps, cc=cc, b=b: e.tensor_copy(out=stg[b][:, cc, :], in_=ps), r=[pk], w=[("a_stg", b)])
            P.dma("pool", C.ynT[0][:, kc * KC:(kc + 1) * KC].rearrange("(j p) t -> p j t", p=128), stg[b][:], r=[("a_stg", b)])
        P.barrier(C.bar[:])


def phase_mlstm(C, layer):
    P, nc, S, NT, NQC, I = C.P, C.nc, C.S, C.NT, C.NQC, C.I
    X4 = NT * 4
    with contextlib.ExitStack() as es:
        sb = lambda n, s, d: es.enter_context(nc.sbuf_tensor(_uniq(n), list(s), d))
        uT = sb("c_uT", [128, 8, S], BF16)
        G = sb("c_G", [128, NT, 16], F32)
        wg = sb("c_wg", [128, 8, 16], BF16)
        biasB = sb("c_bias", [128, 16], F32)
        gC = sb("c_gC", [128, 512], F32)
        eq = [sb("c_eq%d" % d, [128, NT, 4], F32) for d in range(2)]
        ek = [sb("c_ek%d" % d, [128, NT, 4], F32) for d in range(2)]
        ekd = [sb("c_ekd%d" % d, [128, NT, 4], F32) for d in range(2)]
        dec = [sb("c_dec%d" % d, [128, NT, 4], F32) for d in range(2)]
        load_T(C, uT, C.uT_d, 8, key="c_uT")
        load_w(C, wg[:], I["w_in"][layer, :, COL["cg"]:COL["cg"] + 16], "c_wg")
        P.dma("sp", biasB[:], I["mlstm_gate_bias"][layer:layer + 1, :].partition_broadcast(128), w=["c_bias"])
        P.dma("sp", gC[:], I["mlstm_norm_g"][layer:layer + 1, :].partition_broadcast(128), w=["c_gC"])
        psg = C.psb[0][:, 0:NT * 16]
        for t in range(NT):
            for j in range(8):
                P.pe(lambda e, t=t, j=j: e.matmul(psg[:, t * 16:(t + 1) * 16], lhsT=uT[:, j, t * 128:(t + 1) * 128], rhs=wg[:, j, :],
                                                 start=(t == 0 and j == 0), stop=(j == 7), skip_group_check=True),
                     r=["c_uT", "c_wg"], w=[("psb", 0, 0)])
        P.dve(lambda e: e.tensor_tensor(out=G[:], in0=psg.rearrange("p (t k) -> p t k", k=16),
                                        in1=biasB[:].unsqueeze(1).to_broadcast([128, NT, 16]), op=ALU.add),
              r=[("psb", 0, 0), "c_bias"], w=["c_G"])
        with contextlib.ExitStack() as es2:
            sb2 = lambda n, s, d: es2.enter_context(nc.sbuf_tensor(_uniq(n), list(s), d))
            af = sb2("c_af", [128, NT, 4], F32)
            lf = sb2("c_lf", [128, NT, 4], F32)
            mn = sb2("c_mn", [128, NT, 4], F32)
            tm = sb2("c_tm", [128, NT, 4], F32)
            lfh = [sb2("c_lfh%d" % i, [128, NT, 4], BF16) for i in range(3)]
            for d in range(2):
                Fd = G[:, :, 4 + 8 * d:8 + 8 * d]
                Id = G[:, :, 8 * d:8 * d + 4]
                P.act(lambda e, Fd=Fd: e.activation(out=af[:], in_=Fd, func=AF.Abs), r=["c_G"], w=["c_af"])
                P.act(lambda e: e.activation(out=af[:], in_=af[:], func=AF.Exp, scale=-1.0), r=["c_af"], w=["c_af"])
                P.act(lambda e: e.activation(out=af[:], in_=af[:], func=AF.Ln, bias=C.onesf[:, 0:1]), r=["c_af"], w=["c_af"])
                P.dve(lambda e, Fd=Fd: e.tensor_single_scalar(out=mn[:], in_=Fd, scalar=0.0, op=ALU.min), r=["c_G"], w=["c_mn"])
                P.dve(lambda e: e.tensor_tensor(out=lf[:], in0=mn[:], in1=af[:], op=ALU.subtract), r=["c_mn", "c_af"], w=["c_lf"])
                tri = C.maskLb if d == 0 else C.maskUb
                pb = C.psb[1][:, 0:X4]
                pt_ = C.psb[1][:, 512:512 + X4]
                for sp in range(3):
                    P.dve(lambda e, sp=sp: e.tensor_copy(out=lfh[sp][:], in_=lf[:]), r=["c_lf"], w=[("c_lfh", sp)])
                    if sp < 2:
                        P.dve(lambda e, sp=sp: e.tensor_copy(out=mn[:], in_=lfh[sp][:]), r=[("c_lfh", sp)], w=["c_mn"])
                        P.dve(lambda e: e.tensor_tensor(out=lf[:], in0=lf[:], in1=mn[:], op=ALU.subtract), r=["c_lf", "c_mn"], w=["c_lf"])
                for sp in range(3):
                    l2 = lfh[sp][:].rearrange("p t h -> p (t h)")
                    P.pe(lambda e, tri=tri, pb=pb, l2=l2, sp=sp: e.matmul(pb, lhsT=tri[:], rhs=l2, start=(sp == 0), stop=(sp == 2)),
                         r=[("c_lfh", sp)], w=[("psb", 1, 0)])
                for sp in range(3):
                    l2 = lfh[sp][:].rearrange("p t h -> p (t h)")
                    P.pe(lambda e, pt_=pt_, l2=l2, sp=sp: e.matmul(pt_, lhsT=C.onesb[:], rhs=l2, start=(sp == 0), stop=(sp == 2)),
                         r=[("c_lfh", sp)], w=[("psb", 1, 1)])
                pb3 = pb.rearrange("p (t h) -> p t h", h=4)
                pt3 = pt_.rearrange("p (t h) -> p t h", h=4)
                P.act(lambda e, d=d, pb3=pb3: e.activation(out=eq[d][:], in_=pb3, func=AF.Exp), r=[("psb", 1, 0)], w=[("c_eq", d)])
                P.dve(lambda e, Id=Id, pb3=pb3: e.tensor_tensor(out=tm[:], in0=Id, in1=pb3, op=ALU.subtract), r=["c_G", ("psb", 1, 0)], w=["c_tm", ("psb", 1, 0)])
                P.act(lambda e, d=d: e.activation(out=ek[d][:], in_=tm[:], func=AF.Exp, bias=C.lnk[:]), r=["c_tm"], w=[("c_ek", d)])
                P.act(lambda e, d=d, pt3=pt3: e.activation(out=dec[d][:], in_=pt3, func=AF.Exp), r=[("psb", 1, 1)], w=[("c_dec", d)])
                P.dve(lambda e, d=d: e.tensor_tensor(out=ekd[d][:], in0=ek[d][:], in1=dec[d][:], op=ALU.mult),
                      r=[("c_ek", d), ("c_dec", d)], w=[("c_ekd", d)])
            P.barrier(C.bar[:])
        import os
        MSTOP = int(os.environ.get("MSTOP", "9"))
        if MSTOP == 1:
            return
        wh = sb("c_wh", [128, 8, 512], BF16)
        T4 = sb("c_T4", [128, 4, S], BF16)
        KD = sb("c_KD", [128, NT, 2, 128], BF16)
        V = sb("c_V", [128, NT, 129], BF16)
        SO = sb("c_SO", [128, NT, 128], BF16)
        HS = sb("c_HS", [128, NT, 128], F32)
        sc4 = [sb("c_sc4%d" % i, [128, 4, 128], BF16) for i in range(2)]
        Cst = [sb("c_Cst%d" % d, [128, 129], F32) for d in range(2)]
        Cbf = [sb("c_Cbf%d" % d, [128, 129], BF16) for d in range(2)]
        STm = [sb("c_STm%d" % d, [128, 128], BF16) for d in range(2)]
        den = [sb("c_den%d" % d, [128, 1], F32) for d in range(2)]
        jk = sb("c_jk", [128, 128], F32)
        ssb = sb("c_ss", [128, NT], F32)
        ytk = sb("c_ytk", [128, NT, 128], BF16)
        P.dve(lambda e: e.memset(V[:], 1.0), w=["c_V"])
        for h in range(4):
            for i, nm in enumerate(("cq", "ck", "cv", "co")):
                P.dma("pool", wh[:, :, i * 128:(i + 1) * 128],
                      I["w_in"][layer, :, COL[nm] + h * 128:COL[nm] + (h + 1) * 128].rearrange("(j p) n -> p j n", p=128), w=[("c_wh", i)])
            def ml_a(t, h=h):
                b = t % 2
                ps = C.psb[b][:, 0:512]
                pk = ("psb", b, 0)
                for j in range(8):
                    P.pe(lambda e, ps=ps, j=j, t=t: e.matmul(ps, lhsT=uT[:, j, t * 128:(t + 1) * 128], rhs=wh[:, j, :],
                                                            start=(j == 0), stop=(j == 7)), r=["c_uT"] + [("c_wh", i4) for i4 in range(4)], w=[pk])
            def ml_b(t, h=h):
                b = t % 2
                ps = C.psb[b][:, 0:512]
                pk = ("psb", b, 0)
                s4 = sc4[b]
                MSKIP = os.environ.get("MSKIP", "")
                for wi, (src, sc) in enumerate(((0, eq[0]), (0, eq[1]), (1, ek[0]), (1, ek[1]))):
                    if "a" in MSKIP:
                        break
                    P.dve(lambda e, ps=ps, s4=s4, wi=wi, src=src, sc=sc, t=t, h=h: e.tensor_scalar(
                        out=s4[:, wi, :], in0=ps[:, src * 128:(src + 1) * 128], scalar1=sc[:, t, h:h + 1], scalar2=None, op0=ALU.mult),
                        r=[pk, ("c_eq", 0), ("c_eq", 1), ("c_ek", 0), ("c_ek", 1)], w=[("c_sc4", b)])
                for d in range(2):
                    if "b" in MSKIP:
                        break
                    P.dve(lambda e, ps=ps, d=d, t=t, h=h: e.tensor_scalar(
                        out=KD[:, t, d, :], in0=ps[:, 128:256], scalar1=ekd[d][:, t, h:h + 1], scalar2=None, op0=ALU.mult),
                        r=[pk, ("c_ekd", d)], w=["c_KD"])
                if "c" not in MSKIP:
                    P.act(lambda e, ps=ps, t=t: e.copy(out=V[:, t, 0:128], in_=ps[:, 256:384]), r=[pk], w=["c_V", pk])
                if "d" not in MSKIP:
                    P.act(lambda e, ps=ps, t=t: e.activation(out=SO[:, t, :], in_=ps[:, 384:512], func=AF.Sigmoid), r=[pk], w=["c_SO", pk])
                pt = C.pst[b]
                if "e" in MSKIP:
                    return
                for wi in range(4):
                    P.pe(lambda e, pt=pt, wi=wi, s4=s4: e.transpose(pt[:, wi * 128:(wi + 1) * 128], s4[:, wi, :], C.ident[:]),
                         r=[("c_sc4", b)], w=[("pst", b)])
                P.act(lambda e, pt=pt, t=t: e.copy(out=T4[:, :, t * 128:(t + 1) * 128], in_=pt[:, 0:512].rearrange("p (j c) -> p j c", j=4)),
                      r=[("pst", b)], w=["c_T4"])
            ml_a(0)
            for t in range(NT):
                if t + 1 < NT:
                    ml_a(t + 1)
                ml_b(t)
            if MSTOP == 2:
                P.barrier(C.bar[:])
                return
            for d in range(2):
                P.dve(lambda e, d=d: e.memset(Cst[d][:], 0.0), w=[("c_Cst", d)])
                P.dve(lambda e, d=d: e.memset(Cbf[d][:], 0.0), w=[("c_Cbf", d)])
            written = set()
            def chain_vars(step, d):
                c = step if d == 0 else NT - 1 - step
                return dict(c=c, mask=(C.maskL if d == 0 else C.maskU),
                            qsT=T4[:, d, c * 128:(c + 1) * 128], ksT=T4[:, 2 + d, c * 128:(c + 1) * 128],
                            psA=C.psb[0][:, d * 512:d * 512 + 128], psB=C.psb[1][:, d * 512:d * 512 + 129],
                            psC=C.psb[2][:, d * 512:d * 512 + 129], kA=("psb", 0, d), kB=("psb", 1, d), kC=("psb", 2, d))

            def chain_pe1(step, h=h):
                for d in range(2):
                    v = chain_vars(step, d)
                    P.pe(lambda e, v=v: e.matmul(v["psA"], lhsT=v["ksT"], rhs=v["qsT"], start=True, stop=True), r=["c_T4"], w=[v["kA"]])
                    P.pe(lambda e, v=v, d=d: e.matmul(v["psC"], lhsT=KD[:, v["c"], d, :], rhs=V[:, v["c"], :], start=True, stop=True),
                         r=["c_KD", "c_V"], w=[v["kC"]])

            def chain_rest(step, h=h):
                for d in range(2):
                    v = chain_vars(step, d)
                    P.dve(lambda e, v=v, d=d: e.tensor_tensor(out=STm[d][:], in0=v["psA"], in1=v["mask"][:], op=ALU.mult),
                          r=[v["kA"]], w=[("c_STm", d)])
                for d in range(2):
                    v = chain_vars(step, d)
                    P.pe(lambda e, v=v, d=d: e.matmul(v["psB"], lhsT=v["qsT"], rhs=Cbf[d][:], start=True, stop=False),
                         r=["c_T4", ("c_Cbf", d)], w=[v["kB"]])
                    P.pe(lambda e, v=v, d=d: e.matmul(v["psB"], lhsT=STm[d][:], rhs=V[:, v["c"], :], start=False, stop=True),
                         r=[("c_STm", d), "c_V"], w=[v["kB"]])
                for d in range(2):
                    v = chain_vars(step, d)
                    c = v["c"]
                    psB, psC, kB, kC = v["psB"], v["psC"], v["kB"], v["kC"]
                    P.dve(lambda e, psC=psC, d=d, c=c, h=h: e.scalar_tensor_tensor(out=Cst[d][:], in0=Cst[d][:], scalar=dec[d][:, c, h:h + 1],
                                                                               in1=psC, op0=ALU.mult, op1=ALU.add),
                          r=[kC, ("c_Cst", d), ("c_dec", d)], w=[("c_Cst", d)])
                    P.act(lambda e, d=d: e.copy(out=Cbf[d][:], in_=Cst[d][:]), r=[("c_Cst", d), kB], w=[("c_Cbf", d)])
                    P.dve(lambda e, psB=psB, d=d: e.tensor_scalar_max(out=den[d][:], in0=psB[:, 128:129], scalar1=1.0), r=[kB], w=[("c_den", d)])
                    P.dve(lambda e, psB=psB, d=d: e.scalar_tensor_tensor(out=den[d][:], in0=psB[:, 128:129], scalar=-1.0, in1=den[d][:],
                                                                         op0=ALU.mult, op1=ALU.max), r=[kB, ("c_den", d)], w=[("c_den", d)])
                    P.dve(lambda e, d=d: e.reciprocal(out=den[d][:], in_=den[d][:]), r=[("c_den", d)], w=[("c_den", d)])
                    if c in written:
                        P.dve(lambda e, psB=psB, d=d, c=c: e.scalar_tensor_tensor(out=HS[:, c, :], in0=psB[:, 0:128], scalar=den[d][:, 0:1],
                                                                                 in1=HS[:, c, :], op0=ALU.mult, op1=ALU.add),
                              r=[kB, ("c_den", d), "c_HS"], w=["c_HS"])
                    else:
                        written.add(c)
                        P.dve(lambda e, psB=psB, d=d, c=c: e.tensor_scalar(out=HS[:, c, :], in0=psB[:, 0:128], scalar1=den[d][:, 0:1],
                                                                          scalar2=None, op0=ALU.mult), r=[kB, ("c_den", d)], w=["c_HS"])

            chain_pe1(0)
            for step in range(NT):
                chain_rest(step)
                if step + 1 < NT:
                    chain_pe1(step + 1)
            if MSTOP == 3:
                P.barrier(C.bar[:])
                return
            for t in range(NT):
                P.act(lambda e, t=t: e.activation(out=jk[:], in_=HS[:, t, :], func=AF.Square, accum_out=ssb[:, t:t + 1]),
                      r=["c_HS"], w=["c_jk", "c_ss"])
            P.act(lambda e: e.activation(out=ssb[:], in_=ssb[:], func=AF.Sqrt, scale=1.0 / 128, bias=C.epsb[:]), r=["c_ss"], w=["c_ss"])
            P.dve(lambda e: e.reciprocal(out=ssb[:], in_=ssb[:]), r=["c_ss"], w=["c_ss"])
            P.dve(lambda e: e.tensor_tensor(out=HS[:], in0=HS[:], in1=ssb[:].unsqueeze(2).to_broadcast([128, NT, 128]), op=ALU.mult),
                  r=["c_HS", "c_ss"], w=["c_HS"])
            P.dve(lambda e, h=h: e.tensor_tensor(out=HS[:], in0=HS[:], in1=gC[:, h * 128:(h + 1) * 128].unsqueeze(1).to_broadcast([128, NT, 128]),
                                                 op=ALU.mult), r=["c_HS", "c_gC"], w=["c_HS"])
            P.dve(lambda e: e.tensor_tensor(out=ytk[:], in0=HS[:], in1=SO[:], op=ALU.mult), r=["c_HS", "c_SO"], w=["c_ytk"])
            stgT = SO[:].rearrange("p t d -> p (t d)")
            for t in range(NT):
                b = (t // 8) % 2
                pt = C.pst[b]
                P.pe(lambda e, pt=pt, t=t: e.transpose(pt[:, (t % 8) * 128:(t % 8 + 1) * 128], ytk[:, t, :], C.ident[:]),
                     r=["c_ytk"], w=[("pst", b)])
                if t % 8 == 7 or t == NT - 1:
                    n8 = t % 8 + 1
                    t0 = (t // 8) * 8
                    P.act(lambda e, pt=pt, n8=n8, t0=t0: e.copy(out=stgT[:, t0 * 128:(t0 + n8) * 128], in_=pt[:, 0:n8 * 128]),
                          r=[("pst", b)], w=["c_SO"])
            P.dma("pool", C.ynT[2][h * 128:(h + 1) * 128, :], stgT, r=["c_SO"])
        P.barrier(C.bar[:])


def phase_merge(C, layer, xsrc):
    P, nc, S, NT, NQC, I = C.P, C.nc, C.S, C.NT, C.NQC, C.I
    with contextlib.ExitStack() as es:
        sb = lambda n, s, d: es.enter_context(nc.sbuf_tensor(_uniq(n), list(s), d))
        Wg = sb("m_Wg", [128, 8, 4096], BF16)
        Wbr = sb("m_Wbr", [128, 16, 1024], BF16)
        Wo = sb("m_Wo", [128, 8, 1024], BF16)
        uTc = [sb("m_uT%d" % i, [128, 8, 512], BF16) for i in range(1)] * 2
        yc = [sb("m_y%d" % i, [128, 16, 512], BF16) for i in range(1)] * 2
        mT = sb("m_mT", [128, 8, 512], BF16)
        sig = [sb("m_sig%d" % i, [128, 512], F32) for i in range(2)]
        acc = sb("m_acc", [128, 512], F32)
        tmp = sb("m_tmp", [128, 512], F32)
        xt = [sb("m_xt%d" % i, [128, 1024], F32) for i in range(2)]
        for n in range(4):
            P.dma("pool", Wg[:, :, n * 1024:(n + 1) * 1024],
                  I["w_in"][layer, :, COL["g"] + n * 1024:COL["g"] + (n + 1) * 1024].rearrange("(j p) n -> p j n", p=128), w=[("m_Wg", n)])
            P.dma("pool", Wbr[:, n * 4:(n + 1) * 4, :], I["w_branch"][layer, n].rearrange("(j p) n -> p j n", p=128), w=[("m_Wbr", n)])
        P.dma("pool", Wo[:], I["w_out"][layer].rearrange("(j p) n -> p j n", p=128), w=["m_Wo"])
        k = 0
        for tc in range(NQC):
            b = 0
            P.dma("sp", uTc[b][:], C.uT_d[:, tc * 512:(tc + 1) * 512].rearrange("(j p) t -> p j t", p=128), w=[("m_uT", b)])
            for n in range(4):
                P.dma("sp", yc[b][:, n * 4:(n + 1) * 4, :], C.ynT[n][:, tc * 512:(tc + 1) * 512].rearrange("(j p) t -> p j t", p=128),
                      w=[("m_y", b, n)])
            for j in range(8):
                for n in range(4):
                    pi = k % 2
                    k += 1
                    psG = C.psb[pi][:, 0:512]
                    psR = C.psb[pi][:, 512:1024]
                    kG, kR = ("psb", pi, 0), ("psb", pi, 1)
                    for dj in range(8):
                        P.pe(lambda e, psG=psG, dj=dj, n=n, j=j, b=b: e.matmul(
                            psG, lhsT=Wg[:, dj, n * 1024 + j * 128:n * 1024 + (j + 1) * 128], rhs=uTc[b][:, dj, :],
                            start=(dj == 0), stop=(dj == 7)), r=[("m_Wg", n), ("m_uT", b)], w=[kG])
                    for cc in range(4):
                        P.pe(lambda e, psR=psR, cc=cc, n=n, j=j, b=b: e.matmul(
                            psR, lhsT=Wbr[:, n * 4 + cc, j * 128:(j + 1) * 128], rhs=yc[b][:, n * 4 + cc, :],
                            start=(cc == 0), stop=(cc == 3)), r=[("m_Wbr", n), ("m_y", b, n)], w=[kR])
                    sg = sig[pi]
                    P.act(lambda e, sg=sg, psG=psG: e.activation(out=sg[:], in_=psG, func=AF.Sigmoid), r=[kG], w=[("m_sig", pi)])
                    if n == 0:
                        P.dve(lambda e, sg=sg, psR=psR: e.tensor_tensor(out=acc[:], in0=sg[:], in1=psR, op=ALU.mult),
                              r=[("m_sig", pi), kR], w=["m_acc"])
                    elif n < 3:
                        P.dve(lambda e, sg=sg, psR=psR: e.tensor_tensor(out=tmp[:], in0=sg[:], in1=psR, op=ALU.mult),
                              r=[("m_sig", pi), kR], w=["m_tmp"])
                        P.dve(lambda e: e.tensor_tensor(out=acc[:], in0=acc[:], in1=tmp[:], op=ALU.add), r=["m_acc", "m_tmp"], w=["m_acc"])
                    else:
                        P.dve(lambda e, sg=sg, psR=psR: e.tensor_tensor(out=tmp[:], in0=sg[:], in1=psR, op=ALU.mult),
                              r=[("m_sig", pi), kR], w=["m_tmp"])
                        P.dve(lambda e, j=j: e.tensor_tensor(out=mT[:, j, :], in0=acc[:], in1=tmp[:], op=ALU.add),
                              r=["m_acc", "m_tmp"], w=["m_mT"])
            for tt in range(4):
                t = tc * 4 + tt
                xb = t % 2
                P.dma("sp", xt[xb][:], xsrc[t * 128:(t + 1) * 128, :], w=[("m_xt", xb)])
                ps = C.psb[2]
                for half in range(2):
                    for dj in range(8):
                        P.pe(lambda e, ps=ps, half=half, dj=dj, tt=tt: e.matmul(
                            ps[:, half * 512:(half + 1) * 512], lhsT=mT[:, dj, tt * 128:(tt + 1) * 128], rhs=Wo[:, dj, half * 512:(half + 1) * 512],
                            start=(dj == 0), stop=(dj == 7)), r=["m_mT", "m_Wo"], w=[("psb", 2, half)])
                P.dve(lambda e, ps=ps, xb=xb: e.tensor_tensor(out=xt[xb][:], in0=xt[xb][:], in1=ps[:], op=ALU.add),
                      r=[("m_xt", xb), ("psb", 2, 0), ("psb", 2, 1)], w=[("m_xt", xb)])
                P.dma("pool", C.xres[t * 128:(t + 1) * 128, :], xt[xb][:], r=[("m_xt", xb)])
        P.barrier(C.bar[:])


def phase_ffn(C, layer):
    P, nc, S, NT, NQC, I = C.P, C.nc, C.S, C.NT, C.NQC, C.I
    TB = min(1024, S)
    NCC = DFF // 128
    NH = NCC // 2
    NB = S // TB
    with contextlib.ExitStack() as es:
        sb = lambda n, s, d: es.enter_context(nc.sbuf_tensor(_uniq(n), list(s), d))
        WuA = sb("f_WuA", [128, 8, NH * 128], BF16)
        WuL = sb("f_WuL", [128, 8, NH * 128], BF16)
        Wd = sb("f_Wd", [128, NH, 1024], BF16)
        cw = sb("f_cw", [128, NCC, 3], F32)
        cb = sb("f_cb", [128, NCC], F32)
        hT = sb("f_hT", [128, NH, TB], BF16)
        vTc = [sb("f_vT%d" % i, [128, 8, TB + 2], BF16) for i in range(2)]
        aS = [sb("f_aS%d" % i, [128, TB + 2], F32) for i in range(2)]
        t1 = [sb("f_t1%d" % i, [128, TB], F32) for i in range(2)]
        z2 = [sb("f_z2%d" % i, [128, TB], F32) for i in range(2)]
        sg = [sb("f_sg%d" % i, [128, TB], F32) for i in range(2)]
        xt = [sb("f_xt%d" % i, [128, 1024], F32) for i in range(2)]
        for j in range(3):
            P.dma("sp", cw[:, :, j:j + 1], I["conv_w"][layer, j:j + 1, :].rearrange("o (c p) -> p c o", p=128), w=["f_cw"],
                  allow_slow_non_contiguous=True)
        P.dma("sp", cb[:].unsqueeze(2), I["conv_b"][layer:layer + 1, :].rearrange("o (c p) -> p c o", p=128), w=["f_cb"],
              allow_slow_non_contiguous=True)
        k = 0
        kv = 0
        for hp in range(2):
            c_lo = hp * NH
            for q4 in range(0, NH * 128, 512):
                n = min(512, NH * 128 - q4)
                P.dma("pool", WuA[:, :, q4:q4 + n],
                      I["w_up"][layer, :, c_lo * 128 + q4:c_lo * 128 + q4 + n].rearrange("(j p) n -> p j n", p=128), w=[("f_WuA", q4 // 512)])
                P.dma("pool", WuL[:, :, q4:q4 + n],
                      I["w_up"][layer, :, DFF + c_lo * 128 + q4:DFF + c_lo * 128 + q4 + n].rearrange("(j p) n -> p j n", p=128), w=[("f_WuL", q4 // 512)])
            P.dma("pool", Wd[:], I["w_down"][layer, c_lo * 128:(c_lo + NH) * 128, :].rearrange("(j p) n -> p j n", p=128), w=["f_Wd"])
            for blk in range(NB):
                t0 = blk * TB
                vb = kv % 2
                kv += 1
                vt = vTc[vb]
                lo = max(t0 - 1, 0)
                hi = min(t0 + TB + 1, S)
                c0 = lo - (t0 - 1)
                if t0 == 0:
                    P.dve(lambda e, vt=vt: e.memset(vt[:, :, 0:1], 0.0), w=[("f_vT", vb)])
                if t0 + TB == S:
                    P.dve(lambda e, vt=vt: e.memset(vt[:, :, TB + 1:TB + 2], 0.0), w=[("f_vT", vb)])
                P.dma("sp", vt[:, :, c0:c0 + (hi - lo)], C.uT_d[:, lo:hi].rearrange("(j p) t -> p j t", p=128), w=[("f_vT", vb)])
                pieces = [(0, 512), (512, 512), (1024, 2)] if TB == 1024 else [(0, 512), (512, 2)]
                def stage1(ci, b):
                    cc = c_lo + ci
                    a = aS[b]
                    for pi, (p0, n) in enumerate(pieces):
                        bi = pi % 3
                        ps = C.psb[bi // 2][:, (bi % 2) * 512:(bi % 2) * 512 + n]
                        pk = ("psb", bi // 2, bi % 2)
                        for j in range(8):
                            P.pe(lambda e, ps=ps, j=j, p0=p0, n=n, ci=ci, vt=vt: e.matmul(
                                ps, lhsT=WuA[:, j, ci * 128:(ci + 1) * 128], rhs=vt[:, j, p0:p0 + n],
                                start=(j == 0), stop=(j == 7)), r=[("f_WuA", ci // 4), ("f_vT", vb)], w=[pk])
                        P.act(lambda e, ps=ps, a=a, p0=p0, n=n: e.copy(out=a[:, p0:p0 + n], in_=ps), r=[pk], w=[("f_aS", b)])
                    T1, Z2 = t1[b], z2[b]
                    P.dve(lambda e, a=a, cc=cc, T1=T1: e.tensor_scalar(out=T1[:], in0=a[:, 1:TB + 1], scalar1=cw[:, cc, 1:2], scalar2=cb[:, cc:cc + 1],
                                                                       op0=ALU.mult, op1=ALU.add), r=[("f_aS", b), "f_cw", "f_cb"], w=[("f_t1", b)])
                    P.dve(lambda e, a=a, cc=cc, T1=T1: e.scalar_tensor_tensor(out=T1[:], in0=a[:, 0:TB], scalar=cw[:, cc, 0:1], in1=T1[:],
                                                                              op0=ALU.mult, op1=ALU.add), r=[("f_aS", b), "f_cw", ("f_t1", b)], w=[("f_t1", b)])
                    P.dve(lambda e, a=a, cc=cc, T1=T1: e.scalar_tensor_tensor(out=T1[:], in0=a[:, 2:TB + 2], scalar=cw[:, cc, 2:3], in1=T1[:],
                                                                              op0=ALU.mult, op1=ALU.add), r=[("f_aS", b), "f_cw", ("f_t1", b)], w=[("f_t1", b)])
                    P.pool(lambda e, T1=T1, Z2=Z2: e.tensor_tensor(out=Z2[:], in0=T1[:], in1=T1[:], op=ALU.mult), r=[("f_t1", b)], w=[("f_z2", b)])
                    P.pool(lambda e, Z2=Z2: e.tensor_scalar(out=Z2[:], in0=Z2[:], scalar1=0.044715, scalar2=1.0, op0=ALU.mult, op1=ALU.add),
                           r=[("f_z2", b)], w=[("f_z2", b)])
                    P.pool(lambda e, T1=T1, Z2=Z2: e.tensor_tensor(out=Z2[:], in0=Z2[:], in1=T1[:], op=ALU.mult), r=[("f_z2", b), ("f_t1", b)], w=[("f_z2", b)])

                def stage2(ci, b):
                    T1, Z2, SG = t1[b], z2[b], sg[b]
                    P.act(lambda e, Z2=Z2, SG=SG: e.activation(out=SG[:], in_=Z2[:], func=AF.Sigmoid, scale=1.5957691216057308),
                          r=[("f_z2", b)], w=[("f_sg", b)])
                    P.dve(lambda e, T1=T1, SG=SG: e.tensor_tensor(out=SG[:], in0=SG[:], in1=T1[:], op=ALU.mult), r=[("f_sg", b), ("f_t1", b)], w=[("f_sg", b)])
                    for li in range(TB // 512):
                        bi = 3 + (li % 2)
                        ps = C.psb[bi // 2][:, (bi % 2) * 512:(bi % 2) * 512 + 512]
                        pk = ("psb", bi // 2, bi % 2)
                        for j in range(8):
                            P.pe(lambda e, ps=ps, j=j, li=li, ci=ci, vt=vt: e.matmul(
                                ps, lhsT=WuL[:, j, ci * 128:(ci + 1) * 128], rhs=vt[:, j, 1 + li * 512:1 + (li + 1) * 512],
                                start=(j == 0), stop=(j == 7)), r=[("f_WuL", ci // 4), ("f_vT", vb)], w=[pk])
                        P.dve(lambda e, ps=ps, li=li, ci=ci, SG=SG: e.tensor_tensor(out=hT[:, ci, li * 512:(li + 1) * 512], in0=SG[:, li * 512:(li + 1) * 512],
                                                                                    in1=ps, op=ALU.mult), r=[("f_sg", b), pk], w=["f_hT"])

                bs = []
                for ci in range(NH):
                    bs.append(k % 2)
                    k += 1
                stage1(0, bs[0])
                for ci in range(NH):
                    if ci + 1 < NH:
                        stage1(ci + 1, bs[ci + 1])
                    stage2(ci, bs[ci])
                for tt in range(TB // 128):
                    t = t0 // 128 + tt
                    xb = t % 2
                    P.dma("sp", xt[xb][:], C.xres[t * 128:(t + 1) * 128, :], w=[("f_xt", xb)])
                    ps = C.psb[2]
                    for half in range(2):
                        for ci in range(NH):
                            P.pe(lambda e, ps=ps, half=half, ci=ci, tt=tt: e.matmul(
                                ps[:, half * 512:(half + 1) * 512], lhsT=hT[:, ci, tt * 128:(tt + 1) * 128], rhs=Wd[:, ci, half * 512:(half + 1) * 512],
                                start=(ci == 0), stop=(ci == NH - 1)), r=["f_hT", "f_Wd"], w=[("psb", 2, half)])
                    P.dve(lambda e, ps=ps, xb=xb: e.tensor_tensor(out=xt[xb][:], in0=xt[xb][:], in1=ps[:], op=ALU.add),
                          r=[("f_xt", xb), ("psb", 2, 0), ("psb", 2, 1)], w=[("f_xt", xb)])
                    P.dma("pool", C.xres[t * 128:(t + 1) * 128, :], xt[xb][:], r=[("f_xt", xb)])
        P.barrier(C.bar[:])


def phase_out(C, dst):
    P, nc, S, NT, I = C.P, C.nc, C.S, C.NT, C.I
    last = []
    with contextlib.ExitStack() as es:
        sb = lambda n, s, d: es.enter_context(nc.sbuf_tensor(_uniq(n), list(s), d))
        gB = sb("o_gB", [128, D], F32)
        xt = [sb("o_xt%d" % i, [128, D], F32) for i in range(2)]
        yo = [sb("o_y%d" % i, [128, D], F32) for i in range(2)]
        junk = sb("o_junk", [128, D], F32)
        ssq = [sb("o_ssq%d" % i, [128, 1], F32) for i in range(2)]
        P.dma("sp", gB[:], I["final_norm_g"].partition_broadcast(128), w=["o_gB"])
        for t in range(NT):
            b = t % 2
            P.dma("sp", xt[b][:], C.xres[t * 128:(t + 1) * 128, :], w=[("o_xt", b)])
            P.act(lambda e, b=b: e.activation(out=junk[:], in_=xt[b][:], func=AF.Square, accum_out=ssq[b][:]),
                  r=[("o_xt", b)], w=["o_junk", ("o_ssq", b)])
            P.act(lambda e, b=b: e.activation(out=ssq[b][:], in_=ssq[b][:], func=AF.Sqrt, scale=1.0 / D, bias=C.epsb[:]),
                  r=[("o_ssq", b)], w=[("o_ssq", b)])
            P.dve(lambda e, b=b: e.reciprocal(out=ssq[b][:], in_=ssq[b][:]), r=[("o_ssq", b)], w=[("o_ssq", b)])
            P.dve(lambda e, b=b: e.scalar_tensor_tensor(out=yo[b][:], in0=xt[b][:], scalar=ssq[b][:, 0:1], in1=gB[:],
                                                        op0=ALU.mult, op1=ALU.mult),
                  r=[("o_xt", b), ("o_ssq", b), "o_gB"], w=[("o_y", b)])
            last.append(P.dma("pool", dst[t * 128:(t + 1) * 128, :], yo[b][:], r=[("o_y", b)]))
        P.barrier(C.bar[:])
    return last


_CACHE = {}


def kernel(**inputs):
    S = SEQ
    inp = {k: np.asarray(v) for k, v in inputs.items()}
    nb = inp["x"].shape[0]
    nc, st = build(S, DEPTH, stage="full")
    consts = host_consts(S)
    in_maps = [make_in_map(inp, inp["x"][b], S, consts) for b in range(nb)]
    res = run_bass_kernel_spmd(nc, in_maps, core_ids=list(range(nb)))
    out = np.stack([np.asarray(r["out"], dtype=np.float32) for r in res.results], axis=0)
    return out


def make_in_map(inp, x, S, consts=None):
    c = consts if consts is not None else host_consts(S)
    f = lambda a: np.ascontiguousarray(np.asarray(a, dtype=np.float32))
    m = {
        "x": f(x),
        "norm_mix_g": f(inp["norm_mix_g"]),
        "w_in": f(inp["w_in"]),
        "mlstm_gate_bias": f(inp["mlstm_gate_bias"]).reshape(DEPTH, 16),
        "qk_norm_g": f(inp["qk_norm_g"]).reshape(DEPTH, 128),
        "mlstm_norm_g": f(inp["mlstm_norm_g"]),
        "diff_lambda": f(inp["diff_lambda"]).reshape(DEPTH, 256),
        "diff_norm_g": f(inp["diff_norm_g"]),
        "rel_bias": f(inp["rel_bias"]).reshape(1, 128),
        "w_branch": f(inp["w_branch"]),
        "w_out": f(inp["w_out"]),
        "norm_ffn_g": f(inp["norm_ffn_g"]),
        "w_up": f(inp["w_up"]),
        "conv_w": f(inp["conv_w"]),
        "conv_b": f(inp["conv_b"]),
        "w_down": f(inp["w_down"]),
        "final_norm_g": f(inp["final_norm_g"]).reshape(1, D),
    }
    m.update(c)
    return m
```

```python
import math
import contextlib
import numpy as np
import ml_dtypes
import concourse.bass as bass
import concourse.mybir as mybir
from concourse.bass_utils import run_bass_kernel_spmd

F32 = mybir.dt.float32
BF16 = mybir.dt.bfloat16
AF = mybir.ActivationFunctionType
ALU = mybir.AluOpType
AX = mybir.AxisListType

D = 1024
DEPTH = 2
BATCH = 4
SEQ = 4096
IN_W = 8976
DFF = 2816
EPS = 1e-6
COL = dict(a=0, bq=512, bk=1024, bv=1152, cq=1280, ck=1792, cv=2304, co=2816, cg=3328,
           dq=3344, dk=3856, dv=4368, g=4880)

COMPUTE = ("pe", "act", "dve", "pool")
NDMASEM = 8


class Op:
    __slots__ = ("eng", "fn", "deps", "signal", "ev", "is_dma")

    def __init__(self, eng, fn, is_dma=False):
        self.eng = eng
        self.fn = fn
        self.deps = set()
        self.signal = False
        self.ev = None
        self.is_dma = is_dma


class Prog:
    def __init__(self, nc, same_engine_sync=True):
        self.nc = nc
        self.ops = []
        self.last_w = {}
        self.readers = {}
        self.same_engine_sync = same_engine_sync
        self.engines = {"pe": nc.tensor, "act": nc.scalar, "dve": nc.vector,
                        "pool": nc.gpsimd, "sp": nc.sync}
        self.since_barrier = []
        self.barrier_op = None

    def op(self, eng, fn, r=(), w=(), dma=False):
        o = Op(eng, fn, is_dma=dma)
        idx = len(self.ops)
        if self.barrier_op is not None:
            o.deps.add(self.barrier_op)
        for k in r:
            lw = self.last_w.get(k)
            if lw is not None:
                o.deps.add(lw)
        for k in w:
            lw = self.last_w.get(k)
            if lw is not None:
                o.deps.add(lw)
            for rd in self.readers.get(k, ()):
                o.deps.add(rd)
        for k in r:
            self.readers.setdefault(k, []).append(idx)
        for k in w:
            self.last_w[k] = idx
            self.readers[k] = []
        self.ops.append(o)
        self.since_barrier.append(idx)
        return idx

    def barrier(self, scratch):
        prev = list(self.since_barrier)
        o = Op("dve", lambda e: e.memset(scratch, 0.0))
        if self.barrier_op is not None:
            o.deps.add(self.barrier_op)
        o.deps.update(prev)
        idx = len(self.ops)
        self.ops.append(o)
        self.barrier_op = idx
        self.since_barrier = []
        self.last_w = {}
        self.readers = {}
        return idx

    def pe(self, fn, r=(), w=()):
        return self.op("pe", fn, r, w)

    def act(self, fn, r=(), w=()):
        return self.op("act", fn, r, w)

    def dve(self, fn, r=(), w=()):
        return self.op("dve", fn, r, w)

    def pool(self, fn, r=(), w=()):
        return self.op("pool", fn, r, w)

    def dma(self, q, out, in_, r=(), w=(), **kw):
        return self.op(q, lambda e: e.dma_start(out=out, in_=in_, **kw), r, w, dma=True)

    def emit(self, final_wait_ops=()):
        nc = self.nc
        ops = self.ops
        for i, o in enumerate(ops):
            keep = set()
            for d in o.deps:
                p = ops[d]
                if p.is_dma:
                    keep.add(d)
                    continue
                if p.eng == o.eng and not o.is_dma:
                    if o.eng == "pe":
                        continue
                    if not self.same_engine_sync:
                        continue
                keep.add(d)
            latest = {}
            keep2 = set()
            for d in keep:
                p = ops[d]
                if p.is_dma:
                    keep2.add(d)
                else:
                    if p.eng not in latest or d > latest[p.eng]:
                        latest[p.eng] = d
            keep2.update(latest.values())
            o.deps = keep2
            for d in keep2:
                ops[d].signal = True
        for d in final_wait_ops:
            ops[d].signal = True
        es = contextlib.ExitStack()
        sems = {}
        for e in COMPUTE:
            sems[e] = es.enter_context(nc.semaphore("s_" + e))
        dsems = {}
        for q in ("sp", "pool", "act"):
            dsems[q] = [es.enter_context(nc.semaphore("d_%s%d" % (q, j))) for j in range(NDMASEM)]
        cnt = {e: 0 for e in COMPUTE}
        dcnt = {q: 0 for q in dsems}
        seen = {e: {} for e in self.engines}
        nwait = 0
        plan = {e: [] for e in self.engines}

        def need(engname, ev, waits):
            nonlocal nwait
            sem, val = ev
            s = seen[engname]
            if s.get(id(sem), 0) >= val:
                return
            s[id(sem)] = val
            waits.append((sem, val))
            nwait += 1

        for i, o in enumerate(ops):
            waits = []
            for d in sorted(o.deps):
                need(o.eng, ops[d].ev, waits)
            if o.is_dma:
                q = o.eng
                j = dcnt[q]
                dcnt[q] += 1
                sem = dsems[q][j % NDMASEM]
                if j >= NDMASEM:
                    need(q, (sem, 16 * (j // NDMASEM)), waits)
                o.ev = (sem, 16 * (j // NDMASEM + 1))
                plan[o.eng].append((waits, o, sem, 16))
            else:
                if o.signal:
                    cnt[o.eng] += 1
                    o.ev = (sems[o.eng], cnt[o.eng])
                    plan[o.eng].append((waits, o, sems[o.eng], 1))
                else:
                    plan[o.eng].append((waits, o, None, 0))
        fw = []
        for d in final_wait_ops:
            need("sp", ops[d].ev, fw)
        plan["sp"].append((fw, None, None, 0))

        def run_engine(name, e):
            for waits, o, sem, inc in plan[name]:
                for (ws, wv) in waits:
                    e.wait_ge(ws, wv)
                if o is None:
                    continue
                ins = o.fn(e)
                if sem is not None:
                    ins.then_inc(sem, inc)

        with nc.Block() as block:
            @block.sync
            def _(e):
                run_engine("sp", e)

            @block.tensor
            def _(e):
                run_engine("pe", e)

            @block.scalar
            def _(e):
                run_engine("act", e)

            @block.vector
            def _(e):
                run_engine("dve", e)

            @block.gpsimd
            def _(e):
                run_engine("pool", e)
        self.stats = dict(n_ops=len(ops), n_wait=nwait, cnt=dict(cnt), dcnt=dict(dcnt))
        es.close()
        return self.stats


_UNIQ = [0]


def _uniq(n):
    _UNIQ[0] += 1
    return "%s_%d" % (n, _UNIQ[0])


class Ring:
    def __init__(self, aps, name):
        self.aps = aps
        self.name = name
        self.i = 0

    def next(self):
        j = self.i % len(self.aps)
        self.i += 1
        return self.aps[j], (self.name, j)


def rel_bucket_np(rel):
    half = 16
    max_exact = 8
    ret = np.where(rel > 0, half, 0)
    n = np.abs(rel)
    nf = np.maximum(n, 1).astype(np.float32)
    large = max_exact + (np.log(nf / np.float32(max_exact)) / np.float32(math.log(128 / max_exact))
                         * np.float32(half - max_exact)).astype(np.int32)
    large = np.minimum(large, half - 1)
    return ret + np.where(n < max_exact, n, large)


def host_consts(S):
    c = {}
    k = np.arange(S, dtype=np.int64)
    ks = (k[:, None] * k[None, :]) % S
    ang = (2.0 * np.pi / S) * ks.astype(np.float64)
    c["dftc"] = np.cos(ang).astype(np.float32).astype(ml_dtypes.bfloat16)
    c["dfts"] = (-np.sin(ang)).astype(np.float32).astype(ml_dtypes.bfloat16)
    j = np.arange(64)
    a64 = 2.0 * np.pi * ((j[:, None] * j[None, :]) % 64) / 64.0
    nrm = 1.0 / math.sqrt(S * 64.0)
    bd = np.zeros((2, 128, 128), np.float64)
    for g in range(2):
        bd[0, g * 64:(g + 1) * 64, g * 64:(g + 1) * 64] = np.cos(a64) * nrm
        bd[1, g * 64:(g + 1) * 64, g * 64:(g + 1) * 64] = np.sin(a64) * nrm
    c["bdcs"] = bd.astype(np.float32).astype(ml_dtypes.bfloat16)
    rows = S // 64
    row_id = np.repeat(np.arange(rows, dtype=np.float32), 64)
    col_id = np.tile(np.arange(64, dtype=np.float32), rows)
    inv = (np.float32(10000.0) ** (-np.arange(16, dtype=np.float32) / np.float32(16))).astype(np.float32)
    angr = np.concatenate([row_id[:, None] * inv, col_id[:, None] * inv], axis=-1).astype(np.float32)
    c["ropec"] = np.cos(angr).astype(np.float32)
    c["ropes"] = np.sin(angr).astype(np.float32)
    i = np.arange(128)[:, None]
    m = np.arange(1152)[None, :]
    c["relf"] = (i - m + 512).astype(np.float32)
    return c


def bias_steps():
    rel = np.arange(-700, 701)
    b = rel_bucket_np(rel)
    seq = [int(b[0])]
    thr = []
    for idx in range(1, len(rel)):
        if b[idx] != b[idx - 1]:
            seq.append(int(b[idx]))
            thr.append(int(rel[idx]))
    return seq, thr


class Ctx:
    pass


def build(S, depth, stage="full", taps=()):
    NT = S // 128
    NQC = S // 512
    nc = bass.Bass("TRN2", target_bir_lowering=False)
    P = Prog(nc)
    C = Ctx()
    C.nc, C.P, C.S, C.NT, C.NQC = nc, P, S, NT, NQC

    def din(name, shape, dt=F32):
        return nc.dram_tensor(name, list(shape), dt, kind="ExternalInput").ap()

    I = {}
    I["x"] = din("x", [S, D])
    I["norm_mix_g"] = din("norm_mix_g", [DEPTH, D])
    I["w_in"] = din("w_in", [DEPTH, D, IN_W])
    I["mlstm_gate_bias"] = din("mlstm_gate_bias", [DEPTH, 16])
    I["qk_norm_g"] = din("qk_norm_g", [DEPTH, 128])
    I["mlstm_norm_g"] = din("mlstm_norm_g", [DEPTH, 512])
    I["diff_lambda"] = din("diff_lambda", [DEPTH, 256])
    I["diff_norm_g"] = din("diff_norm_g", [DEPTH, 128])
    I["rel_bias"] = din("rel_bias", [1, 128])
    I["w_branch"] = din("w_branch", [DEPTH, 4, 512, D])
    I["w_out"] = din("w_out", [DEPTH, D, D])
    I["norm_ffn_g"] = din("norm_ffn_g", [DEPTH, D])
    I["w_up"] = din("w_up", [DEPTH, D, 2 * DFF])
    I["conv_w"] = din("conv_w", [DEPTH, 3, DFF])
    I["conv_b"] = din("conv_b", [DEPTH, DFF])
    I["w_down"] = din("w_down", [DEPTH, DFF, D])
    I["final_norm_g"] = din("final_norm_g", [1, D])
    I["dftc"] = din("dftc", [S, S], BF16)
    I["dfts"] = din("dfts", [S, S], BF16)
    I["bdcs"] = din("bdcs", [2, 128, 128], BF16)
    I["ropec"] = din("ropec", [S, 32])
    I["ropes"] = din("ropes", [S, 32])
    I["relf"] = din("relf", [128, 1152])
    C.I = I
    out = nc.dram_tensor("out", [S, D], F32, kind="ExternalOutput").ap()
    C.tap = {}
    for (nm, shp, dt) in taps:
        C.tap[nm] = nc.dram_tensor("tap_" + nm, list(shp), dt, kind="ExternalOutput").ap()

    def dscr(name, shape, dt=BF16):
        return nc.dram_tensor(name, list(shape), dt).ap()

    C.xres = dscr("xres", [S, D], F32)
    C.uT_d = dscr("uT_d", [D, S])
    C.ynT = [dscr("ynT%d" % n, [512, S]) for n in range(4)]

    ges = contextlib.ExitStack()
    C.ges = ges

    def gsb(name, shape, dt):
        return ges.enter_context(nc.sbuf_tensor(name, list(shape), dt))

    C.psb = [ges.enter_context(nc.psum_tensor("psb%d" % i, [128, 1024], F32)) for i in range(3)]
    C.pst = [ges.enter_context(nc.psum_tensor("pst%d" % i, [128, 1024], BF16)) for i in range(2)]
    C.identf = gsb("identf", [128, 128], F32)
    C.ident = gsb("ident", [128, 128], BF16)
    C.maskL = gsb("maskL", [128, 128], F32)
    C.maskU = gsb("maskU", [128, 128], F32)
    C.onesf = gsb("onesf", [128, 128], F32)
    C.maskLb = gsb("maskLb", [128, 128], BF16)
    C.maskUb = gsb("maskUb", [128, 128], BF16)
    C.onesb = gsb("onesb", [128, 128], BF16)
    C.bar = gsb("bar", [128, 1], F32)
    C.rbB = gsb("rbB", [128, 128], F32)
    C.epsb = gsb("epsb", [128, 1], F32)
    C.lnk = gsb("lnk", [128, 1], F32)

    P.pool(lambda e: e.iota(C.identf[:], [[1, 128]], 0, channel_multiplier=-1,
                            allow_small_or_imprecise_dtypes=True), w=["identf"])
    P.dve(lambda e: e.tensor_single_scalar(out=C.ident[:], in_=C.identf[:], scalar=0.0, op=ALU.is_equal),
          r=["identf"], w=["ident"])
    P.dve(lambda e: e.tensor_single_scalar(out=C.maskL[:], in_=C.identf[:], scalar=0.0, op=ALU.is_ge),
          r=["identf"], w=["maskL"])
    P.dve(lambda e: e.tensor_single_scalar(out=C.maskU[:], in_=C.identf[:], scalar=0.0, op=ALU.is_le),
          r=["identf"], w=["maskU"])
    P.dve(lambda e: e.memset(C.onesf[:], 1.0), w=["onesf"])
    P.dve(lambda e: e.memset(C.onesb[:], 1.0), w=["onesb"])
    P.dve(lambda e: e.tensor_copy(out=C.maskLb[:], in_=C.maskL[:]), r=["maskL"], w=["maskLb"])
    P.dve(lambda e: e.tensor_copy(out=C.maskUb[:], in_=C.maskU[:]), r=["maskU"], w=["maskUb"])
    P.dve(lambda e: e.memset(C.epsb[:], EPS), w=["epsb"])
    P.dve(lambda e: e.memset(C.lnk[:], -0.5 * math.log(128.0)), w=["lnk"])
    P.dma("sp", C.rbB[:], I["rel_bias"].partition_broadcast(128), w=["rbB"])
    P.barrier(C.bar[:])

    setup_bias_tables(C)
    fin = []
    done = False
    for layer in range(depth):
        xsrc = I["x"] if layer == 0 else C.xres
        phase_norm(C, xsrc, I["norm_mix_g"][layer:layer + 1, :], C.uT_d)
        if stage == "norm":
            fin.append(copy_dram(C, C.tap["uT"], C.uT_d, [D, S], BF16)); break
        if stage in ("gqa", "full", "merge", "layer"):
            phase_gqa(C, layer)
        if stage == "gqa":
            fin.append(copy_dram(C, C.tap["ybT"], C.ynT[1], [512, S], BF16)); break
        if stage in ("diff", "full", "merge", "layer"):
            phase_diff(C, layer)
        if stage == "diff":
            fin.append(copy_dram(C, C.tap["ydT"], C.ynT[3], [512, S], BF16)); break
        if stage in ("four", "full", "merge", "layer"):
            phase_four(C, layer)
        if stage == "four":
            fin.append(copy_dram(C, C.tap["yaT"], C.ynT[0], [512, S], BF16)); break
        if stage in ("mlstm", "full", "merge", "layer"):
            phase_mlstm(C, layer)
        if stage == "mlstm":
            fin.append(copy_dram(C, C.tap["ycT"], C.ynT[2], [512, S], BF16)); break
        phase_merge(C, layer, xsrc)
        if stage == "merge":
            fin.append(copy_dram(C, C.tap["xmid"], C.xres, [S, D], F32)); break
        phase_norm(C, C.xres, I["norm_ffn_g"][layer:layer + 1, :], C.uT_d)
        phase_ffn(C, layer)
        if stage == "layer":
            fin.append(copy_dram(C, C.tap["xl"], C.xres, [S, D], F32)); break
    if stage == "full":
        fin.extend(phase_out(C, out))
    st = P.emit(final_wait_ops=fin)
    ges.close()
    return nc, st


def copy_dram(C, dst, src, shape, dt):
    P, nc = C.P, C.nc
    rows, cols = shape
    last = None
    with contextlib.ExitStack() as es:
        t = es.enter_context(nc.sbuf_tensor(_uniq("cpy"), [128, cols], dt))
        for r0 in range(0, rows, 128):
            P.dma("sp", t[:], src[r0:r0 + 128, :], w=["cpy"])
            last = P.dma("sp", dst[r0:r0 + 128, :], t[:], r=["cpy"])
        P.barrier(C.bar[:])
    return last


def phase_norm(C, xsrc, g_row, dstT):
    P, nc, S, NT = C.P, C.nc, C.S, C.NT
    with contextlib.ExitStack() as es:
        sb = lambda n, s, d: es.enter_context(nc.sbuf_tensor(_uniq(n), list(s), d))
        gB = sb("n_gB", [128, D], F32)
        xt = [sb("n_xt%d" % i, [128, D], F32) for i in range(4)]
        junk = sb("n_junk", [128, D], F32)
        ub = [sb("n_ub%d" % i, [128, D], BF16) for i in range(4)]
        ssq = [sb("n_ssq%d" % i, [128, 1], F32) for i in range(4)]
        stg = [sb("n_stg%d" % i, [128, 8, 512], BF16) for i in range(2)]
        P.dma("sp", gB[:], g_row.partition_broadcast(128), w=["n_gB"])
        def stage_a(t):
            b = t % 4
            P.dma("sp", xt[b][:], xsrc[t * 128:(t + 1) * 128, :], w=[("n_xt", b)])
            P.act(lambda e, b=b: e.activation(out=junk[:], in_=xt[b][:], func=AF.Square, accum_out=ssq[b][:]),
                  r=[("n_xt", b)], w=["n_junk", ("n_ssq", b)])
            P.act(lambda e, b=b: e.activation(out=ssq[b][:], in_=ssq[b][:], func=AF.Sqrt, scale=1.0 / D, bias=C.epsb[:]),
                  r=[("n_ssq", b)], w=[("n_ssq", b)])
            P.dve(lambda e, b=b: e.reciprocal(out=ssq[b][:], in_=ssq[b][:]), r=[("n_ssq", b)], w=[("n_ssq", b)])
            P.dve(lambda e, b=b: e.scalar_tensor_tensor(out=ub[b][:], in0=xt[b][:], scalar=ssq[b][:, 0:1], in1=gB[:],
                                                        op0=ALU.mult, op1=ALU.mult),
                  r=[("n_xt", b), ("n_ssq", b), "n_gB"], w=[("n_ub", b)])
            pb_ = t % 2
            pt = C.pst[pb_]
            for j in range(8):
                P.pe(lambda e, b=b, j=j, pt=pt: e.transpose(pt[:, j * 128:(j + 1) * 128], ub[b][:, j * 128:(j + 1) * 128], C.ident[:]),
                     r=[("n_ub", b)], w=[("pst", pb_)])

        def stage_b(t):
            pb_ = t % 2
            pt = C.pst[pb_]
            sgi = (t // 4) % 2
            tt = t % 4
            P.act(lambda e, pt=pt, sgi=sgi, tt=tt: e.copy(out=stg[sgi][:, :, tt * 128:(tt + 1) * 128],
                                                          in_=pt[:].rearrange("p (j c) -> p j c", j=8)),
                  r=[("pst", pb_)], w=[("n_stg", sgi)])
            if tt == 3:
                t0 = (t // 4) * 512
                P.dma("pool", dstT[:, t0:t0 + 512].rearrange("(j p) t -> p j t", p=128), stg[sgi][:],
                      r=[("n_stg", sgi)])

        stage_a(0)
        for t in range(NT):
            if t + 1 < NT:
                stage_a(t + 1)
            stage_b(t)
        P.barrier(C.bar[:])


def load_T(C, dst, srcT, ncc, q="sp", key=None):
    C.P.dma(q, dst[:], srcT.rearrange("(j p) t -> p j t", p=128), w=[key])


def load_w(C, dst, wsrc, key, q="pool"):
    C.P.dma(q, dst, wsrc.rearrange("(j p) n -> p j n", p=128), w=[key])


def attention(C, QT, KT, qk_key, Vaug, v_key, dv, scale, ptile, out_cb, bias_fn=None, obufs=None):
    P, S, NT, NQC = C.P, C.S, C.NT, C.NQC
    NP = NT // 2
    steps = [(qc, sp) for qc in range(NQC) for sp in range(NP)]
    po = C.psb[2]

    def bias_of(st, qc):
        return bias_fn(st, qc) if bias_fn is not None else None

    def issue_qk(i):
        qc, sp = steps[i]
        buf = i % 2
        ps = C.psb[buf]
        for u in range(2):
            st = 2 * sp + u
            psS = ps[:, u * 512:(u + 1) * 512]
            kS = ("psb", buf, u)
            bias = bias_of(st, qc)
            band = bias is not None and bias[0] == "band"
            P.pe(lambda e, psS=psS, st=st, qc=qc, band=band: e.matmul(
                psS, lhsT=KT[:, st * 128:(st + 1) * 128], rhs=QT[:, qc * 512:(qc + 1) * 512],
                start=True, stop=not band), r=[qk_key], w=[kS])
            if band:
                P.pe(lambda e, psS=psS, bt=bias[1]: e.matmul(psS, lhsT=C.ident[:], rhs=bt, start=False, stop=True),
                     r=[bias[2], "ident"], w=[kS])

    def issue_exp(i):
        qc, sp = steps[i]
        buf = i % 2
        ps = C.psb[buf]
        pt, pk = ptile.next()
        b0 = bias_of(2 * sp, qc)
        b1 = bias_of(2 * sp + 1, qc)
        c0 = b0 if (b0 is not None and b0[0] == "const") else None
        c1 = b1 if (b1 is not None and b1[0] == "const") else None
        same = (c0 is None and c1 is None)
        if same:
            if c0 is None:
                P.act(lambda e, pt=pt, ps=ps: e.activation(out=pt[:, 0:1024], in_=ps[:, 0:1024], func=AF.Exp, scale=scale),
                      r=[("psb", buf, 0), ("psb", buf, 1)], w=[pk])
            else:
                P.act(lambda e, pt=pt, ps=ps, bb=c0[1]: e.activation(out=pt[:, 0:1024], in_=ps[:, 0:1024], func=AF.Exp, scale=scale, bias=bb),
                      r=[("psb", buf, 0), ("psb", buf, 1), c0[2]], w=[pk])
        else:
            for u, cb in enumerate((c0, c1)):
                if cb is None:
                    P.act(lambda e, pt=pt, ps=ps, u=u: e.activation(out=pt[:, u * 512:(u + 1) * 512], in_=ps[:, u * 512:(u + 1) * 512],
                                                                   func=AF.Exp, scale=scale), r=[("psb", buf, u)], w=[pk])
                else:
                    P.act(lambda e, pt=pt, ps=ps, u=u, bb=cb[1]: e.activation(out=pt[:, u * 512:(u + 1) * 512], in_=ps[:, u * 512:(u + 1) * 512],
                                                                             func=AF.Exp, scale=scale, bias=bb),
                          r=[("psb", buf, u), cb[2]], w=[pk])
        return pt, pk

    def issue_pv(i, pt, pk):
        qc, sp = steps[i]
        for u in range(2):
            st = 2 * sp + u
            for qt in range(4):
                o_ap = po[:, qt * 256:qt * 256 + dv + 1]
                P.pe(lambda e, o_ap=o_ap, pt=pt, qt=qt, st=st, u=u, sp=sp: e.matmul(
                    o_ap, lhsT=pt[:, u * 512 + qt * 128:u * 512 + (qt + 1) * 128], rhs=Vaug(st),
                    start=(sp == 0 and u == 0 and qt % 2 == 0), stop=(st == NT - 1), skip_group_check=True),
                    r=[pk, v_key], w=[("psb", 2, qt // 2)])
        if sp == NP - 1:
            ob, ok = obufs.next()
            for bk in range(2):
                P.dve(lambda e, ob=ob, bk=bk: e.tensor_copy(
                    out=ob[:, bk * 512:(bk + 1) * 512].rearrange("p (q c) -> p q c", q=2)[:, :, 0:dv + 1],
                    in_=po[:, bk * 512:(bk + 1) * 512].rearrange("p (q c) -> p q c", q=2)[:, :, 0:dv + 1]),
                    r=[("psb", 2, bk)], w=[ok])
            for qt in range(4):
                out_cb(qc, qt, ob[:, qt * 256:qt * 256 + dv + 1], ok)

    n = len(steps)
    issue_qk(0)
    for i in range(n):
        if i + 1 < n:
            issue_qk(i + 1)
        pt, pk = issue_exp(i)
        issue_pv(i, pt, pk)


def phase_gqa(C, layer):
    P, nc, S, NT, NQC, I = C.P, C.nc, C.S, C.NT, C.NQC, C.I
    with contextlib.ExitStack() as es:
        sb = lambda n, s, d: es.enter_context(nc.sbuf_tensor(_uniq(n), list(s), d))
        QT = sb("b_QT", [128, 4, S], BF16)
        KT = sb("b_KT", [128, 2, S], BF16)
        V = sb("b_V", [128, NT, 2, 65], BF16)
        ropec = sb("b_rc", [128, NT, 32], F32)
        ropes = sb("b_rs", [128, NT, 32], F32)
        g640 = sb("b_g", [128, 10, 64], F32)
        gq = sb("b_gq", [128, 128], F32)
        with contextlib.ExitStack() as es2:
            sb2 = lambda n, s, d: es2.enter_context(nc.sbuf_tensor(_uniq(n), list(s), d))
            uT = sb2("b_uT", [128, 8, S], BF16)
            wB = sb2("b_w", [128, 8, 768], BF16)
            sq = sb2("b_sq", [128, 10, 64], F32)
            ss = sb2("b_ss", [128, 10], F32)
            qn = sb2("b_qn", [128, 10, 64], F32)
            t1 = sb2("b_t1", [128, 10, 32], F32)
            t2 = sb2("b_t2", [128, 10, 32], F32)
            qr = [sb2("b_qr%d" % i, [128, 12, 64], BF16) for i in range(2)]
            load_T(C, uT, C.uT_d, 8, key="b_uT")
            load_w(C, wB[:], I["w_in"][layer, :, COL["bq"]:COL["bq"] + 768], "b_w")
            P.dma("sp", ropec[:], I["ropec"].rearrange("(t p) c -> p t c", p=128), w=["b_rc"])
            P.dma("sp", ropes[:], I["ropes"].rearrange("(t p) c -> p t c", p=128), w=["b_rs"])
            P.dma("sp", gq[:], I["qk_norm_g"][layer:layer + 1, :].partition_broadcast(128), w=["b_gq"])
            P.dve(lambda e: e.memset(V[:], 1.0), w=["b_V"])
            P.dve(lambda e: e.tensor_copy(out=g640[:, 0:8, :], in_=gq[:, 0:64].unsqueeze(1).to_broadcast([128, 8, 64])),
                  r=["b_gq"], w=["b_g"])
            P.dve(lambda e: e.tensor_copy(out=g640[:, 8:10, :], in_=gq[:, 64:128].unsqueeze(1).to_broadcast([128, 2, 64])),
                  r=["b_gq", "b_g"], w=["b_g"])
            def gq_a(t):
                b = t % 2
                ps = C.psb[b]
                for half, (c0, n) in enumerate(((0, 512), (512, 256))):
                    for j in range(8):
                        P.pe(lambda e, ps=ps, half=half, c0=c0, n=n, j=j, t=t: e.matmul(
                            ps[:, half * 512:half * 512 + n], lhsT=uT[:, j, t * 128:(t + 1) * 128],
                            rhs=wB[:, j, c0:c0 + n], start=(j == 0), stop=(j == 7)),
                            r=["b_uT", "b_w"], w=[("psb", b, half)])
            def gq_b(t):
                b = t % 2
                ps = C.psb[b]
                kq = [("psb", b, 0), ("psb", b, 1)]
                qk_ps = ps[:, 0:640].rearrange("p (h d) -> p h d", d=64)
                P.act(lambda e, qk_ps=qk_ps: e.activation(out=sq[:], in_=qk_ps, func=AF.Square), r=kq, w=["b_sq"])
                P.dve(lambda e: e.tensor_reduce(out=ss[:], in_=sq[:], axis=AX.X, op=ALU.add), r=["b_sq"], w=["b_ss"])
                P.act(lambda e: e.activation(out=ss[:], in_=ss[:], func=AF.Sqrt, scale=1.0 / 64, bias=C.epsb[:]),
                      r=["b_ss"], w=["b_ss"])
                P.dve(lambda e: e.reciprocal(out=ss[:], in_=ss[:]), r=["b_ss"], w=["b_ss"])
                P.dve(lambda e, qk_ps=qk_ps: e.tensor_tensor(out=qn[:], in0=qk_ps, in1=ss[:].unsqueeze(2).to_broadcast([128, 10, 64]),
                                                              op=ALU.mult), r=kq + ["b_ss"], w=["b_qn"])
                P.dve(lambda e: e.tensor_tensor(out=qn[:], in0=qn[:], in1=g640[:], op=ALU.mult), r=["b_qn", "b_g"], w=["b_qn"])
                cb = ropec[:, t, :].unsqueeze(1).to_broadcast([128, 10, 32])
                sbb = ropes[:, t, :].unsqueeze(1).to_broadcast([128, 10, 32])
                x1 = qn[:, :, 0:32]
                x2 = qn[:, :, 32:64]
                q_out = qr[b]
                P.dve(lambda e, cb=cb: e.tensor_tensor(out=t1[:], in0=x1, in1=cb, op=ALU.mult), r=["b_qn", "b_rc"], w=["b_t1"])
                P.dve(lambda e, sbb=sbb: e.tensor_tensor(out=t2[:], in0=x2, in1=sbb, op=ALU.mult), r=["b_qn", "b_rs"], w=["b_t2"])
                P.dve(lambda e, q_out=q_out: e.tensor_tensor(out=q_out[:, 0:10, 0:32], in0=t1[:], in1=t2[:], op=ALU.subtract),
                      r=["b_t1", "b_t2"], w=[("b_qr", b)])
                P.dve(lambda e, sbb=sbb: e.tensor_tensor(out=t1[:], in0=x1, in1=sbb, op=ALU.mult), r=["b_qn", "b_rs", ("b_qr", b)], w=["b_t1"])
                P.dve(lambda e, cb=cb: e.tensor_tensor(out=t2[:], in0=x2, in1=cb, op=ALU.mult), r=["b_qn", "b_rc", ("b_qr", b)], w=["b_t2"])
                P.dve(lambda e, q_out=q_out: e.tensor_tensor(out=q_out[:, 0:10, 32:64], in0=t1[:], in1=t2[:], op=ALU.add),
                      r=["b_t1", "b_t2"], w=[("b_qr", b)])
                P.dve(lambda e, q_out=q_out: e.tensor_copy(out=q_out[:, 10:12, :], in_=q_out[:, 9:10, :].to_broadcast([128, 2, 64])),
                      r=[("b_qr", b)], w=[("b_qr", b)])
                P.dve(lambda e, q_out=q_out: e.tensor_copy(out=q_out[:, 9:10, :], in_=q_out[:, 8:9, :]),
                      r=[("b_qr", b)], w=[("b_qr", b)])
                pt = C.pst[b]
                for j in range(6):
                    P.pe(lambda e, pt=pt, j=j, q_out=q_out: e.transpose(
                        pt[:, j * 128:(j + 1) * 128], q_out[:, 2 * j:2 * j + 2, :].rearrange("p a d -> p (a d)"), C.ident[:]),
                        r=[("b_qr", b)], w=[("pst", b)])
                P.act(lambda e, pt=pt, t=t: e.copy(out=QT[:, :, t * 128:(t + 1) * 128],
                                                   in_=pt[:, 0:512].rearrange("p (j c) -> p j c", j=4)),
                      r=[("pst", b)], w=["b_QT"])
                P.act(lambda e, pt=pt, t=t: e.copy(out=KT[:, :, t * 128:(t + 1) * 128],
                                                   in_=pt[:, 512:768].rearrange("p (j c) -> p j c", j=2)),
                      r=[("pst", b)], w=["b_KT"])
                P.act(lambda e, ps=ps, t=t: e.copy(out=V[:, t, :, 0:64], in_=ps[:, 640:768].rearrange("p (g d) -> p g d", g=2)),
                      r=kq, w=["b_V"] + kq)
            gq_a(0)
            for t in range(NT):
                if t + 1 < NT:
                    gq_a(t + 1)
                gq_b(t)
            P.barrier(C.bar[:])
        pts = [sb("b_pt%d" % i, [128, 1024], BF16) for i in range(3)]
        ring = Ring([p[:] for p in pts], "b_pt")
        obs = [sb("b_ob%d" % i, [128, 1024], F32) for i in range(2)]
        obufs = Ring([p[:] for p in obs], "b_ob")
        rec = sb("b_rec", [128, 1], F32)
        stg = sb("b_stg", [128, 4, 512], BF16)
        ytokall = sb("b_ytokall", [128, NT, 512], BF16)
        for h in range(8):
            g = h // 4
            base = 64 * (h % 2)
            QTh = QT[base:base + 64, h // 2, :]
            KTh = KT[base:base + 64, g, :]

            def out_cb(qc, qt, ps_ap, ps_key, h=h):
                P.dve(lambda e: e.reciprocal(out=rec[:], in_=ps_ap[:, 64:65]), r=[ps_key], w=["b_rec"])
                P.dve(lambda e: e.tensor_scalar(out=ytokall[:, qc * 4 + qt, h * 64:(h + 1) * 64], in0=ps_ap[:, 0:64],
                                                scalar1=rec[:, 0:1], scalar2=None, op0=ALU.mult),
                      r=[ps_key, "b_rec"], w=["b_ytokall"])
            attention(C, QTh, KTh, "b_QT", lambda st, g=g: V[:, st, g, :], "b_V", 64, 0.125, ring, out_cb, obufs=obufs)
        for t in range(NT):
            b = t % 2
            pt = C.pst[b]
            for j in range(4):
                P.pe(lambda e, pt=pt, j=j, t=t: e.transpose(pt[:, j * 128:(j + 1) * 128], ytokall[:, t, j * 128:(j + 1) * 128], C.ident[:]),
                     r=["b_ytokall"], w=[("pst", b)])
            tt = t % 4
            P.act(lambda e, pt=pt, tt=tt: e.copy(out=stg[:, :, tt * 128:(tt + 1) * 128],
                                                 in_=pt[:, 0:512].rearrange("p (j c) -> p j c", j=4)),
                  r=[("pst", b)], w=["b_stg"])
            if tt == 3:
                t0 = (t // 4) * 512
                P.dma("pool", C.ynT[1][:, t0:t0 + 512].rearrange("(j p) t -> p j t", p=128), stg[:], r=["b_stg"])
        P.barrier(C.bar[:])


def lam_init_of(layer):
    return 0.8 - 0.6 * math.exp(-0.3 * layer)


def setup_bias_tables(C):
    P, nc = C.P, C.nc
    seq, thr = bias_steps()
    C.W2b = [C.ges.enter_context(nc.sbuf_tensor("W2b%d" % h, [128, 1152], BF16)) for h in range(4)]
    with contextlib.ExitStack() as es:
        sb = lambda n, s, d: es.enter_context(nc.sbuf_tensor(_uniq(n), list(s), d))
        relf = sb("s_relf", [128, 1152], F32)
        acc = sb("s_acc", [128, 1152], F32)
        tmp = sb("s_tmp", [128, 1152], F32)
        dB = sb("s_dB", [128, len(thr), 4], F32)
        P.dma("sp", relf[:], C.I["relf"], w=["s_relf"])
        rb3 = C.rbB[:].rearrange("p (b h) -> p b h", h=4)
        for k in range(len(thr)):
            P.dve(lambda e, k=k: e.tensor_tensor(out=dB[:, k, :], in0=rb3[:, seq[k + 1], :], in1=rb3[:, seq[k], :], op=ALU.subtract),
                  r=["rbB"], w=["s_dB"])
        for h in range(4):
            P.dve(lambda e, h=h: e.tensor_scalar(out=acc[:], in0=relf[:], scalar1=0.0, scalar2=C.rbB[:, seq[0] * 4 + h:seq[0] * 4 + h + 1],
                                                 op0=ALU.mult, op1=ALU.add), r=["s_relf", "rbB"], w=["s_acc"])
            for k in range(len(thr)):
                P.dve(lambda e, k=k, h=h: e.tensor_scalar(out=tmp[:], in0=relf[:], scalar1=float(thr[k]), scalar2=dB[:, k, h:h + 1],
                                                          op0=ALU.is_ge, op1=ALU.mult), r=["s_relf", "s_dB"], w=["s_tmp"])
                P.dve(lambda e: e.tensor_tensor(out=acc[:], in0=acc[:], in1=tmp[:], op=ALU.add), r=["s_acc", "s_tmp"], w=["s_acc"])
            P.act(lambda e, h=h: e.activation(out=C.W2b[h][:], in_=acc[:], func=AF.Copy, scale=8.0), r=["s_acc"], w=[("W2b", h)])
        P.barrier(C.bar[:])


def phase_diff(C, layer):
    P, nc, S, NT, NQC, I = C.P, C.nc, C.S, C.NT, C.NQC, C.I
    li = lam_init_of(layer)
    with contextlib.ExitStack() as es:
        sb = lambda n, s, d: es.enter_context(nc.sbuf_tensor(_uniq(n), list(s), d))
        QT = sb("d_QT", [128, 4, S], BF16)
        KT = sb("d_KT", [128, 4, S], BF16)
        V = sb("d_V", [128, NT, 4, 129], BF16)
        with contextlib.ExitStack() as es2:
            sb2 = lambda n, s, d: es2.enter_context(nc.sbuf_tensor(_uniq(n), list(s), d))
            uT = sb2("d_uT", [128, 8, S], BF16)
            wqk = sb2("d_wqk", [128, 8, 1024], BF16)
            load_T(C, uT, C.uT_d, 8, key="d_uT")
            load_w(C, wqk[:], I["w_in"][layer, :, COL["dq"]:COL["dq"] + 1024], "d_wqk")
            P.dve(lambda e: e.memset(V[:], 1.0), w=["d_V"])
            k = 0
            for blk in range(8):
                dst = QT if blk < 4 else KT
                for tc in range(NQC):
                    bi = k % 4
                    k += 1
                    ps = C.psb[bi // 2][:, (bi % 2) * 512:(bi % 2) * 512 + 512]
                    pk = ("psb", bi // 2, bi % 2)
                    for j in range(8):
                        P.pe(lambda e, ps=ps, j=j, blk=blk, tc=tc: e.matmul(
                            ps, lhsT=wqk[:, j, blk * 128:(blk + 1) * 128], rhs=uT[:, j, tc * 512:(tc + 1) * 512],
                            start=(j == 0), stop=(j == 7)), r=["d_uT", "d_wqk"], w=[pk])
                    eng = P.act if k % 2 == 0 else P.dve
                    if k % 2 == 0:
                        P.act(lambda e, ps=ps, dst=dst, blk=blk, tc=tc: e.copy(out=dst[:, blk % 4, tc * 512:(tc + 1) * 512], in_=ps),
                              r=[pk], w=["d_QK"])
                    else:
                        P.dve(lambda e, ps=ps, dst=dst, blk=blk, tc=tc: e.tensor_copy(out=dst[:, blk % 4, tc * 512:(tc + 1) * 512], in_=ps),
                              r=[pk], w=["d_QK"])
            wv = wqk[:, :, 0:512]
            load_w(C, wv, I["w_in"][layer, :, COL["dv"]:COL["dv"] + 512], "d_wqk")
            for t in range(NT):
                bi = t % 4
                ps = C.psb[bi // 2][:, (bi % 2) * 512:(bi % 2) * 512 + 512]
                pk = ("psb", bi // 2, bi % 2)
                for j in range(8):
                    P.pe(lambda e, ps=ps, j=j, t=t: e.matmul(ps, lhsT=uT[:, j, t * 128:(t + 1) * 128], rhs=wv[:, j, :],
                                                            start=(j == 0), stop=(j == 7)), r=["d_uT", "d_wqk"], w=[pk])
                P.act(lambda e, ps=ps, t=t: e.copy(out=V[:, t, :, 0:128], in_=ps.rearrange("p (h d) -> p h d", h=4)),
                      r=[pk], w=["d_V"])
            P.barrier(C.bar[:])
        lpB = sb("d_lp", [128, 256], F32)
        lpr = sb("d_lpr", [128, 128], F32)
        s12 = sb("d_s12", [128, 2], F32)
        nlam = sb("d_nlam", [128, 1], F32)
        gsub = sb("d_gsub", [128, 128], F32)
        P.dma("sp", lpB[:], I["diff_lambda"][layer:layer + 1, :].partition_broadcast(128), w=["d_lp"])
        P.dma("sp", gsub[:], I["diff_norm_g"][layer:layer + 1, :].partition_broadcast(128), w=["d_gsub"])
        lp4 = lpB[:].rearrange("p (a b d) -> p a b d", a=2, b=2)
        P.dve(lambda e: e.tensor_tensor(out=lpr[:].rearrange("p (a d) -> p a d", a=2), in0=lp4[:, :, 0, :], in1=lp4[:, :, 1, :], op=ALU.mult),
              r=["d_lp"], w=["d_lpr"])
        P.dve(lambda e: e.tensor_reduce(out=s12[:], in_=lpr[:].rearrange("p (a d) -> p a d", a=2), axis=AX.X, op=ALU.add),
              r=["d_lpr"], w=["d_s12"])
        P.act(lambda e: e.activation(out=s12[:], in_=s12[:], func=AF.Exp), r=["d_s12"], w=["d_s12"])
        P.dve(lambda e: e.tensor_tensor(out=nlam[:], in0=s12[:, 1:2], in1=s12[:, 0:1], op=ALU.subtract), r=["d_s12"], w=["d_nlam"])
        P.dve(lambda e: e.tensor_scalar_add(out=nlam[:], in0=nlam[:], scalar1=-li), r=["d_nlam"], w=["d_nlam"])
        P.dve(lambda e: e.tensor_scalar_mul(out=gsub[:], in0=gsub[:], scalar1=1.0 - li), r=["d_gsub"], w=["d_gsub"])
        pts = [sb("d_pt%d" % i, [128, 1024], BF16) for i in range(3)]
        ring = Ring([p[:] for p in pts], "d_pt")
        obs = [sb("d_ob%d" % i, [128, 1024], F32) for i in range(2)]
        obufs = Ring([p[:] for p in obs], "d_ob")
        o1n = sb("d_o1n", [128, NT, 128], F32)
        ytok = sb("d_ytok", [128, NT, 512], BF16)
        rec = sb("d_rec", [128, 1], F32)
        osb = sb("d_osb", [128, 128], F32)
        junk = sb("d_junk", [128, 128], F32)
        ssN = sb("d_ssN", [128, NT], F32)
        stg = sb("d_stg", [128, 4, 512], BF16)
        for h in range(4):
            def bias_fn(st, qc, h=h):
                dlt = st - 4 * qc
                if -1 <= dlt <= 4:
                    return ("band", C.W2b[h][:, 512 - 128 * dlt:1024 - 128 * dlt], ("W2b", h))
                if dlt > 4:
                    return ("const", C.rbB[:, 31 * 4 + h:31 * 4 + h + 1], "rbB", 31)
                return ("const", C.rbB[:, 15 * 4 + h:15 * 4 + h + 1], "rbB", 15)
            for m in range(2):
                base = 64 * m
                QTh = QT[base:base + 64, h, :]
                KTh = KT[base:base + 64, h, :]
                if m == 0:
                    def out_cb(qc, qt, ps_ap, ps_key, h=h):
                        t = qc * 4 + qt
                        P.dve(lambda e: e.reciprocal(out=rec[:], in_=ps_ap[:, 128:129]), r=[ps_key], w=["d_rec"])
                        P.dve(lambda e: e.tensor_scalar(out=o1n[:, t, :], in0=ps_ap[:, 0:128], scalar1=rec[:, 0:1], scalar2=None, op0=ALU.mult),
                              r=[ps_key, "d_rec"], w=["d_o1n"])
                else:
                    def out_cb(qc, qt, ps_ap, ps_key, h=h):
                        t = qc * 4 + qt
                        P.dve(lambda e: e.reciprocal(out=rec[:], in_=ps_ap[:, 128:129]), r=[ps_key], w=["d_rec"])
                        P.dve(lambda e: e.tensor_tensor(out=rec[:], in0=rec[:], in1=nlam[:], op=ALU.mult), r=["d_rec", "d_nlam"], w=["d_rec"])
                        P.dve(lambda e: e.scalar_tensor_tensor(out=o1n[:, t, :], in0=ps_ap[:, 0:128], scalar=rec[:, 0:1], in1=o1n[:, t, :],
                                                               op0=ALU.mult, op1=ALU.add), r=[ps_key, "d_rec", "d_o1n"], w=["d_o1n"])
                attention(C, QTh, KTh, "d_QK", lambda st, h=h: V[:, st, h, :], "d_V", 128, 0.125, ring, out_cb, bias_fn=bias_fn, obufs=obufs)
            for t in range(NT):
                P.act(lambda e, t=t: e.activation(out=junk[:], in_=o1n[:, t, :], func=AF.Square, accum_out=ssN[:, t:t + 1]),
                      r=["d_o1n"], w=["d_junk", "d_ssN"])
            P.act(lambda e: e.activation(out=ssN[:], in_=ssN[:], func=AF.Sqrt, scale=1.0 / 128, bias=C.epsb[:]), r=["d_ssN"], w=["d_ssN"])
            P.dve(lambda e: e.reciprocal(out=ssN[:], in_=ssN[:]), r=["d_ssN"], w=["d_ssN"])
            P.dve(lambda e: e.tensor_tensor(out=o1n[:], in0=o1n[:], in1=ssN[:].unsqueeze(2).to_broadcast([128, NT, 128]), op=ALU.mult),
                  r=["d_o1n", "d_ssN"], w=["d_o1n"])
            P.dve(lambda e, h=h: e.tensor_tensor(out=ytok[:, :, h * 128:(h + 1) * 128], in0=o1n[:],
                                                 in1=gsub[:].unsqueeze(1).to_broadcast([128, NT, 128]), op=ALU.mult),
                  r=["d_o1n", "d_gsub"], w=["d_ytok"])
        store_T(C, ytok, C.ynT[3], 4, "d_ytok", stg, "d_stg")
        P.barrier(C.bar[:])


def store_T(C, ytok, dstT, ncc, ykey, stg, skey):
    P, NT = C.P, C.NT
    for t in range(NT):
        b = t % 2
        pt = C.pst[b]
        for j in range(ncc):
            P.pe(lambda e, pt=pt, j=j, t=t: e.transpose(pt[:, j * 128:(j + 1) * 128], ytok[:, t, j * 128:(j + 1) * 128], C.ident[:]),
                 r=[ykey], w=[("pst", b)])
        tt = t % 4
        P.act(lambda e, pt=pt, tt=tt: e.copy(out=stg[:, 0:ncc, tt * 128:(tt + 1) * 128],
                                             in_=pt[:, 0:ncc * 128].rearrange("p (j c) -> p j c", j=ncc)),
              r=[("pst", b)], w=[skey])
        if tt == 3:
            t0 = (t // 4) * 512
            P.dma("pool", dstT[:, t0:t0 + 512].rearrange("(j p) t -> p j t", p=128), stg[:, 0:ncc, :], r=[skey])


def phase_four(C, layer):
    P, nc, S, NT, NQC, I = C.P, C.nc, C.S, C.NT, C.NQC, C.I
    KC = 256
    with contextlib.ExitStack() as es:
        sb = lambda n, s, d: es.enter_context(nc.sbuf_tensor(_uniq(n), list(s), d))
        GCS = sb("a_GCS", [128, NT, 1024], BF16)
        with contextlib.ExitStack() as es2:
            sb2 = lambda n, s, d: es2.enter_context(nc.sbuf_tensor(_uniq(n), list(s), d))
            uT = sb2("a_uT", [128, 8, S], BF16)
            wA = sb2("a_w", [128, 8, 512], BF16)
            bd = sb2("a_bd", [128, 2, 128], BF16)
            aT = sb2("a_aT", [128, 4, S], BF16)
            load_T(C, uT, C.uT_d, 8, key="a_uT")
            load_w(C, wA[:], I["w_in"][layer, :, 0:512], "a_w")
            P.dma("sp", bd[:], I["bdcs"].rearrange("a p c -> p a c"), w=["a_bd"])
            k = 0
            for blk in range(4):
                for tc in range(NQC):
                    bi = k % 4
                    k += 1
                    ps = C.psb[bi // 2][:, (bi % 2) * 512:(bi % 2) * 512 + 512]
                    pk = ("psb", bi // 2, bi % 2)
                    for j in range(8):
                        P.pe(lambda e, ps=ps, j=j, blk=blk, tc=tc: e.matmul(
                            ps, lhsT=wA[:, j, blk * 128:(blk + 1) * 128], rhs=uT[:, j, tc * 512:(tc + 1) * 512],
                            start=(j == 0), stop=(j == 7)), r=["a_uT", "a_w"], w=[pk])
                    if k % 2 == 0:
                        P.act(lambda e, ps=ps, blk=blk, tc=tc: e.copy(out=aT[:, blk, tc * 512:(tc + 1) * 512], in_=ps), r=[pk], w=["a_aT"])
                    else:
                        P.dve(lambda e, ps=ps, blk=blk, tc=tc: e.tensor_copy(out=aT[:, blk, tc * 512:(tc + 1) * 512], in_=ps), r=[pk], w=["a_aT"])
            for st in range(NT):
                b = st % 2
                ps = C.psb[b]
                for cs in range(2):
                    for cc in range(4):
                        P.pe(lambda e, ps=ps, cs=cs, cc=cc, st=st: e.matmul(
                            ps[:, cs * 512 + cc * 128:cs * 512 + (cc + 1) * 128], lhsT=aT[:, cc, st * 128:(st + 1) * 128],
                            rhs=bd[:, cs, :], start=True, stop=True, skip_group_check=True), r=["a_aT", "a_bd"], w=[("psb", b, cs)])
                if st % 2 == 0:
                    P.act(lambda e, ps=ps, st=st: e.copy(out=GCS[:, st, :], in_=ps[:]), r=[("psb", b, 0), ("psb", b, 1)], w=["a_GCS"])
                else:
                    P.dve(lambda e, ps=ps, st=st: e.tensor_copy(out=GCS[:, st, :], in_=ps[:]), r=[("psb", b, 0), ("psb", b, 1)], w=["a_GCS"])
            P.barrier(C.bar[:])
        DC = [sb("a_DC%d" % i, [128, NT, KC], BF16) for i in range(2)]
        DS = [sb("a_DS%d" % i, [128, NT, KC], BF16) for i in range(2)]
        stg = [sb("a_stg%d" % i, [128, 4, KC], BF16) for i in range(2)]
        for kc in range(S // KC):
            b = kc % 2
            P.dma("sp", DC[b][:], I["dftc"][:, kc * KC:(kc + 1) * KC].rearrange("(t p) k -> p t k", p=128), w=[("a_DC", b)])
            P.dma("sp", DS[b][:], I["dfts"][:, kc * KC:(kc + 1) * KC].rearrange("(t p) k -> p t k", p=128), w=[("a_DS", b)])
            for cc in range(4):
                bi = cc
                ps = C.psb[bi // 2][:, (bi % 2) * 512:(bi % 2) * 512 + KC]
                pk = ("psb", bi // 2, bi % 2)
                for st in range(NT):
                    P.pe(lambda e, ps=ps, st=st, cc=cc, b=b: e.matmul(ps, lhsT=GCS[:, st, cc * 128:(cc + 1) * 128], rhs=DC[b][:, st, :],
                                                                     start=(st == 0), stop=False), r=["a_GCS", ("a_DC", b)], w=[pk])
                for st in range(NT):
                    P.pe(lambda e, ps=ps, st=st, cc=cc, b=b: e.matmul(ps, lhsT=GCS[:, st, 512 + cc * 128:512 + (cc + 1) * 128], rhs=DS[b][:, st, :],
                                                                     start=False, stop=(st == NT - 1)), r=["a_GCS", ("a_DS", b)], w=[pk])
                if cc % 2 == 0:
                    P.act(lambda e, ps=ps, cc=cc, b=b: e.copy(out=stg[b][:, cc, :], in_=ps), r=[pk], w=[("a_stg", b)])
                else:
                    P.dve(lambda e, ps=ps, cc=cc, b=b: e.tensor_copy(out=stg[b][:, cc, :], in_=ps), r=[pk], w=[("a_stg", b)])
            P.dma("pool", C.ynT[0][:, kc * KC:(kc + 1) * KC].rearrange("(j p) t -> p j t", p=128), stg[b][:], r=[("a_stg", b)])
        P.barrier(C.bar[:])


def phase_mlstm(C, layer):
    P, nc, S, NT, NQC, I = C.P, C.nc, C.S, C.NT, C.NQC, C.I
    X4 = NT * 4
    with contextlib.ExitStack() as es:
        sb = lambda n, s, d: es.enter_context(nc.sbuf_tensor(_uniq(n), list(s), d))
        uT = sb("c_uT", [128, 8, S], BF16)
        G = sb("c_G", [128, NT, 16], F32)
        wg = sb("c_wg", [128, 8, 16], BF16)
        biasB = sb("c_bias", [128, 16], F32)
        gC = sb("c_gC", [128, 512], F32)
        eq = [sb("c_eq%d" % d, [128, NT, 4], F32) for d in range(2)]
        ek = [sb("c_ek%d" % d, [128, NT, 4], F32) for d in range(2)]
        ekd = [sb("c_ekd%d" % d, [128, NT, 4], F32) for d in range(2)]
        dec = [sb("c_dec%d" % d, [128, NT, 4], F32) for d in range(2)]
        load_T(C, uT, C.uT_d, 8, key="c_uT")
        load_w(C, wg[:], I["w_in"][layer, :, COL["cg"]:COL["cg"] + 16], "c_wg")
        P.dma("sp", biasB[:], I["mlstm_gate_bias"][layer:layer + 1, :].partition_broadcast(128), w=["c_bias"])
        P.dma("sp", gC[:], I["mlstm_norm_g"][layer:layer + 1, :].partition_broadcast(128), w=["c_gC"])
        psg = C.psb[0][:, 0:NT * 16]
        for t in range(NT):
            for j in range(8):
                P.pe(lambda e, t=t, j=j: e.matmul(psg[:, t * 16:(t + 1) * 16], lhsT=uT[:, j, t * 128:(t + 1) * 128], rhs=wg[:, j, :],
                                                 start=(t == 0 and j == 0), stop=(j == 7), skip_group_check=True),
                     r=["c_uT", "c_wg"], w=[("psb", 0, 0)])
        P.dve(lambda e: e.tensor_tensor(out=G[:], in0=psg.rearrange("p (t k) -> p t k", k=16),
                                        in1=biasB[:].unsqueeze(1).to_broadcast([128, NT, 16]), op=ALU.add),
              r=[("psb", 0, 0), "c_bias"], w=["c_G"])
        with contextlib.ExitStack() as es2:
            sb2 = lambda n, s, d: es2.enter_context(nc.sbuf_tensor(_uniq(n), list(s), d))
            af = sb2("c_af", [128, NT, 4], F32)
            lf = sb2("c_lf", [128, NT, 4], F32)
            mn = sb2("c_mn", [128, NT, 4], F32)
            tm = sb2("c_tm", [128, NT, 4], F32)
            lfh = [sb2("c_lfh%d" % i, [128, NT, 4], BF16) for i in range(3)]
            for d in range(2):
                Fd = G[:, :, 4 + 8 * d:8 + 8 * d]
                Id = G[:, :, 8 * d:8 * d + 4]
                P.act(lambda e, Fd=Fd: e.activation(out=af[:], in_=Fd, func=AF.Abs), r=["c_G"], w=["c_af"])
                P.act(lambda e: e.activation(out=af[:], in_=af[:], func=AF.Exp, scale=-1.0), r=["c_af"], w=["c_af"])
                P.act(lambda e: e.activation(out=af[:], in_=af[:], func=AF.Ln, bias=C.onesf[:, 0:1]), r=["c_af"], w=["c_af"])
                P.dve(lambda e, Fd=Fd: e.tensor_single_scalar(out=mn[:], in_=Fd, scalar=0.0, op=ALU.min), r=["c_G"], w=["c_mn"])
                P.dve(lambda e: e.tensor_tensor(out=lf[:], in0=mn[:], in1=af[:], op=ALU.subtract), r=["c_mn", "c_af"], w=["c_lf"])
                tri = C.maskLb if d == 0 else C.maskUb
                pb = C.psb[1][:, 0:X4]
                pt_ = C.psb[1][:, 512:512 + X4]
                for sp in range(3):
                    P.dve(lambda e, sp=sp: e.tensor_copy(out=lfh[sp][:], in_=lf[:]), r=["c_lf"], w=[("c_lfh", sp)])
                    if sp < 2:
                        P.dve(lambda e, sp=sp: e.tensor_copy(out=mn[:], in_=lfh[sp][:]), r=[("c_lfh", sp)], w=["c_mn"])
                        P.dve(lambda e: e.tensor_tensor(out=lf[:], in0=lf[:], in1=mn[:], op=ALU.subtract), r=["c_lf", "c_mn"], w=["c_lf"])
                for sp in range(3):
                    l2 = lfh[sp][:].rearrange("p t h -> p (t h)")
                    P.pe(lambda e, tri=tri, pb=pb, l2=l2, sp=sp: e.matmul(pb, lhsT=tri[:], rhs=l2, start=(sp == 0), stop=(sp == 2)),
                         r=[("c_lfh", sp)], w=[("psb", 1, 0)])
                for sp in range(3):
                    l2 = lfh[sp][:].rearrange("p t h -> p (t h)")
                    P.pe(lambda e, pt_=pt_, l2=l2, sp=sp: e.matmul(pt_, lhsT=C.onesb[:], rhs=l2, start=(sp == 0), stop=(sp == 2)),
                         r=[("c_lfh", sp)], w=[("psb", 1, 1)])
                pb3 = pb.rearrange("p (t h) -> p t h", h=4)
                pt3 = pt_.rearrange("p (t h) -> p t h", h=4)
                P.act(lambda e, d=d, pb3=pb3: e.activation(out=eq[d][:], in_=pb3, func=AF.Exp), r=[("psb", 1, 0)], w=[("c_eq", d)])
                P.dve(lambda e, Id=Id, pb3=pb3: e.tensor_tensor(out=tm[:], in0=Id, in1=pb3, op=ALU.subtract), r=["c_G", ("psb", 1, 0)], w=["c_tm", ("psb", 1, 0)])
                P.act(lambda e, d=d: e.activation(out=ek[d][:], in_=tm[:], func=AF.Exp, bias=C.lnk[:]), r=["c_tm"], w=[("c_ek", d)])
                P.act(lambda e, d=d, pt3=pt3: e.activation(out=dec[d][:], in_=pt3, func=AF.Exp), r=[("psb", 1, 1)], w=[("c_dec", d)])
                P.dve(lambda e, d=d: e.tensor_tensor(out=ekd[d][:], in0=ek[d][:], in1=dec[d][:], op=ALU.mult),
                      r=[("c_ek", d), ("c_dec", d)], w=[("c_ekd", d)])
            P.barrier(C.bar[:])
        import os
        MSTOP = int(os.environ.get("MSTOP", "9"))
        if MSTOP == 1:
            return
        wh = sb("c_wh", [128, 8, 512], BF16)
        T4 = sb("c_T4", [128, 4, S], BF16)
        KD = sb("c_KD", [128, NT, 2, 128], BF16)
        V = sb("c_V", [128, NT, 129], BF16)
        SO = sb("c_SO", [128, NT, 128], BF16)
        HS = sb("c_HS", [128, NT, 128], F32)
        sc4 = [sb("c_sc4%d" % i, [128, 4, 128], BF16) for i in range(2)]
        Cst = [sb("c_Cst%d" % d, [128, 129], F32) for d in range(2)]
        Cbf = [sb("c_Cbf%d" % d, [128, 129], BF16) for d in range(2)]
        STm = [sb("c_STm%d" % d, [128, 128], BF16) for d in range(2)]
        den = [sb("c_den%d" % d, [128, 1], F32) for d in range(2)]
        jk = sb("c_jk", [128, 128], F32)
        ssb = sb("c_ss", [128, NT], F32)
        ytk = sb("c_ytk", [128, NT, 128], BF16)
        P.dve(lambda e: e.memset(V[:], 1.0), w=["c_V"])
        for h in range(4):
            for i, nm in enumerate(("cq", "ck", "cv", "co")):
                P.dma("pool", wh[:, :, i * 128:(i + 1) * 128],
                      I["w_in"][layer, :, COL[nm] + h * 128:COL[nm] + (h + 1) * 128].rearrange("(j p) n -> p j n", p=128), w=[("c_wh", i)])
            def ml_a(t, h=h):
                b = t % 2
                ps = C.psb[b][:, 0:512]
                pk = ("psb", b, 0)
                for j in range(8):
                    P.pe(lambda e, ps=ps, j=j, t=t: e.matmul(ps, lhsT=uT[:, j, t * 128:(t + 1) * 128], rhs=wh[:, j, :],
                                                            start=(j == 0), stop=(j == 7)), r=["c_uT"] + [("c_wh", i4) for i4 in range(4)], w=[pk])
            def ml_b(t, h=h):
                b = t % 2
                ps = C.psb[b][:, 0:512]
                pk = ("psb", b, 0)
                s4 = sc4[b]
                MSKIP = os.environ.get("MSKIP", "")
                for wi, (src, sc) in enumerate(((0, eq[0]), (0, eq[1]), (1, ek[0]), (1, ek[1]))):
                    if "a" in MSKIP:
                        break
                    P.dve(lambda e, ps=ps, s4=s4, wi=wi, src=src, sc=sc, t=t, h=h: e.tensor_scalar(
                        out=s4[:, wi, :], in0=ps[:, src * 128:(src + 1) * 128], scalar1=sc[:, t, h:h + 1], scalar2=None, op0=ALU.mult),
                        r=[pk, ("c_eq", 0), ("c_eq", 1), ("c_ek", 0), ("c_ek", 1)], w=[("c_sc4", b)])
                for d in range(2):
                    if "b" in MSKIP:
                        break
                    P.dve(lambda e, ps=ps, d=d, t=t, h=h: e.tensor_scalar(
                        out=KD[:, t, d, :], in0=ps[:, 128:256], scalar1=ekd[d][:, t, h:h + 1], scalar2=None, op0=ALU.mult),
                        r=[pk, ("c_ekd", d)], w=["c_KD"])
                if "c" not in MSKIP:
                    P.act(lambda e, ps=ps, t=t: e.copy(out=V[:, t, 0:128], in_=ps[:, 256:384]), r=[pk], w=["c_V", pk])
                if "d" not in MSKIP:
                    P.act(lambda e, ps=ps, t=t: e.activation(out=SO[:, t, :], in_=ps[:, 384:512], func=AF.Sigmoid), r=[pk], w=["c_SO", pk])
                pt = C.pst[b]
                if "e" in MSKIP:
                    return
                for wi in range(4):
                    P.pe(lambda e, pt=pt, wi=wi, s4=s4: e.transpose(pt[:, wi * 128:(wi + 1) * 128], s4[:, wi, :], C.ident[:]),
                         r=[("c_sc4", b)], w=[("pst", b)])
                P.act(lambda e, pt=pt, t=t: e.copy(out=T4[:, :, t * 128:(t + 1) * 128], in_=pt[:, 0:512].rearrange("p (j c) -> p j c", j=4)),
                      r=[("pst", b)], w=["c_T4"])
            ml_a(0)
            for t in range(NT):
                if t + 1 < NT:
                    ml_a(t + 1)
                ml_b(t)
            if MSTOP == 2:
                P.barrier(C.bar[:])
                return
            for d in range(2):
                P.dve(lambda e, d=d: e.memset(Cst[d][:], 0.0), w=[("c_Cst", d)])
                P.dve(lambda e, d=d: e.memset(Cbf[d][:], 0.0), w=[("c_Cbf", d)])
            written = set()
            def chain_vars(step, d):
                c = step if d == 0 else NT - 1 - step
                return dict(c=c, mask=(C.maskL if d == 0 else C.maskU),
                            qsT=T4[:, d, c * 128:(c + 1) * 128], ksT=T4[:, 2 + d, c * 128:(c + 1) * 128],
                            psA=C.psb[0][:, d * 512:d * 512 + 128], psB=C.psb[1][:, d * 512:d * 512 + 129],
                            psC=C.psb[2][:, d * 512:d * 512 + 129], kA=("psb", 0, d), kB=("psb", 1, d), kC=("psb", 2, d))

            def chain_pe1(step, h=h):
                for d in range(2):
                    v = chain_vars(step, d)
                    P.pe(lambda e, v=v: e.matmul(v["psA"], lhsT=v["ksT"], rhs=v["qsT"], start=True, stop=True), r=["c_T4"], w=[v["kA"]])
                    P.pe(lambda e, v=v, d=d: e.matmul(v["psC"], lhsT=KD[:, v["c"], d, :], rhs=V[:, v["c"], :], start=True, stop=True),
                         r=["c_KD", "c_V"], w=[v["kC"]])

            def chain_rest(step, h=h):
                for d in range(2):
                    v = chain_vars(step, d)
                    P.dve(lambda e, v=v, d=d: e.tensor_tensor(out=STm[d][:], in0=v["psA"], in1=v["mask"][:], op=ALU.mult),
                          r=[v["kA"]], w=[("c_STm", d)])
                for d in range(2):
                    v = chain_vars(step, d)
                    P.pe(lambda e, v=v, d=d: e.matmul(v["psB"], lhsT=v["qsT"], rhs=Cbf[d][:], start=True, stop=False),
                         r=["c_T4", ("c_Cbf", d)], w=[v["kB"]])
                    P.pe(lambda e, v=v, d=d: e.matmul(v["psB"], lhsT=STm[d][:], rhs=V[:, v["c"], :], start=False, stop=True),
                         r=[("c_STm", d), "c_V"], w=[v["kB"]])
                for d in range(2):
                    v = chain_vars(step, d)
                    c = v["c"]
                    psB, psC, kB, kC = v["psB"], v["psC"], v["kB"], v["kC"]
                    P.dve(lambda e, psC=psC, d=d, c=c, h=h: e.scalar_tensor_tensor(out=Cst[d][:], in0=Cst[d][:], scalar=dec[d][:, c, h:h + 1],
                                                                               in1=psC, op0=ALU.mult, op1=ALU.add),
                          r=[kC, ("c_Cst", d), ("c_dec", d)], w=[("c_Cst", d)])
                    P.act(lambda e, d=d: e.copy(out=Cbf[d][:], in_=Cst[d][:]), r=[("c_Cst", d), kB], w=[("c_Cbf", d)])
                    P.dve(lambda e, psB=psB, d=d: e.tensor_scalar_max(out=den[d][:], in0=psB[:, 128:129], scalar1=1.0), r=[kB], w=[("c_den", d)])
                    P.dve(lambda e, psB=psB, d=d: e.scalar_tensor_tensor(out=den[d][:], in0=psB[:, 128:129], scalar=-1.0, in1=den[d][:],
                                                                         op0=ALU.mult, op1=ALU.max), r=[kB, ("c_den", d)], w=[("c_den", d)])
                    P.dve(lambda e, d=d: e.reciprocal(out=den[d][:], in_=den[d][:]), r=[("c_den", d)], w=[("c_den", d)])
                    if c in written:
                        P.dve(lambda e, psB=psB, d=d, c=c: e.scalar_tensor_tensor(out=HS[:, c, :], in0=psB[:, 0:128], scalar=den[d][:, 0:1],
                                                                                 in1=HS[:, c, :], op0=ALU.mult, op1=ALU.add),
                              r=[kB, ("c_den", d), "c_HS"], w=["c_HS"])
                    else:
                        written.add(c)
                        P.dve(lambda e, psB=psB, d=d, c=c: e.tensor_scalar(out=HS[:, c, :], in0=psB[:, 0:128], scalar1=den[d][:, 0:1],
                                                                          scalar2=None, op0=ALU.mult), r=[kB, ("c_den", d)], w=["c_HS"])

            chain_pe1(0)
            for step in range(NT):
                chain_rest(step)
                if step + 1 < NT:
                    chain_pe1(step + 1)
            if MSTOP == 3:
                P.barrier(C.bar[:])
                return
            for t in range(NT):
                P.act(lambda e, t=t: e.activation(out=jk[:], in_=HS[:, t, :], func=AF.Square, accum_out=ssb[:, t:t + 1]),
                      r=["c_HS"], w=["c_jk", "c_ss"])
            P.act(lambda e: e.activation(out=ssb[:], in_=ssb[:], func=AF.Sqrt, scale=1.0 / 128, bias=C.epsb[:]), r=["c_ss"], w=["c_ss"])
            P.dve(lambda e: e.reciprocal(out=ssb[:], in_=ssb[:]), r=["c_ss"], w=["c_ss"])
            P.dve(lambda e: e.tensor_tensor(out=HS[:], in0=HS[:], in1=ssb[:].unsqueeze(2).to_broadcast([128, NT, 128]), op=ALU.mult),
                  r=["c_HS", "c_ss"], w=["c_HS"])
            P.dve(lambda e, h=h: e.tensor_tensor(out=HS[:], in0=HS[:], in1=gC[:, h * 128:(h + 1) * 128].unsqueeze(1).to_broadcast([128, NT, 128]),
                                                 op=ALU.mult), r=["c_HS", "c_gC"], w=["c_HS"])
            P.dve(lambda e: e.tensor_tensor(out=ytk[:], in0=HS[:], in1=SO[:], op=ALU.mult), r=["c_HS", "c_SO"], w=["c_ytk"])
            stgT = SO[:].rearrange("p t d -> p (t d)")
            for t in range(NT):
                b = (t // 8) % 2
                pt = C.pst[b]
                P.pe(lambda e, pt=pt, t=t: e.transpose(pt[:, (t % 8) * 128:(t % 8 + 1) * 128], ytk[:, t, :], C.ident[:]),
                     r=["c_ytk"], w=[("pst", b)])
                if t % 8 == 7 or t == NT - 1:
                    n8 = t % 8 + 1
                    t0 = (t // 8) * 8
                    P.act(lambda e, pt=pt, n8=n8, t0=t0: e.copy(out=stgT[:, t0 * 128:(t0 + n8) * 128], in_=pt[:, 0:n8 * 128]),
                          r=[("pst", b)], w=["c_SO"])
            P.dma("pool", C.ynT[2][h * 128:(h + 1) * 128, :], stgT, r=["c_SO"])
        P.barrier(C.bar[:])


def phase_merge(C, layer, xsrc):
    P, nc, S, NT, NQC, I = C.P, C.nc, C.S, C.NT, C.NQC, C.I
    with contextlib.ExitStack() as es:
        sb = lambda n, s, d: es.enter_context(nc.sbuf_tensor(_uniq(n), list(s), d))
        Wg = sb("m_Wg", [128, 8, 4096], BF16)
        Wbr = sb("m_Wbr", [128, 16, 1024], BF16)
        Wo = sb("m_Wo", [128, 8, 1024], BF16)
        uTc = [sb("m_uT%d" % i, [128, 8, 512], BF16) for i in range(1)] * 2
        yc = [sb("m_y%d" % i, [128, 16, 512], BF16) for i in range(1)] * 2
        mT = sb("m_mT", [128, 8, 512], BF16)
        sig = [sb("m_sig%d" % i, [128, 512], F32) for i in range(2)]
        acc = sb("m_acc", [128, 512], F32)
        tmp = sb("m_tmp", [128, 512], F32)
        xt = [sb("m_xt%d" % i, [128, 1024], F32) for i in range(2)]
        for n in range(4):
            P.dma("pool", Wg[:, :, n * 1024:(n + 1) * 1024],
                  I["w_in"][layer, :, COL["g"] + n * 1024:COL["g"] + (n + 1) * 1024].rearrange("(j p) n -> p j n", p=128), w=[("m_Wg", n)])
            P.dma("pool", Wbr[:, n * 4:(n + 1) * 4, :], I["w_branch"][layer, n].rearrange("(j p) n -> p j n", p=128), w=[("m_Wbr", n)])
        P.dma("pool", Wo[:], I["w_out"][layer].rearrange("(j p) n -> p j n", p=128), w=["m_Wo"])
        k = 0
        for tc in range(NQC):
            b = 0
            P.dma("sp", uTc[b][:], C.uT_d[:, tc * 512:(tc + 1) * 512].rearrange("(j p) t -> p j t", p=128), w=[("m_uT", b)])
            for n in range(4):
                P.dma("sp", yc[b][:, n * 4:(n + 1) * 4, :], C.ynT[n][:, tc * 512:(tc + 1) * 512].rearrange("(j p) t -> p j t", p=128),
                      w=[("m_y", b, n)])
            for j in range(8):
                for n in range(4):
                    pi = k % 2
                    k += 1
                    psG = C.psb[pi][:, 0:512]
                    psR = C.psb[pi][:, 512:1024]
                    kG, kR = ("psb", pi, 0), ("psb", pi, 1)
                    for dj in range(8):
                        P.pe(lambda e, psG=psG, dj=dj, n=n, j=j, b=b: e.matmul(
                            psG, lhsT=Wg[:, dj, n * 1024 + j * 128:n * 1024 + (j + 1) * 128], rhs=uTc[b][:, dj, :],
                            start=(dj == 0), stop=(dj == 7)), r=[("m_Wg", n), ("m_uT", b)], w=[kG])
                    for cc in range(4):
                        P.pe(lambda e, psR=psR, cc=cc, n=n, j=j, b=b: e.matmul(
                            psR, lhsT=Wbr[:, n * 4 + cc, j * 128:(j + 1) * 128], rhs=yc[b][:, n * 4 + cc, :],
                            start=(cc == 0), stop=(cc == 3)), r=[("m_Wbr", n), ("m_y", b, n)], w=[kR])
                    sg = sig[pi]
                    P.act(lambda e, sg=sg, psG=psG: e.activation(out=sg[:], in_=psG, func=AF.Sigmoid), r=[kG], w=[("m_sig", pi)])
                    if n == 0:
                        P.dve(lambda e, sg=sg, psR=psR: e.tensor_tensor(out=acc[:], in0=sg[:], in1=psR, op=ALU.mult),
                              r=[("m_sig", pi), kR], w=["m_acc"])
                    elif n < 3:
                        P.dve(lambda e, sg=sg, psR=psR: e.tensor_tensor(out=tmp[:], in0=sg[:], in1=psR, op=ALU.mult),
                              r=[("m_sig", pi), kR], w=["m_tmp"])
                        P.dve(lambda e: e.tensor_tensor(out=acc[:], in0=acc[:], in1=tmp[:], op=ALU.add), r=["m_acc", "m_tmp"], w=["m_acc"])
                    else:
                        P.dve(lambda e, sg=sg, psR=psR: e.tensor_tensor(out=tmp[:], in0=sg[:], in1=psR, op=ALU.mult),
                              r=[("m_sig", pi), kR], w=["m_tmp"])
                        P.dve(lambda e, j=j: e.tensor_tensor(out=mT[:, j, :], in0=acc[:], in1=tmp[:], op=ALU.add),
                              r=["m_acc", "m_tmp"], w=["m_mT"])
            for tt in range(4):
                t = tc * 4 + tt
                xb = t % 2
                P.dma("sp", xt[xb][:], xsrc[t * 128:(t + 1) * 128, :], w=[("m_xt", xb)])
                ps = C.psb[2]
                for half in range(2):
                    for dj in range(8):
                        P.pe(lambda e, ps=ps, half=half, dj=dj, tt=tt: e.matmul(
                            ps[:, half * 512:(half + 1) * 512], lhsT=mT[:, dj, tt * 128:(tt + 1) * 128], rhs=Wo[:, dj, half * 512:(half + 1) * 512],
                            start=(dj == 0), stop=(dj == 7)), r=["m_mT", "m_Wo"], w=[("psb", 2, half)])
                P.dve(lambda e, ps=ps, xb=xb: e.tensor_tensor(out=xt[xb][:], in0=xt[xb][:], in1=ps[:], op=ALU.add),
                      r=[("m_xt", xb), ("psb", 2, 0), ("psb", 2, 1)], w=[("m_xt", xb)])
                P.dma("pool", C.xres[t * 128:(t + 1) * 128, :], xt[xb][:], r=[("m_xt", xb)])
        P.barrier(C.bar[:])


def phase_ffn(C, layer):
    P, nc, S, NT, NQC, I = C.P, C.nc, C.S, C.NT, C.NQC, C.I
    TB = min(1024, S)
    NCC = DFF // 128
    NH = NCC // 2
    NB = S // TB
    with contextlib.ExitStack() as es:
        sb = lambda n, s, d: es.enter_context(nc.sbuf_tensor(_uniq(n), list(s), d))
        WuA = sb("f_WuA", [128, 8, NH * 128], BF16)
        WuL = sb("f_WuL", [128, 8, NH * 128], BF16)
        Wd = sb("f_Wd", [128, NH, 1024], BF16)
        cw = sb("f_cw", [128, NCC, 3], F32)
        cb = sb("f_cb", [128, NCC], F32)
        hT = sb("f_hT", [128, NH, TB], BF16)
        vTc = [sb("f_vT%d" % i, [128, 8, TB + 2], BF16) for i in range(2)]
        aS = [sb("f_aS%d" % i, [128, TB + 2], F32) for i in range(2)]
        t1 = [sb("f_t1%d" % i, [128, TB], F32) for i in range(2)]
        z2 = [sb("f_z2%d" % i, [128, TB], F32) for i in range(2)]
        sg = [sb("f_sg%d" % i, [128, TB], F32) for i in range(2)]
        xt = [sb("f_xt%d" % i, [128, 1024], F32) for i in range(2)]
        for j in range(3):
            P.dma("sp", cw[:, :, j:j + 1], I["conv_w"][layer, j:j + 1, :].rearrange("o (c p) -> p c o", p=128), w=["f_cw"],
                  allow_slow_non_contiguous=True)
        P.dma("sp", cb[:].unsqueeze(2), I["conv_b"][layer:layer + 1, :].rearrange("o (c p) -> p c o", p=128), w=["f_cb"],
              allow_slow_non_contiguous=True)
        k = 0
        kv = 0
        for hp in range(2):
            c_lo = hp * NH
            for q4 in range(0, NH * 128, 512):
                n = min(512, NH * 128 - q4)
                P.dma("pool", WuA[:, :, q4:q4 + n],
                      I["w_up"][layer, :, c_lo * 128 + q4:c_lo * 128 + q4 + n].rearrange("(j p) n -> p j n", p=128), w=[("f_WuA", q4 // 512)])
                P.dma("pool", WuL[:, :, q4:q4 + n],
                      I["w_up"][layer, :, DFF + c_lo * 128 + q4:DFF + c_lo * 128 + q4 + n].rearrange("(j p) n -> p j n", p=128), w=[("f_WuL", q4 // 512)])
            P.dma("pool", Wd[:], I["w_down"][layer, c_lo * 128:(c_lo + NH) * 128, :].rearrange("(j p) n -> p j n", p=128), w=["f_Wd"])
            for blk in range(NB):
                t0 = blk * TB
                vb = kv % 2
                kv += 1
                vt = vTc[vb]
                lo = max(t0 - 1, 0)
                hi = min(t0 + TB + 1, S)
                c0 = lo - (t0 - 1)
                if t0 == 0:
                    P.dve(lambda e, vt=vt: e.memset(vt[:, :, 0:1], 0.0), w=[("f_vT", vb)])
                if t0 + TB == S:
                    P.dve(lambda e, vt=vt: e.memset(vt[:, :, TB + 1:TB + 2], 0.0), w=[("f_vT", vb)])
                P.dma("sp", vt[:, :, c0:c0 + (hi - lo)], C.uT_d[:, lo:hi].rearrange("(j p) t -> p j t", p=128), w=[("f_vT", vb)])
                pieces = [(0, 512), (512, 512), (1024, 2)] if TB == 1024 else [(0, 512), (512, 2)]
                def stage1(ci, b):
                    cc = c_lo + ci
                    a = aS[b]
                    for pi, (p0, n) in enumerate(pieces):
                        bi = pi % 3
                        ps = C.psb[bi // 2][:, (bi % 2) * 512:(bi % 2) * 512 + n]
                        pk = ("psb", bi // 2, bi % 2)
                        for j in range(8):
                            P.pe(lambda e, ps=ps, j=j, p0=p0, n=n, ci=ci, vt=vt: e.matmul(
                                ps, lhsT=WuA[:, j, ci * 128:(ci + 1) * 128], rhs=vt[:, j, p0:p0 + n],
                                start=(j == 0), stop=(j == 7)), r=[("f_WuA", ci // 4), ("f_vT", vb)], w=[pk])
                        P.act(lambda e, ps=ps, a=a, p0=p0, n=n: e.copy(out=a[:, p0:p0 + n], in_=ps), r=[pk], w=[("f_aS", b)])
                    T1, Z2 = t1[b], z2[b]
                    P.dve(lambda e, a=a, cc=cc, T1=T1: e.tensor_scalar(out=T1[:], in0=a[:, 1:TB + 1], scalar1=cw[:, cc, 1:2], scalar2=cb[:, cc:cc + 1],
                                                                       op0=ALU.mult, op1=ALU.add), r=[("f_aS", b), "f_cw", "f_cb"], w=[("f_t1", b)])
                    P.dve(lambda e, a=a, cc=cc, T1=T1: e.scalar_tensor_tensor(out=T1[:], in0=a[:, 0:TB], scalar=cw[:, cc, 0:1], in1=T1[:],
                                                                              op0=ALU.mult, op1=ALU.add), r=[("f_aS", b), "f_cw", ("f_t1", b)], w=[("f_t1", b)])
                    P.dve(lambda e, a=a, cc=cc, T1=T1: e.scalar_tensor_tensor(out=T1[:], in0=a[:, 2:TB + 2], scalar=cw[:, cc, 2:3], in1=T1[:],
                                                                              op0=ALU.mult, op1=ALU.add), r=[("f_aS", b), "f_cw", ("f_t1", b)], w=[("f_t1", b)])
                    P.pool(lambda e, T1=T1, Z2=Z2: e.tensor_tensor(out=Z2[:], in0=T1[:], in1=T1[:], op=ALU.mult), r=[("f_t1", b)], w=[("f_z2", b)])
                    P.pool(lambda e, Z2=Z2: e.tensor_scalar(out=Z2[:], in0=Z2[:], scalar1=0.044715, scalar2=1.0, op0=ALU.mult, op1=ALU.add),
                           r=[("f_z2", b)], w=[("f_z2", b)])
                    P.pool(lambda e, T1=T1, Z2=Z2: e.tensor_tensor(out=Z2[:], in0=Z2[:], in1=T1[:], op=ALU.mult), r=[("f_z2", b), ("f_t1", b)], w=[("f_z2", b)])

                def stage2(ci, b):
                    T1, Z2, SG = t1[b], z2[b], sg[b]
                    P.act(lambda e, Z2=Z2, SG=SG: e.activation(out=SG[:], in_=Z2[:], func=AF.Sigmoid, scale=1.5957691216057308),
                          r=[("f_z2", b)], w=[("f_sg", b)])
                    P.dve(lambda e, T1=T1, SG=SG: e.tensor_tensor(out=SG[:], in0=SG[:], in1=T1[:], op=ALU.mult), r=[("f_sg", b), ("f_t1", b)], w=[("f_sg", b)])
                    for li in range(TB // 512):
                        bi = 3 + (li % 2)
                        ps = C.psb[bi // 2][:, (bi % 2) * 512:(bi % 2) * 512 + 512]
                        pk = ("psb", bi // 2, bi % 2)
                        for j in range(8):
                            P.pe(lambda e, ps=ps, j=j, li=li, ci=ci, vt=vt: e.matmul(
                                ps, lhsT=WuL[:, j, ci * 128:(ci + 1) * 128], rhs=vt[:, j, 1 + li * 512:1 + (li + 1) * 512],
                                start=(j == 0), stop=(j == 7)), r=[("f_WuL", ci // 4), ("f_vT", vb)], w=[pk])
                        P.dve(lambda e, ps=ps, li=li, ci=ci, SG=SG: e.tensor_tensor(out=hT[:, ci, li * 512:(li + 1) * 512], in0=SG[:, li * 512:(li + 1) * 512],
                                                                                    in1=ps, op=ALU.mult), r=[("f_sg", b), pk], w=["f_hT"])

                bs = []
                for ci in range(NH):
                    bs.append(k % 2)
                    k += 1
                stage1(0, bs[0])
                for ci in range(NH):
                    if ci + 1 < NH:
                        stage1(ci + 1, bs[ci + 1])
                    stage2(ci, bs[ci])
                for tt in range(TB // 128):
                    t = t0 // 128 + tt
                    xb = t % 2
                    P.dma("sp", xt[xb][:], C.xres[t * 128:(t + 1) * 128, :], w=[("f_xt", xb)])
                    ps = C.psb[2]
                    for half in range(2):
                        for ci in range(NH):
                            P.pe(lambda e, ps=ps, half=half, ci=ci, tt=tt: e.matmul(
                                ps[:, half * 512:(half + 1) * 512], lhsT=hT[:, ci, tt * 128:(tt + 1) * 128], rhs=Wd[:, ci, half * 512:(half + 1) * 512],
                                start=(ci == 0), stop=(ci == NH - 1)), r=["f_hT", "f_Wd"], w=[("psb", 2, half)])
                    P.dve(lambda e, ps=ps, xb=xb: e.tensor_tensor(out=xt[xb][:], in0=xt[xb][:], in1=ps[:], op=ALU.add),
                          r=[("f_xt", xb), ("psb", 2, 0), ("psb", 2, 1)], w=[("f_xt", xb)])
                    P.dma("pool", C.xres[t * 128:(t + 1) * 128, :], xt[xb][:], r=[("f_xt", xb)])
        P.barrier(C.bar[:])


def phase_out(C, dst):
    P, nc, S, NT, I = C.P, C.nc, C.S, C.NT, C.I
    last = []
    with contextlib.ExitStack() as es:
        sb = lambda n, s, d: es.enter_context(nc.sbuf_tensor(_uniq(n), list(s), d))
        gB = sb("o_gB", [128, D], F32)
        xt = [sb("o_xt%d" % i, [128, D], F32) for i in range(2)]
        yo = [sb("o_y%d" % i, [128, D], F32) for i in range(2)]
        junk = sb("o_junk", [128, D], F32)
        ssq = [sb("o_ssq%d" % i, [128, 1], F32) for i in range(2)]
        P.dma("sp", gB[:], I["final_norm_g"].partition_broadcast(128), w=["o_gB"])
        for t in range(NT):
            b = t % 2
            P.dma("sp", xt[b][:], C.xres[t * 128:(t + 1) * 128, :], w=[("o_xt", b)])
            P.act(lambda e, b=b: e.activation(out=junk[:], in_=xt[b][:], func=AF.Square, accum_out=ssq[b][:]),
                  r=[("o_xt", b)], w=["o_junk", ("o_ssq", b)])
            P.act(lambda e, b=b: e.activation(out=ssq[b][:], in_=ssq[b][:], func=AF.Sqrt, scale=1.0 / D, bias=C.epsb[:]),
                  r=[("o_ssq", b)], w=[("o_ssq", b)])
            P.dve(lambda e, b=b: e.reciprocal(out=ssq[b][:], in_=ssq[b][:]), r=[("o_ssq", b)], w=[("o_ssq", b)])
            P.dve(lambda e, b=b: e.scalar_tensor_tensor(out=yo[b][:], in0=xt[b][:], scalar=ssq[b][:, 0:1], in1=gB[:],
                                                        op0=ALU.mult, op1=ALU.mult),
                  r=[("o_xt", b), ("o_ssq", b), "o_gB"], w=[("o_y", b)])
            last.append(P.dma("pool", dst[t * 128:(t + 1) * 128, :], yo[b][:], r=[("o_y", b)]))
        P.barrier(C.bar[:])
    return last


_CACHE = {}


def kernel(**inputs):
    S = SEQ
    inp = {k: np.asarray(v) for k, v in inputs.items()}
    nb = inp["x"].shape[0]
    nc, st = build(S, DEPTH, stage="full")
    consts = host_consts(S)
    in_maps = [make_in_map(inp, inp["x"][b], S, consts) for b in range(nb)]
    res = run_bass_kernel_spmd(nc, in_maps, core_ids=list(range(nb)))
    out = np.stack([np.asarray(r["out"], dtype=np.float32) for r in res.results], axis=0)
    return out


def make_in_map(inp, x, S, consts=None):
    c = consts if consts is not None else host_consts(S)
    f = lambda a: np.ascontiguousarray(np.asarray(a, dtype=np.float32))
    m = {
        "x": f(x),
        "norm_mix_g": f(inp["norm_mix_g"]),
        "w_in": f(inp["w_in"]),
        "mlstm_gate_bias": f(inp["mlstm_gate_bias"]).reshape(DEPTH, 16),
        "qk_norm_g": f(inp["qk_norm_g"]).reshape(DEPTH, 128),
        "mlstm_norm_g": f(inp["mlstm_norm_g"]),
        "diff_lambda": f(inp["diff_lambda"]).reshape(DEPTH, 256),
        "diff_norm_g": f(inp["diff_norm_g"]),
        "rel_bias": f(inp["rel_bias"]).reshape(1, 128),
        "w_branch": f(inp["w_branch"]),
        "w_out": f(inp["w_out"]),
        "norm_ffn_g": f(inp["norm_ffn_g"]),
        "w_up": f(inp["w_up"]),
        "conv_w": f(inp["conv_w"]),
        "conv_b": f(inp["conv_b"]),
        "w_down": f(inp["w_down"]),
        "final_norm_g": f(inp["final_norm_g"]).reshape(1, D),
    }
    m.update(c)
    return m
```

```python
import math
import contextlib
import numpy as np
import ml_dtypes
import concourse.bass as bass
import concourse.mybir as mybir
from concourse.bass_utils import run_bass_kernel_spmd

F32 = mybir.dt.float32
BF16 = mybir.dt.bfloat16
AF = mybir.ActivationFunctionType
ALU = mybir.AluOpType
AX = mybir.AxisListType

D = 1024
DEPTH = 2
BATCH = 4
SEQ = 4096
IN_W = 8976
DFF = 2816
EPS = 1e-6
COL = dict(a=0, bq=512, bk=1024, bv=1152, cq=1280, ck=1792, cv=2304, co=2816, cg=3328,
           dq=3344, dk=3856, dv=4368, g=4880)

COMPUTE = ("pe", "act", "dve", "pool")
NDMASEM = 8


class Op:
    __slots__ = ("eng", "fn", "deps", "signal", "ev", "is_dma", "raw")

    def __init__(self, eng, fn, is_dma=False):
        self.eng = eng
        self.fn = fn
        self.deps = set()
        self.raw = set()
        self.signal = False
        self.ev = None
        self.is_dma = is_dma


class Prog:
    def __init__(self, nc, same_engine_sync=True):
        self.nc = nc
        self.ops = []
        self.last_w = {}
        self.readers = {}
        self.same_engine_sync = same_engine_sync
        self.engines = {"pe": nc.tensor, "act": nc.scalar, "dve": nc.vector,
                        "pool": nc.gpsimd, "sp": nc.sync}
        self.since_barrier = []
        self.barrier_op = None

    def op(self, eng, fn, r=(), w=(), dma=False):
        o = Op(eng, fn, is_dma=dma)
        idx = len(self.ops)
        if self.barrier_op is not None:
            o.deps.add(self.barrier_op)
        for k in r:
            lw = self.last_w.get(k)
            if lw is not None:
                o.deps.add(lw)
                o.raw.add(lw)
        for k in w:
            lw = self.last_w.get(k)
            if lw is not None:
                o.deps.add(lw)
            for rd in self.readers.get(k, ()):
                o.deps.add(rd)
        for k in r:
            self.readers.setdefault(k, []).append(idx)
        for k in w:
            self.last_w[k] = idx
            self.readers[k] = []
        self.ops.append(o)
        self.since_barrier.append(idx)
        return idx

    def barrier(self, scratch):
        prev = list(self.since_barrier)
        o = Op("dve", lambda e: e.memset(scratch, 0.0))
        if self.barrier_op is not None:
            o.deps.add(self.barrier_op)
        o.deps.update(prev)
        idx = len(self.ops)
        self.ops.append(o)
        self.barrier_op = idx
        self.since_barrier = []
        self.last_w = {}
        self.readers = {}
        return idx

    def pe(self, fn, r=(), w=()):
        return self.op("pe", fn, r, w)

    def act(self, fn, r=(), w=()):
        return self.op("act", fn, r, w)

    def dve(self, fn, r=(), w=()):
        return self.op("dve", fn, r, w)

    def pool(self, fn, r=(), w=()):
        return self.op("pool", fn, r, w)

    def dma(self, q, out, in_, r=(), w=(), **kw):
        return self.op(q, lambda e: e.dma_start(out=out, in_=in_, **kw), r, w, dma=True)

    def emit(self, final_wait_ops=()):
        nc = self.nc
        ops = self.ops
        for i, o in enumerate(ops):
            keep = set()
            for d in o.deps:
                p = ops[d]
                if p.is_dma:
                    keep.add(d)
                    continue
                if p.eng == o.eng and not o.is_dma:
                    if o.eng == "pe":
                        continue
                    if not self.same_engine_sync:
                        continue
                    if d not in o.raw:
                        continue
                keep.add(d)
            latest = {}
            keep2 = set()
            for d in keep:
                p = ops[d]
                if p.is_dma:
                    keep2.add(d)
                else:
                    if p.eng not in latest or d > latest[p.eng]:
                        latest[p.eng] = d
            keep2.update(latest.values())
            o.deps = keep2
            for d in keep2:
                ops[d].signal = True
        for d in final_wait_ops:
            ops[d].signal = True
        es = contextlib.ExitStack()
        sems = {}
        for e in COMPUTE:
            sems[e] = es.enter_context(nc.semaphore("s_" + e))
        dsems = {}
        for q in ("sp", "pool", "act"):
            dsems[q] = [es.enter_context(nc.semaphore("d_%s%d" % (q, j))) for j in range(NDMASEM)]
        cnt = {e: 0 for e in COMPUTE}
        dcnt = {q: 0 for q in dsems}
        seen = {e: {} for e in self.engines}
        nwait = 0
        plan = {e: [] for e in self.engines}

        def need(engname, ev, waits):
            nonlocal nwait
            sem, val = ev
            s = seen[engname]
            if s.get(id(sem), 0) >= val:
                return
            s[id(sem)] = val
            waits.append((sem, val))
            nwait += 1

        for i, o in enumerate(ops):
            waits = []
            for d in sorted(o.deps):
                need(o.eng, ops[d].ev, waits)
            if o.is_dma:
                q = o.eng
                j = dcnt[q]
                dcnt[q] += 1
                sem = dsems[q][j % NDMASEM]
                if j >= NDMASEM:
                    need(q, (sem, 16 * (j // NDMASEM)), waits)
                o.ev = (sem, 16 * (j // NDMASEM + 1))
                plan[o.eng].append((waits, o, sem, 16))
            else:
                if o.signal:
                    cnt[o.eng] += 1
                    o.ev = (sems[o.eng], cnt[o.eng])
                    plan[o.eng].append((waits, o, sems[o.eng], 1))
                else:
                    plan[o.eng].append((waits, o, None, 0))
        fw = []
        for d in final_wait_ops:
            need("sp", ops[d].ev, fw)
        plan["sp"].append((fw, None, None, 0))

        def run_engine(name, e):
            for waits, o, sem, inc in plan[name]:
                for (ws, wv) in waits:
                    e.wait_ge(ws, wv)
                if o is None:
                    continue
                ins = o.fn(e)
                if sem is not None:
                    ins.then_inc(sem, inc)

        with nc.Block() as block:
            @block.sync
            def _(e):
                run_engine("sp", e)

            @block.tensor
            def _(e):
                run_engine("pe", e)

            @block.scalar
            def _(e):
                run_engine("act", e)

            @block.vector
            def _(e):
                run_engine("dve", e)

            @block.gpsimd
            def _(e):
                run_engine("pool", e)
        self.stats = dict(n_ops=len(ops), n_wait=nwait, cnt=dict(cnt), dcnt=dict(dcnt))
        es.close()
        return self.stats


_UNIQ = [0]


def _uniq(n):
    _UNIQ[0] += 1
    return "%s_%d" % (n, _UNIQ[0])


class Ring:
    def __init__(self, aps, name):
        self.aps = aps
        self.name = name
        self.i = 0

    def next(self):
        j = self.i % len(self.aps)
        self.i += 1
        return self.aps[j], (self.name, j)


def rel_bucket_np(rel):
    half = 16
    max_exact = 8
    ret = np.where(rel > 0, half, 0)
    n = np.abs(rel)
    nf = np.maximum(n, 1).astype(np.float32)
    large = max_exact + (np.log(nf / np.float32(max_exact)) / np.float32(math.log(128 / max_exact))
                         * np.float32(half - max_exact)).astype(np.int32)
    large = np.minimum(large, half - 1)
    return ret + np.where(n < max_exact, n, large)


def host_consts(S):
    c = {}
    k = np.arange(S, dtype=np.int64)
    ks = (k[:, None] * k[None, :]) % S
    ang = (2.0 * np.pi / S) * ks.astype(np.float64)
    c["dftc"] = np.cos(ang).astype(np.float32).astype(ml_dtypes.bfloat16)
    c["dfts"] = (-np.sin(ang)).astype(np.float32).astype(ml_dtypes.bfloat16)
    j = np.arange(64)
    a64 = 2.0 * np.pi * ((j[:, None] * j[None, :]) % 64) / 64.0
    nrm = 1.0 / math.sqrt(S * 64.0)
    bd = np.zeros((2, 128, 128), np.float64)
    for g in range(2):
        bd[0, g * 64:(g + 1) * 64, g * 64:(g + 1) * 64] = np.cos(a64) * nrm
        bd[1, g * 64:(g + 1) * 64, g * 64:(g + 1) * 64] = np.sin(a64) * nrm
    c["bdcs"] = bd.astype(np.float32).astype(ml_dtypes.bfloat16)
    rows = S // 64
    row_id = np.repeat(np.arange(rows, dtype=np.float32), 64)
    col_id = np.tile(np.arange(64, dtype=np.float32), rows)
    inv = (np.float32(10000.0) ** (-np.arange(16, dtype=np.float32) / np.float32(16))).astype(np.float32)
    angr = np.concatenate([row_id[:, None] * inv, col_id[:, None] * inv], axis=-1).astype(np.float32)
    c["ropec"] = np.cos(angr).astype(np.float32)
    c["ropes"] = np.sin(angr).astype(np.float32)
    i = np.arange(128)[:, None]
    m = np.arange(1152)[None, :]
    c["relf"] = (i - m + 512).astype(np.float32)
    return c


def bias_steps():
    rel = np.arange(-700, 701)
    b = rel_bucket_np(rel)
    seq = [int(b[0])]
    thr = []
    for idx in range(1, len(rel)):
        if b[idx] != b[idx - 1]:
            seq.append(int(b[idx]))
            thr.append(int(rel[idx]))
    return seq, thr


class Ctx:
    pass


def build(S, depth, stage="full", taps=()):
    NT = S // 128
    NQC = S // 512
    nc = bass.Bass("TRN2", target_bir_lowering=False)
    P = Prog(nc)
    C = Ctx()
    C.nc, C.P, C.S, C.NT, C.NQC = nc, P, S, NT, NQC

    def din(name, shape, dt=F32):
        return nc.dram_tensor(name, list(shape), dt, kind="ExternalInput").ap()

    I = {}
    I["x"] = din("x", [S, D])
    I["norm_mix_g"] = din("norm_mix_g", [DEPTH, D])
    I["w_in"] = din("w_in", [DEPTH, D, IN_W])
    I["mlstm_gate_bias"] = din("mlstm_gate_bias", [DEPTH, 16])
    I["qk_norm_g"] = din("qk_norm_g", [DEPTH, 128])
    I["mlstm_norm_g"] = din("mlstm_norm_g", [DEPTH, 512])
    I["diff_lambda"] = din("diff_lambda", [DEPTH, 256])
    I["diff_norm_g"] = din("diff_norm_g", [DEPTH, 128])
    I["rel_bias"] = din("rel_bias", [1, 128])
    I["w_branch"] = din("w_branch", [DEPTH, 4, 512, D])
    I["w_out"] = din("w_out", [DEPTH, D, D])
    I["norm_ffn_g"] = din("norm_ffn_g", [DEPTH, D])
    I["w_up"] = din("w_up", [DEPTH, D, 2 * DFF])
    I["conv_w"] = din("conv_w", [DEPTH, 3, DFF])
    I["conv_b"] = din("conv_b", [DEPTH, DFF])
    I["w_down"] = din("w_down", [DEPTH, DFF, D])
    I["final_norm_g"] = din("final_norm_g", [1, D])
    I["dftc"] = din("dftc", [S, S], BF16)
    I["dfts"] = din("dfts", [S, S], BF16)
    I["bdcs"] = din("bdcs", [2, 128, 128], BF16)
    I["ropec"] = din("ropec", [S, 32])
    I["ropes"] = din("ropes", [S, 32])
    I["relf"] = din("relf", [128, 1152])
    C.I = I
    out = nc.dram_tensor("out", [S, D], F32, kind="ExternalOutput").ap()
    C.tap = {}
    for (nm, shp, dt) in taps:
        C.tap[nm] = nc.dram_tensor("tap_" + nm, list(shp), dt, kind="ExternalOutput").ap()

    def dscr(name, shape, dt=BF16):
        return nc.dram_tensor(name, list(shape), dt).ap()

    C.xres = dscr("xres", [S, D], F32)
    C.uT_d = dscr("uT_d", [D, S])
    C.ynT = [dscr("ynT%d" % n, [512, S]) for n in range(4)]

    ges = contextlib.ExitStack()
    C.ges = ges

    def gsb(name, shape, dt):
        return ges.enter_context(nc.sbuf_tensor(name, list(shape), dt))

    C.psb = [ges.enter_context(nc.psum_tensor("psb%d" % i, [128, 1024], F32)) for i in range(3)]
    C.pst = [ges.enter_context(nc.psum_tensor("pst%d" % i, [128, 1024], BF16)) for i in range(2)]
    C.identf = gsb("identf", [128, 128], F32)
    C.ident = gsb("ident", [128, 128], BF16)
    C.maskL = gsb("maskL", [128, 128], F32)
    C.maskU = gsb("maskU", [128, 128], F32)
    C.onesf = gsb("onesf", [128, 128], F32)
    C.maskLb = gsb("maskLb", [128, 128], BF16)
    C.maskUb = gsb("maskUb", [128, 128], BF16)
    C.onesb = gsb("onesb", [128, 128], BF16)
    C.bar = gsb("bar", [128, 1], F32)
    C.rbB = gsb("rbB", [128, 128], F32)
    C.epsb = gsb("epsb", [128, 1], F32)
    C.lnk = gsb("lnk", [128, 1], F32)

    P.pool(lambda e: e.iota(C.identf[:], [[1, 128]], 0, channel_multiplier=-1,
                            allow_small_or_imprecise_dtypes=True), w=["identf"])
    P.dve(lambda e: e.tensor_single_scalar(out=C.ident[:], in_=C.identf[:], scalar=0.0, op=ALU.is_equal),
          r=["identf"], w=["ident"])
    P.dve(lambda e: e.tensor_single_scalar(out=C.maskL[:], in_=C.identf[:], scalar=0.0, op=ALU.is_ge),
          r=["identf"], w=["maskL"])
    P.dve(lambda e: e.tensor_single_scalar(out=C.maskU[:], in_=C.identf[:], scalar=0.0, op=ALU.is_le),
          r=["identf"], w=["maskU"])
    P.dve(lambda e: e.memset(C.onesf[:], 1.0), w=["onesf"])
    P.dve(lambda e: e.memset(C.onesb[:], 1.0), w=["onesb"])
    P.dve(lambda e: e.tensor_copy(out=C.maskLb[:], in_=C.maskL[:]), r=["maskL"], w=["maskLb"])
    P.dve(lambda e: e.tensor_copy(out=C.maskUb[:], in_=C.maskU[:]), r=["maskU"], w=["maskUb"])
    P.dve(lambda e: e.memset(C.epsb[:], EPS), w=["epsb"])
    P.dve(lambda e: e.memset(C.lnk[:], -0.5 * math.log(128.0)), w=["lnk"])
    P.dma("sp", C.rbB[:], I["rel_bias"].partition_broadcast(128), w=["rbB"])
    P.barrier(C.bar[:])

    setup_bias_tables(C)
    fin = []
    done = False
    for layer in range(depth):
        xsrc = I["x"] if layer == 0 else C.xres
        phase_norm(C, xsrc, I["norm_mix_g"][layer:layer + 1, :], C.uT_d)
        if stage == "norm":
            fin.append(copy_dram(C, C.tap["uT"], C.uT_d, [D, S], BF16)); break
        if stage in ("gqa", "full", "merge", "layer"):
            phase_gqa(C, layer)
        if stage == "gqa":
            fin.append(copy_dram(C, C.tap["ybT"], C.ynT[1], [512, S], BF16)); break
        if stage in ("diff", "full", "merge", "layer"):
            phase_diff(C, layer)
        if stage == "diff":
            fin.append(copy_dram(C, C.tap["ydT"], C.ynT[3], [512, S], BF16)); break
        if stage in ("four", "full", "merge", "layer"):
            phase_four(C, layer)
        if stage == "four":
            fin.append(copy_dram(C, C.tap["yaT"], C.ynT[0], [512, S], BF16)); break
        if stage in ("mlstm", "full", "merge", "layer"):
            phase_mlstm(C, layer)
        if stage == "mlstm":
            fin.append(copy_dram(C, C.tap["ycT"], C.ynT[2], [512, S], BF16)); break
        phase_merge(C, layer, xsrc)
        if stage == "merge":
            fin.append(copy_dram(C, C.tap["xmid"], C.xres, [S, D], F32)); break
        phase_norm(C, C.xres, I["norm_ffn_g"][layer:layer + 1, :], C.uT_d)
        phase_ffn(C, layer)
        if stage == "layer":
            fin.append(copy_dram(C, C.tap["xl"], C.xres, [S, D], F32)); break
    if stage == "full":
        fin.extend(phase_out(C, out))
    st = P.emit(final_wait_ops=fin)
    ges.close()
    return nc, st


def copy_dram(C, dst, src, shape, dt):
    P, nc = C.P, C.nc
    rows, cols = shape
    last = None
    with contextlib.ExitStack() as es:
        t = es.enter_context(nc.sbuf_tensor(_uniq("cpy"), [128, cols], dt))
        for r0 in range(0, rows, 128):
            P.dma("sp", t[:], src[r0:r0 + 128, :], w=["cpy"])
            last = P.dma("sp", dst[r0:r0 + 128, :], t[:], r=["cpy"])
        P.barrier(C.bar[:])
    return last


def phase_norm(C, xsrc, g_row, dstT):
    P, nc, S, NT = C.P, C.nc, C.S, C.NT
    with contextlib.ExitStack() as es:
        sb = lambda n, s, d: es.enter_context(nc.sbuf_tensor(_uniq(n), list(s), d))
        gB = sb("n_gB", [128, D], F32)
        xt = [sb("n_xt%d" % i, [128, D], F32) for i in range(4)]
        junk = sb("n_junk", [128, D], F32)
        ub = [sb("n_ub%d" % i, [128, D], BF16) for i in range(4)]
        ssq = [sb("n_ssq%d" % i, [128, 1], F32) for i in range(4)]
        stg = [sb("n_stg%d" % i, [128, 8, 512], BF16) for i in range(2)]
        P.dma("sp", gB[:], g_row.partition_broadcast(128), w=["n_gB"])
        def stage_a(t):
            b = t % 4
            P.dma("sp", xt[b][:], xsrc[t * 128:(t + 1) * 128, :], w=[("n_xt", b)])
            P.act(lambda e, b=b: e.activation(out=junk[:], in_=xt[b][:], func=AF.Square, accum_out=ssq[b][:]),
                  r=[("n_xt", b)], w=["n_junk", ("n_ssq", b)])
            P.act(lambda e, b=b: e.activation(out=ssq[b][:], in_=ssq[b][:], func=AF.Sqrt, scale=1.0 / D, bias=C.epsb[:]),
                  r=[("n_ssq", b)], w=[("n_ssq", b)])
            P.dve(lambda e, b=b: e.reciprocal(out=ssq[b][:], in_=ssq[b][:]), r=[("n_ssq", b)], w=[("n_ssq", b)])
            P.dve(lambda e, b=b: e.scalar_tensor_tensor(out=ub[b][:], in0=xt[b][:], scalar=ssq[b][:, 0:1], in1=gB[:],
                                                        op0=ALU.mult, op1=ALU.mult),
                  r=[("n_xt", b), ("n_ssq", b), "n_gB"], w=[("n_ub", b)])
            pb_ = t % 2
            pt = C.pst[pb_]
            for j in range(8):
                P.pe(lambda e, b=b, j=j, pt=pt: e.transpose(pt[:, j * 128:(j + 1) * 128], ub[b][:, j * 128:(j + 1) * 128], C.ident[:]),
                     r=[("n_ub", b)], w=[("pst", pb_)])

        def stage_b(t):
            pb_ = t % 2
            pt = C.pst[pb_]
            sgi = (t // 4) % 2
            tt = t % 4
            P.act(lambda e, pt=pt, sgi=sgi, tt=tt: e.copy(out=stg[sgi][:, :, tt * 128:(tt + 1) * 128],
                                                          in_=pt[:].rearrange("p (j c) -> p j c", j=8)),
                  r=[("pst", pb_)], w=[("n_stg", sgi)])
            if tt == 3:
                t0 = (t // 4) * 512
                P.dma("pool", dstT[:, t0:t0 + 512].rearrange("(j p) t -> p j t", p=128), stg[sgi][:],
                      r=[("n_stg", sgi)])

        stage_a(0)
        for t in range(NT):
            if t + 1 < NT:
                stage_a(t + 1)
            stage_b(t)
        P.barrier(C.bar[:])


def load_T(C, dst, srcT, ncc, q="sp", key=None):
    C.P.dma(q, dst[:], srcT.rearrange("(j p) t -> p j t", p=128), w=[key])


def load_w(C, dst, wsrc, key, q="pool"):
    C.P.dma(q, dst, wsrc.rearrange("(j p) n -> p j n", p=128), w=[key])


def attention(C, QT, KT, qk_key, Vaug, v_key, dv, scale, ptile, out_cb, bias_fn=None, obufs=None):
    P, S, NT, NQC = C.P, C.S, C.NT, C.NQC
    NP = NT // 2
    steps = [(qc, sp) for qc in range(NQC) for sp in range(NP)]
    po = C.psb[2]

    def bias_of(st, qc):
        return bias_fn(st, qc) if bias_fn is not None else None

    def issue_qk(i):
        qc, sp = steps[i]
        buf = i % 2
        ps = C.psb[buf]
        for u in range(2):
            st = 2 * sp + u
            psS = ps[:, u * 512:(u + 1) * 512]
            kS = ("psb", buf, u)
            bias = bias_of(st, qc)
            band = bias is not None and bias[0] == "band"
            P.pe(lambda e, psS=psS, st=st, qc=qc, band=band: e.matmul(
                psS, lhsT=KT[:, st * 128:(st + 1) * 128], rhs=QT[:, qc * 512:(qc + 1) * 512],
                start=True, stop=not band), r=[qk_key], w=[kS])
            if band:
                P.pe(lambda e, psS=psS, bt=bias[1]: e.matmul(psS, lhsT=C.ident[:], rhs=bt, start=False, stop=True),
                     r=[bias[2], "ident"], w=[kS])

    def issue_exp(i):
        qc, sp = steps[i]
        buf = i % 2
        ps = C.psb[buf]
        pt, pk = ptile.next()
        b0 = bias_of(2 * sp, qc)
        b1 = bias_of(2 * sp + 1, qc)
        c0 = b0 if (b0 is not None and b0[0] == "const") else None
        c1 = b1 if (b1 is not None and b1[0] == "const") else None
        same = (c0 is None and c1 is None)
        if same:
            if c0 is None:
                P.act(lambda e, pt=pt, ps=ps: e.activation(out=pt[:, 0:1024], in_=ps[:, 0:1024], func=AF.Exp, scale=scale),
                      r=[("psb", buf, 0), ("psb", buf, 1)], w=[pk])
            else:
                P.act(lambda e, pt=pt, ps=ps, bb=c0[1]: e.activation(out=pt[:, 0:1024], in_=ps[:, 0:1024], func=AF.Exp, scale=scale, bias=bb),
                      r=[("psb", buf, 0), ("psb", buf, 1), c0[2]], w=[pk])
        else:
            for u, cb in enumerate((c0, c1)):
                if cb is None:
                    P.act(lambda e, pt=pt, ps=ps, u=u: e.activation(out=pt[:, u * 512:(u + 1) * 512], in_=ps[:, u * 512:(u + 1) * 512],
                                                                   func=AF.Exp, scale=scale), r=[("psb", buf, u)], w=[pk])
                else:
                    P.act(lambda e, pt=pt, ps=ps, u=u, bb=cb[1]: e.activation(out=pt[:, u * 512:(u + 1) * 512], in_=ps[:, u * 512:(u + 1) * 512],
                                                                             func=AF.Exp, scale=scale, bias=bb),
                          r=[("psb", buf, u), cb[2]], w=[pk])
        return pt, pk

    def issue_pv(i, pt, pk):
        qc, sp = steps[i]
        for u in range(2):
            st = 2 * sp + u
            for qt in range(4):
                o_ap = po[:, qt * 256:qt * 256 + dv + 1]
                P.pe(lambda e, o_ap=o_ap, pt=pt, qt=qt, st=st, u=u, sp=sp: e.matmul(
                    o_ap, lhsT=pt[:, u * 512 + qt * 128:u * 512 + (qt + 1) * 128], rhs=Vaug(st),
                    start=(sp == 0 and u == 0 and qt % 2 == 0), stop=(st == NT - 1), skip_group_check=True),
                    r=[pk, v_key], w=[("psb", 2, qt // 2)])
        if sp == NP - 1:
            ob, ok = obufs.next()
            for bk in range(2):
                P.dve(lambda e, ob=ob, bk=bk: e.tensor_copy(
                    out=ob[:, bk * 512:(bk + 1) * 512].rearrange("p (q c) -> p q c", q=2)[:, :, 0:dv + 1],
                    in_=po[:, bk * 512:(bk + 1) * 512].rearrange("p (q c) -> p q c", q=2)[:, :, 0:dv + 1]),
                    r=[("psb", 2, bk)], w=[ok])
            for qt in range(4):
                out_cb(qc, qt, ob[:, qt * 256:qt * 256 + dv + 1], ok)

    n = len(steps)
    issue_qk(0)
    for i in range(n):
        if i + 1 < n:
            issue_qk(i + 1)
        pt, pk = issue_exp(i)
        issue_pv(i, pt, pk)


def phase_gqa(C, layer):
    P, nc, S, NT, NQC, I = C.P, C.nc, C.S, C.NT, C.NQC, C.I
    with contextlib.ExitStack() as es:
        sb = lambda n, s, d: es.enter_context(nc.sbuf_tensor(_uniq(n), list(s), d))
        QT = sb("b_QT", [128, 4, S], BF16)
        KT = sb("b_KT", [128, 2, S], BF16)
        V = sb("b_V", [128, NT, 2, 65], BF16)
        ropec = sb("b_rc", [128, NT, 32], F32)
        ropes = sb("b_rs", [128, NT, 32], F32)
        g640 = sb("b_g", [128, 10, 64], F32)
        gq = sb("b_gq", [128, 128], F32)
        with contextlib.ExitStack() as es2:
            sb2 = lambda n, s, d: es2.enter_context(nc.sbuf_tensor(_uniq(n), list(s), d))
            uT = sb2("b_uT", [128, 8, S], BF16)
            wB = sb2("b_w", [128, 8, 768], BF16)
            sq = sb2("b_sq", [128, 10, 64], F32)
            ss = sb2("b_ss", [128, 10], F32)
            qn = sb2("b_qn", [128, 10, 64], F32)
            t1 = sb2("b_t1", [128, 10, 32], F32)
            t2 = sb2("b_t2", [128, 10, 32], F32)
            qr = [sb2("b_qr%d" % i, [128, 12, 64], BF16) for i in range(2)]
            load_T(C, uT, C.uT_d, 8, key="b_uT")
            load_w(C, wB[:], I["w_in"][layer, :, COL["bq"]:COL["bq"] + 768], "b_w")
            P.dma("sp", ropec[:], I["ropec"].rearrange("(t p) c -> p t c", p=128), w=["b_rc"])
            P.dma("sp", ropes[:], I["ropes"].rearrange("(t p) c -> p t c", p=128), w=["b_rs"])
            P.dma("sp", gq[:], I["qk_norm_g"][layer:layer + 1, :].partition_broadcast(128), w=["b_gq"])
            P.dve(lambda e: e.memset(V[:], 1.0), w=["b_V"])
            P.dve(lambda e: e.tensor_copy(out=g640[:, 0:8, :], in_=gq[:, 0:64].unsqueeze(1).to_broadcast([128, 8, 64])),
                  r=["b_gq"], w=["b_g"])
            P.dve(lambda e: e.tensor_copy(out=g640[:, 8:10, :], in_=gq[:, 64:128].unsqueeze(1).to_broadcast([128, 2, 64])),
                  r=["b_gq", "b_g"], w=["b_g"])
            def gq_a(t):
                b = t % 2
                ps = C.psb[b]
                for half, (c0, n) in enumerate(((0, 512), (512, 256))):
                    for j in range(8):
                        P.pe(lambda e, ps=ps, half=half, c0=c0, n=n, j=j, t=t: e.matmul(
                            ps[:, half * 512:half * 512 + n], lhsT=uT[:, j, t * 128:(t + 1) * 128],
                            rhs=wB[:, j, c0:c0 + n], start=(j == 0), stop=(j == 7)),
                            r=["b_uT", "b_w"], w=[("psb", b, half)])
            def gq_b(t):
                b = t % 2
                ps = C.psb[b]
                kq = [("psb", b, 0), ("psb", b, 1)]
                qk_ps = ps[:, 0:640].rearrange("p (h d) -> p h d", d=64)
                P.act(lambda e, qk_ps=qk_ps: e.activation(out=sq[:], in_=qk_ps, func=AF.Square), r=kq, w=["b_sq"])
                P.dve(lambda e: e.tensor_reduce(out=ss[:], in_=sq[:], axis=AX.X, op=ALU.add), r=["b_sq"], w=["b_ss"])
                P.act(lambda e: e.activation(out=ss[:], in_=ss[:], func=AF.Sqrt, scale=1.0 / 64, bias=C.epsb[:]),
                      r=["b_ss"], w=["b_ss"])
                P.dve(lambda e: e.reciprocal(out=ss[:], in_=ss[:]), r=["b_ss"], w=["b_ss"])
                P.dve(lambda e, qk_ps=qk_ps: e.tensor_tensor(out=qn[:], in0=qk_ps, in1=ss[:].unsqueeze(2).to_broadcast([128, 10, 64]),
                                                              op=ALU.mult), r=kq + ["b_ss"], w=["b_qn"])
                P.dve(lambda e: e.tensor_tensor(out=qn[:], in0=qn[:], in1=g640[:], op=ALU.mult), r=["b_qn", "b_g"], w=["b_qn"])
                cb = ropec[:, t, :].unsqueeze(1).to_broadcast([128, 10, 32])
                sbb = ropes[:, t, :].unsqueeze(1).to_broadcast([128, 10, 32])
                x1 = qn[:, :, 0:32]
                x2 = qn[:, :, 32:64]
                q_out = qr[b]
                P.dve(lambda e, cb=cb: e.tensor_tensor(out=t1[:], in0=x1, in1=cb, op=ALU.mult), r=["b_qn", "b_rc"], w=["b_t1"])
                P.dve(lambda e, sbb=sbb: e.tensor_tensor(out=t2[:], in0=x2, in1=sbb, op=ALU.mult), r=["b_qn", "b_rs"], w=["b_t2"])
                P.dve(lambda e, q_out=q_out: e.tensor_tensor(out=q_out[:, 0:10, 0:32], in0=t1[:], in1=t2[:], op=ALU.subtract),
                      r=["b_t1", "b_t2"], w=[("b_qr", b)])
                P.dve(lambda e, sbb=sbb: e.tensor_tensor(out=t1[:], in0=x1, in1=sbb, op=ALU.mult), r=["b_qn", "b_rs", ("b_qr", b)], w=["b_t1"])
                P.dve(lambda e, cb=cb: e.tensor_tensor(out=t2[:], in0=x2, in1=cb, op=ALU.mult), r=["b_qn", "b_rc", ("b_qr", b)], w=["b_t2"])
                P.dve(lambda e, q_out=q_out: e.tensor_tensor(out=q_out[:, 0:10, 32:64], in0=t1[:], in1=t2[:], op=ALU.add),
                      r=["b_t1", "b_t2"], w=[("b_qr", b)])
                P.dve(lambda e, q_out=q_out: e.tensor_copy(out=q_out[:, 10:12, :], in_=q_out[:, 9:10, :].to_broadcast([128, 2, 64])),
                      r=[("b_qr", b)], w=[("b_qr", b)])
                P.dve(lambda e, q_out=q_out: e.tensor_copy(out=q_out[:, 9:10, :], in_=q_out[:, 8:9, :]),
                      r=[("b_qr", b)], w=[("b_qr", b)])
                pt = C.pst[b]
                for j in range(6):
                    P.pe(lambda e, pt=pt, j=j, q_out=q_out: e.transpose(
                        pt[:, j * 128:(j + 1) * 128], q_out[:, 2 * j:2 * j + 2, :].rearrange("p a d -> p (a d)"), C.ident[:]),
                        r=[("b_qr", b)], w=[("pst", b)])
                P.act(lambda e, pt=pt, t=t: e.copy(out=QT[:, :, t * 128:(t + 1) * 128],
                                                   in_=pt[:, 0:512].rearrange("p (j c) -> p j c", j=4)),
                      r=[("pst", b)], w=["b_QT"])
                P.act(lambda e, pt=pt, t=t: e.copy(out=KT[:, :, t * 128:(t + 1) * 128],
                                                   in_=pt[:, 512:768].rearrange("p (j c) -> p j c", j=2)),
                      r=[("pst", b)], w=["b_KT"])
                P.act(lambda e, ps=ps, t=t: e.copy(out=V[:, t, :, 0:64], in_=ps[:, 640:768].rearrange("p (g d) -> p g d", g=2)),
                      r=kq, w=["b_V"] + kq)
            gq_a(0)
            for t in range(NT):
                if t + 1 < NT:
                    gq_a(t + 1)
                gq_b(t)
            P.barrier(C.bar[:])
        pts = [sb("b_pt%d" % i, [128, 1024], BF16) for i in range(3)]
        ring = Ring([p[:] for p in pts], "b_pt")
        obs = [sb("b_ob%d" % i, [128, 1024], F32) for i in range(2)]
        obufs = Ring([p[:] for p in obs], "b_ob")
        rec = sb("b_rec", [128, 1], F32)
        stg = sb("b_stg", [128, 4, 512], BF16)
        ytokall = sb("b_ytokall", [128, NT, 512], BF16)
        for h in range(8):
            g = h // 4
            base = 64 * (h % 2)
            QTh = QT[base:base + 64, h // 2, :]
            KTh = KT[base:base + 64, g, :]

            def out_cb(qc, qt, ps_ap, ps_key, h=h):
                P.dve(lambda e: e.reciprocal(out=rec[:], in_=ps_ap[:, 64:65]), r=[ps_key], w=["b_rec"])
                P.dve(lambda e: e.tensor_scalar(out=ytokall[:, qc * 4 + qt, h * 64:(h + 1) * 64], in0=ps_ap[:, 0:64],
                                                scalar1=rec[:, 0:1], scalar2=None, op0=ALU.mult),
                      r=[ps_key, "b_rec"], w=["b_ytokall"])
            attention(C, QTh, KTh, "b_QT", lambda st, g=g: V[:, st, g, :], "b_V", 64, 0.125, ring, out_cb, obufs=obufs)
        for t in range(NT):
            b = t % 2
            pt = C.pst[b]
            for j in range(4):
                P.pe(lambda e, pt=pt, j=j, t=t: e.transpose(pt[:, j * 128:(j + 1) * 128], ytokall[:, t, j * 128:(j + 1) * 128], C.ident[:]),
                     r=["b_ytokall"], w=[("pst", b)])
            tt = t % 4
            P.act(lambda e, pt=pt, tt=tt: e.copy(out=stg[:, :, tt * 128:(tt + 1) * 128],
                                                 in_=pt[:, 0:512].rearrange("p (j c) -> p j c", j=4)),
                  r=[("pst", b)], w=["b_stg"])
            if tt == 3:
                t0 = (t // 4) * 512
                P.dma("pool", C.ynT[1][:, t0:t0 + 512].rearrange("(j p) t -> p j t", p=128), stg[:], r=["b_stg"])
        P.barrier(C.bar[:])


def lam_init_of(layer):
    return 0.8 - 0.6 * math.exp(-0.3 * layer)


def setup_bias_tables(C):
    P, nc = C.P, C.nc
    seq, thr = bias_steps()
    C.W2b = [C.ges.enter_context(nc.sbuf_tensor("W2b%d" % h, [128, 1152], BF16)) for h in range(4)]
    with contextlib.ExitStack() as es:
        sb = lambda n, s, d: es.enter_context(nc.sbuf_tensor(_uniq(n), list(s), d))
        relf = sb("s_relf", [128, 1152], F32)
        acc = sb("s_acc", [128, 1152], F32)
        tmp = sb("s_tmp", [128, 1152], F32)
        dB = sb("s_dB", [128, len(thr), 4], F32)
        P.dma("sp", relf[:], C.I["relf"], w=["s_relf"])
        rb3 = C.rbB[:].rearrange("p (b h) -> p b h", h=4)
        for k in range(len(thr)):
            P.dve(lambda e, k=k: e.tensor_tensor(out=dB[:, k, :], in0=rb3[:, seq[k + 1], :], in1=rb3[:, seq[k], :], op=ALU.subtract),
                  r=["rbB"], w=["s_dB"])
        for h in range(4):
            P.dve(lambda e, h=h: e.tensor_scalar(out=acc[:], in0=relf[:], scalar1=0.0, scalar2=C.rbB[:, seq[0] * 4 + h:seq[0] * 4 + h + 1],
                                                 op0=ALU.mult, op1=ALU.add), r=["s_relf", "rbB"], w=["s_acc"])
            for k in range(len(thr)):
                P.dve(lambda e, k=k, h=h: e.tensor_scalar(out=tmp[:], in0=relf[:], scalar1=float(thr[k]), scalar2=dB[:, k, h:h + 1],
                                                          op0=ALU.is_ge, op1=ALU.mult), r=["s_relf", "s_dB"], w=["s_tmp"])
                P.dve(lambda e: e.tensor_tensor(out=acc[:], in0=acc[:], in1=tmp[:], op=ALU.add), r=["s_acc", "s_tmp"], w=["s_acc"])
            P.act(lambda e, h=h: e.activation(out=C.W2b[h][:], in_=acc[:], func=AF.Copy, scale=8.0), r=["s_acc"], w=[("W2b", h)])
        P.barrier(C.bar[:])


def phase_diff(C, layer):
    P, nc, S, NT, NQC, I = C.P, C.nc, C.S, C.NT, C.NQC, C.I
    li = lam_init_of(layer)
    with contextlib.ExitStack() as es:
        sb = lambda n, s, d: es.enter_context(nc.sbuf_tensor(_uniq(n), list(s), d))
        QT = sb("d_QT", [128, 4, S], BF16)
        KT = sb("d_KT", [128, 4, S], BF16)
        V = sb("d_V", [128, NT, 4, 129], BF16)
        with contextlib.ExitStack() as es2:
            sb2 = lambda n, s, d: es2.enter_context(nc.sbuf_tensor(_uniq(n), list(s), d))
            uT = sb2("d_uT", [128, 8, S], BF16)
            wqk = sb2("d_wqk", [128, 8, 1024], BF16)
            load_T(C, uT, C.uT_d, 8, key="d_uT")
            load_w(C, wqk[:], I["w_in"][layer, :, COL["dq"]:COL["dq"] + 1024], "d_wqk")
            P.dve(lambda e: e.memset(V[:], 1.0), w=["d_V"])
            k = 0
            for blk in range(8):
                dst = QT if blk < 4 else KT
                for tc in range(NQC):
                    bi = k % 4
                    k += 1
                    ps = C.psb[bi // 2][:, (bi % 2) * 512:(bi % 2) * 512 + 512]
                    pk = ("psb", bi // 2, bi % 2)
                    for j in range(8):
                        P.pe(lambda e, ps=ps, j=j, blk=blk, tc=tc: e.matmul(
                            ps, lhsT=wqk[:, j, blk * 128:(blk + 1) * 128], rhs=uT[:, j, tc * 512:(tc + 1) * 512],
                            start=(j == 0), stop=(j == 7)), r=["d_uT", "d_wqk"], w=[pk])
                    eng = P.act if k % 2 == 0 else P.dve
                    if k % 2 == 0:
                        P.act(lambda e, ps=ps, dst=dst, blk=blk, tc=tc: e.copy(out=dst[:, blk % 4, tc * 512:(tc + 1) * 512], in_=ps),
                              r=[pk], w=["d_QK"])
                    else:
                        P.dve(lambda e, ps=ps, dst=dst, blk=blk, tc=tc: e.tensor_copy(out=dst[:, blk % 4, tc * 512:(tc + 1) * 512], in_=ps),
                              r=[pk], w=["d_QK"])
            wv = wqk[:, :, 0:512]
            load_w(C, wv, I["w_in"][layer, :, COL["dv"]:COL["dv"] + 512], "d_wqk")
            for t in range(NT):
                bi = t % 4
                ps = C.psb[bi // 2][:, (bi % 2) * 512:(bi % 2) * 512 + 512]
                pk = ("psb", bi // 2, bi % 2)
                for j in range(8):
                    P.pe(lambda e, ps=ps, j=j, t=t: e.matmul(ps, lhsT=uT[:, j, t * 128:(t + 1) * 128], rhs=wv[:, j, :],
                                                            start=(j == 0), stop=(j == 7)), r=["d_uT", "d_wqk"], w=[pk])
                P.act(lambda e, ps=ps, t=t: e.copy(out=V[:, t, :, 0:128], in_=ps.rearrange("p (h d) -> p h d", h=4)),
                      r=[pk], w=["d_V"])
            P.barrier(C.bar[:])
        lpB = sb("d_lp", [128, 256], F32)
        lpr = sb("d_lpr", [128, 128], F32)
        s12 = sb("d_s12", [128, 2], F32)
        nlam = sb("d_nlam", [128, 1], F32)
        gsub = sb("d_gsub", [128, 128], F32)
        P.dma("sp", lpB[:], I["diff_lambda"][layer:layer + 1, :].partition_broadcast(128), w=["d_lp"])
        P.dma("sp", gsub[:], I["diff_norm_g"][layer:layer + 1, :].partition_broadcast(128), w=["d_gsub"])
        lp4 = lpB[:].rearrange("p (a b d) -> p a b d", a=2, b=2)
        P.dve(lambda e: e.tensor_tensor(out=lpr[:].rearrange("p (a d) -> p a d", a=2), in0=lp4[:, :, 0, :], in1=lp4[:, :, 1, :], op=ALU.mult),
              r=["d_lp"], w=["d_lpr"])
        P.dve(lambda e: e.tensor_reduce(out=s12[:], in_=lpr[:].rearrange("p (a d) -> p a d", a=2), axis=AX.X, op=ALU.add),
              r=["d_lpr"], w=["d_s12"])
        P.act(lambda e: e.activation(out=s12[:], in_=s12[:], func=AF.Exp), r=["d_s12"], w=["d_s12"])
        P.dve(lambda e: e.tensor_tensor(out=nlam[:], in0=s12[:, 1:2], in1=s12[:, 0:1], op=ALU.subtract), r=["d_s12"], w=["d_nlam"])
        P.dve(lambda e: e.tensor_scalar_add(out=nlam[:], in0=nlam[:], scalar1=-li), r=["d_nlam"], w=["d_nlam"])
        P.dve(lambda e: e.tensor_scalar_mul(out=gsub[:], in0=gsub[:], scalar1=1.0 - li), r=["d_gsub"], w=["d_gsub"])
        pts = [sb("d_pt%d" % i, [128, 1024], BF16) for i in range(3)]
        ring = Ring([p[:] for p in pts], "d_pt")
        obs = [sb("d_ob%d" % i, [128, 1024], F32) for i in range(2)]
        obufs = Ring([p[:] for p in obs], "d_ob")
        o1n = sb("d_o1n", [128, NT, 128], F32)
        ytok = sb("d_ytok", [128, NT, 512], BF16)
        rec = sb("d_rec", [128, 1], F32)
        osb = sb("d_osb", [128, 128], F32)
        junk = sb("d_junk", [128, 128], F32)
        ssN = sb("d_ssN", [128, NT], F32)
        stg = sb("d_stg", [128, 4, 512], BF16)
        for h in range(4):
            def bias_fn(st, qc, h=h):
                dlt = st - 4 * qc
                if -1 <= dlt <= 4:
                    return ("band", C.W2b[h][:, 512 - 128 * dlt:1024 - 128 * dlt], ("W2b", h))
                if dlt > 4:
                    return ("const", C.rbB[:, 31 * 4 + h:31 * 4 + h + 1], "rbB", 31)
                return ("const", C.rbB[:, 15 * 4 + h:15 * 4 + h + 1], "rbB", 15)
            for m in range(2):
                base = 64 * m
                QTh = QT[base:base + 64, h, :]
                KTh = KT[base:base + 64, h, :]
                if m == 0:
                    def out_cb(qc, qt, ps_ap, ps_key, h=h):
                        t = qc * 4 + qt
                        P.dve(lambda e: e.reciprocal(out=rec[:], in_=ps_ap[:, 128:129]), r=[ps_key], w=["d_rec"])
                        P.dve(lambda e: e.tensor_scalar(out=o1n[:, t, :], in0=ps_ap[:, 0:128], scalar1=rec[:, 0:1], scalar2=None, op0=ALU.mult),
                              r=[ps_key, "d_rec"], w=["d_o1n"])
                else:
                    def out_cb(qc, qt, ps_ap, ps_key, h=h):
                        t = qc * 4 + qt
                        P.dve(lambda e: e.reciprocal(out=rec[:], in_=ps_ap[:, 128:129]), r=[ps_key], w=["d_rec"])
                        P.dve(lambda e: e.tensor_tensor(out=rec[:], in0=rec[:], in1=nlam[:], op=ALU.mult), r=["d_rec", "d_nlam"], w=["d_rec"])
                        P.dve(lambda e: e.scalar_tensor_tensor(out=o1n[:, t, :], in0=ps_ap[:, 0:128], scalar=rec[:, 0:1], in1=o1n[:, t, :],
                                                               op0=ALU.mult, op1=ALU.add), r=[ps_key, "d_rec", "d_o1n"], w=["d_o1n"])
                attention(C, QTh, KTh, "d_QK", lambda st, h=h: V[:, st, h, :], "d_V", 128, 0.125, ring, out_cb, bias_fn=bias_fn, obufs=obufs)
            for t in range(NT):
                P.act(lambda e, t=t: e.activation(out=junk[:], in_=o1n[:, t, :], func=AF.Square, accum_out=ssN[:, t:t + 1]),
                      r=["d_o1n"], w=["d_junk", "d_ssN"])
            P.act(lambda e: e.activation(out=ssN[:], in_=ssN[:], func=AF.Sqrt, scale=1.0 / 128, bias=C.epsb[:]), r=["d_ssN"], w=["d_ssN"])
            P.dve(lambda e: e.reciprocal(out=ssN[:], in_=ssN[:]), r=["d_ssN"], w=["d_ssN"])
            P.dve(lambda e: e.tensor_tensor(out=o1n[:], in0=o1n[:], in1=ssN[:].unsqueeze(2).to_broadcast([128, NT, 128]), op=ALU.mult),
                  r=["d_o1n", "d_ssN"], w=["d_o1n"])
            P.dve(lambda e, h=h: e.tensor_tensor(out=ytok[:, :, h * 128:(h + 1) * 128], in0=o1n[:],
                                                 in1=gsub[:].unsqueeze(1).to_broadcast([128, NT, 128]), op=ALU.mult),
                  r=["d_o1n", "d_gsub"], w=["d_ytok"])
        store_T(C, ytok, C.ynT[3], 4, "d_ytok", stg, "d_stg")
        P.barrier(C.bar[:])


def store_T(C, ytok, dstT, ncc, ykey, stg, skey):
    P, NT = C.P, C.NT
    for t in range(NT):
        b = t % 2
        pt = C.pst[b]
        for j in range(ncc):
            P.pe(lambda e, pt=pt, j=j, t=t: e.transpose(pt[:, j * 128:(j + 1) * 128], ytok[:, t, j * 128:(j + 1) * 128], C.ident[:]),
                 r=[ykey], w=[("pst", b)])
        tt = t % 4
        P.act(lambda e, pt=pt, tt=tt: e.copy(out=stg[:, 0:ncc, tt * 128:(tt + 1) * 128],
                                             in_=pt[:, 0:ncc * 128].rearrange("p (j c) -> p j c", j=ncc)),
              r=[("pst", b)], w=[skey])
        if tt == 3:
            t0 = (t // 4) * 512
            P.dma("pool", dstT[:, t0:t0 + 512].rearrange("(j p) t -> p j t", p=128), stg[:, 0:ncc, :], r=[skey])


def phase_four(C, layer):
    P, nc, S, NT, NQC, I = C.P, C.nc, C.S, C.NT, C.NQC, C.I
    KC = 256
    with contextlib.ExitStack() as es:
        sb = lambda n, s, d: es.enter_context(nc.sbuf_tensor(_uniq(n), list(s), d))
        GCS = sb("a_GCS", [128, NT, 1024], BF16)
        with contextlib.ExitStack() as es2:
            sb2 = lambda n, s, d: es2.enter_context(nc.sbuf_tensor(_uniq(n), list(s), d))
            uT = sb2("a_uT", [128, 8, S], BF16)
            wA = sb2("a_w", [128, 8, 512], BF16)
            bd = sb2("a_bd", [128, 2, 128], BF16)
            aT = sb2("a_aT", [128, 4, S], BF16)
            load_T(C, uT, C.uT_d, 8, key="a_uT")
            load_w(C, wA[:], I["w_in"][layer, :, 0:512], "a_w")
            P.dma("sp", bd[:], I["bdcs"].rearrange("a p c -> p a c"), w=["a_bd"])
            k = 0
            for blk in range(4):
                for tc in range(NQC):
                    bi = k % 4
                    k += 1
                    ps = C.psb[bi // 2][:, (bi % 2) * 512:(bi % 2) * 512 + 512]
                    pk = ("psb", bi // 2, bi % 2)
                    for j in range(8):
                        P.pe(lambda e, ps=ps, j=j, blk=blk, tc=tc: e.matmul(
                            ps, lhsT=wA[:, j, blk * 128:(blk + 1) * 128], rhs=uT[:, j, tc * 512:(tc + 1) * 512],
                            start=(j == 0), stop=(j == 7)), r=["a_uT", "a_w"], w=[pk])
                    if k % 2 == 0:
                        P.act(lambda e, ps=ps, blk=blk, tc=tc: e.copy(out=aT[:, blk, tc * 512:(tc + 1) * 512], in_=ps), r=[pk], w=["a_aT"])
                    else:
                        P.dve(lambda e, ps=ps, blk=blk, tc=tc: e.tensor_copy(out=aT[:, blk, tc * 512:(tc + 1) * 512], in_=ps), r=[pk], w=["a_aT"])
            for st in range(NT):
                b = st % 2
                ps = C.psb[b]
                for cs in range(2):
                    for cc in range(4):
                        P.pe(lambda e, ps=ps, cs=cs, cc=cc, st=st: e.matmul(
                            ps[:, cs * 512 + cc * 128:cs * 512 + (cc + 1) * 128], lhsT=aT[:, cc, st * 128:(st + 1) * 128],
                            rhs=bd[:, cs, :], start=True, stop=True, skip_group_check=True), r=["a_aT", "a_bd"], w=[("psb", b, cs)])
                if st % 2 == 0:
                    P.act(lambda e, ps=ps, st=st: e.copy(out=GCS[:, st, :], in_=ps[:]), r=[("psb", b, 0), ("psb", b, 1)], w=["a_GCS"])
                else:
                    P.dve(lambda e, ps=ps, st=st: e.tensor_copy(out=GCS[:, st, :], in_=ps[:]), r=[("psb", b, 0), ("psb", b, 1)], w=["a_GCS"])
            P.barrier(C.bar[:])
        DC = [sb("a_DC%d" % i, [128, NT, KC], BF16) for i in range(2)]
        DS = [sb("a_DS%d" % i, [128, NT, KC], BF16) for i in range(2)]
        stg = [sb("a_stg%d" % i, [128, 4, KC], BF16) for i in range(2)]
        for kc in range(S // KC):
            b = kc % 2
            P.dma("sp", DC[b][:], I["dftc"][:, kc * KC:(kc + 1) * KC].rearrange("(t p) k -> p t k", p=128), w=[("a_DC", b)])
            P.dma("sp", DS[b][:], I["dfts"][:, kc * KC:(kc + 1) * KC].rearrange("(t p) k -> p t k", p=128), w=[("a_DS", b)])
            for cc in range(4):
                bi = cc
                ps = C.psb[bi // 2][:, (bi % 2) * 512:(bi % 2) * 512 + KC]
                pk = ("psb", bi // 2, bi % 2)
                for st in range(NT):
                    P.pe(lambda e, ps=ps, st=st, cc=cc, b=b: e.matmul(ps, lhsT=GCS[:, st, cc * 128:(cc + 1) * 128], rhs=DC[b][:, st, :],
                                                                     start=(st == 0), stop=False), r=["a_GCS", ("a_DC", b)], w=[pk])
                for st in range(NT):
                    P.pe(lambda e, ps=ps, st=st, cc=cc, b=b: e.matmul(ps, lhsT=GCS[:, st, 512 + cc * 128:512 + (cc + 1) * 128], rhs=DS[b][:, st, :],
                                                                     start=False, stop=(st == NT - 1)), r=["a_GCS", ("a_DS", b)], w=[pk])
                if cc % 2 == 0:
                    P.act(lambda e, ps=ps, cc=cc, b=b: e.copy(out=stg[b][:, cc, :], in_=ps), r=[pk], w=[("a_stg", b)])
                else:
                    P.dve(lambda e, ps=ps, cc=cc, b=b: e.tensor_copy(out=stg[b][:, cc, :], in_=ps), r=[pk], w=[("a_stg", b)])
            P.dma("pool", C.ynT[0][:, kc * KC:(kc + 1) * KC].rearrange("(j p) t -> p j t", p=128), stg[b][:], r=[("a_stg", b)])
        P.barrier(C.bar[:])


def phase_mlstm(C, layer):
    P, nc, S, NT, NQC, I = C.P, C.nc, C.S, C.NT, C.NQC, C.I
    X4 = NT * 4
    with contextlib.ExitStack() as es:
        sb = lambda n, s, d: es.enter_context(nc.sbuf_tensor(_uniq(n), list(s), d))
        uT = sb("c_uT", [128, 8, S], BF16)
        G = sb("c_G", [128, NT, 16], F32)
        wg = sb("c_wg", [128, 8, 16], BF16)
        biasB = sb("c_bias", [128, 16], F32)
        gC = sb("c_gC", [128, 512], F32)
        eq = [sb("c_eq%d" % d, [128, NT, 4], F32) for d in range(2)]
        ek = [sb("c_ek%d" % d, [128, NT, 4], F32) for d in range(2)]
        ekd = [sb("c_ekd%d" % d, [128, NT, 4], F32) for d in range(2)]
        dec = [sb("c_dec%d" % d, [128, NT, 4], F32) for d in range(2)]
        load_T(C, uT, C.uT_d, 8, key="c_uT")
        load_w(C, wg[:], I["w_in"][layer, :, COL["cg"]:COL["cg"] + 16], "c_wg")
        P.dma("sp", biasB[:], I["mlstm_gate_bias"][layer:layer + 1, :].partition_broadcast(128), w=["c_bias"])
        P.dma("sp", gC[:], I["mlstm_norm_g"][layer:layer + 1, :].partition_broadcast(128), w=["c_gC"])
        psg = C.psb[0][:, 0:NT * 16]
        for t in range(NT):
            for j in range(8):
                P.pe(lambda e, t=t, j=j: e.matmul(psg[:, t * 16:(t + 1) * 16], lhsT=uT[:, j, t * 128:(t + 1) * 128], rhs=wg[:, j, :],
                                                 start=(t == 0 and j == 0), stop=(j == 7), skip_group_check=True),
                     r=["c_uT", "c_wg"], w=[("psb", 0, 0)])
        P.dve(lambda e: e.tensor_tensor(out=G[:], in0=psg.rearrange("p (t k) -> p t k", k=16),
                                        in1=biasB[:].unsqueeze(1).to_broadcast([128, NT, 16]), op=ALU.add),
              r=[("psb", 0, 0), "c_bias"], w=["c_G"])
        with contextlib.ExitStack() as es2:
            sb2 = lambda n, s, d: es2.enter_context(nc.sbuf_tensor(_uniq(n), list(s), d))
            af = sb2("c_af", [128, NT, 4], F32)
            lf = sb2("c_lf", [128, NT, 4], F32)
            mn = sb2("c_mn", [128, NT, 4], F32)
            tm = sb2("c_tm", [128, NT, 4], F32)
            lfh = [sb2("c_lfh%d" % i, [128, NT, 4], BF16) for i in range(3)]
            for d in range(2):
                Fd = G[:, :, 4 + 8 * d:8 + 8 * d]
                Id = G[:, :, 8 * d:8 * d + 4]
                P.act(lambda e, Fd=Fd: e.activation(out=af[:], in_=Fd, func=AF.Abs), r=["c_G"], w=["c_af"])
                P.act(lambda e: e.activation(out=af[:], in_=af[:], func=AF.Exp, scale=-1.0), r=["c_af"], w=["c_af"])
                P.act(lambda e: e.activation(out=af[:], in_=af[:], func=AF.Ln, bias=C.onesf[:, 0:1]), r=["c_af"], w=["c_af"])
                P.dve(lambda e, Fd=Fd: e.tensor_single_scalar(out=mn[:], in_=Fd, scalar=0.0, op=ALU.min), r=["c_G"], w=["c_mn"])
                P.dve(lambda e: e.tensor_tensor(out=lf[:], in0=mn[:], in1=af[:], op=ALU.subtract), r=["c_mn", "c_af"], w=["c_lf"])
                tri = C.maskLb if d == 0 else C.maskUb
                pb = C.psb[1][:, 0:X4]
                pt_ = C.psb[1][:, 512:512 + X4]
                for sp in range(3):
                    P.dve(lambda e, sp=sp: e.tensor_copy(out=lfh[sp][:], in_=lf[:]), r=["c_lf"], w=[("c_lfh", sp)])
                    if sp < 2:
                        P.dve(lambda e, sp=sp: e.tensor_copy(out=mn[:], in_=lfh[sp][:]), r=[("c_lfh", sp)], w=["c_mn"])
                        P.dve(lambda e: e.tensor_tensor(out=lf[:], in0=lf[:], in1=mn[:], op=ALU.subtract), r=["c_lf", "c_mn"], w=["c_lf"])
                for sp in range(3):
                    l2 = lfh[sp][:].rearrange("p t h -> p (t h)")
                    P.pe(lambda e, tri=tri, pb=pb, l2=l2, sp=sp: e.matmul(pb, lhsT=tri[:], rhs=l2, start=(sp == 0), stop=(sp == 2)),
                         r=[("c_lfh", sp)], w=[("psb", 1, 0)])
                for sp in range(3):
                    l2 = lfh[sp][:].rearrange("p t h -> p (t h)")
                    P.pe(lambda e, pt_=pt_, l2=l2, sp=sp: e.matmul(pt_, lhsT=C.onesb[:], rhs=l2, start=(sp == 0), stop=(sp == 2)),
                         r=[("c_lfh", sp)], w=[("psb", 1, 1)])
                pb3 = pb.rearrange("p (t h) -> p t h", h=4)
                pt3 = pt_.rearrange("p (t h) -> p t h", h=4)
                P.act(lambda e, d=d, pb3=pb3: e.activation(out=eq[d][:], in_=pb3, func=AF.Exp), r=[("psb", 1, 0)], w=[("c_eq", d)])
                P.dve(lambda e, Id=Id, pb3=pb3: e.tensor_tensor(out=tm[:], in0=Id, in1=pb3, op=ALU.subtract), r=["c_G", ("psb", 1, 0)], w=["c_tm", ("psb", 1, 0)])
                P.act(lambda e, d=d: e.activation(out=ek[d][:], in_=tm[:], func=AF.Exp, bias=C.lnk[:]), r=["c_tm"], w=[("c_ek", d)])
                P.act(lambda e, d=d, pt3=pt3: e.activation(out=dec[d][:], in_=pt3, func=AF.Exp), r=[("psb", 1, 1)], w=[("c_dec", d)])
                P.dve(lambda e, d=d: e.tensor_tensor(out=ekd[d][:], in0=ek[d][:], in1=dec[d][:], op=ALU.mult),
                      r=[("c_ek", d), ("c_dec", d)], w=[("c_ekd", d)])
            P.barrier(C.bar[:])
        import os
        MSTOP = int(os.environ.get("MSTOP", "9"))
        if MSTOP == 1:
            return
        wh = sb("c_wh", [128, 8, 512], BF16)
        T4 = sb("c_T4", [128, 4, S], BF16)
        KD = sb("c_KD", [128, NT, 2, 128], BF16)
        V = sb("c_V", [128, NT, 129], BF16)
        SO = sb("c_SO", [128, NT, 128], BF16)
        HS = sb("c_HS", [128, NT, 128], F32)
        sc4 = [sb("c_sc4%d" % i, [128, 4, 128], BF16) for i in range(2)]
        Cst = [sb("c_Cst%d" % d, [128, 129], F32) for d in range(2)]
        Cbf = [sb("c_Cbf%d" % d, [128, 129], BF16) for d in range(2)]
        STm = [sb("c_STm%d" % d, [128, 128], BF16) for d in range(2)]
        den = [sb("c_den%d" % d, [128, 1], F32) for d in range(2)]
        jk = sb("c_jk", [128, 128], F32)
        ssb = sb("c_ss", [128, NT], F32)
        ytk = sb("c_ytk", [128, NT, 128], BF16)
        P.dve(lambda e: e.memset(V[:], 1.0), w=["c_V"])
        for h in range(4):
            for i, nm in enumerate(("cq", "ck", "cv", "co")):
                P.dma("pool", wh[:, :, i * 128:(i + 1) * 128],
                      I["w_in"][layer, :, COL[nm] + h * 128:COL[nm] + (h + 1) * 128].rearrange("(j p) n -> p j n", p=128), w=[("c_wh", i)])
            def ml_a(t, h=h):
                b = t % 2
                ps = C.psb[b][:, 0:512]
                pk = ("psb", b, 0)
                for j in range(8):
                    P.pe(lambda e, ps=ps, j=j, t=t: e.matmul(ps, lhsT=uT[:, j, t * 128:(t + 1) * 128], rhs=wh[:, j, :],
                                                            start=(j == 0), stop=(j == 7)), r=["c_uT"] + [("c_wh", i4) for i4 in range(4)], w=[pk])
            def ml_b(t, h=h):
                b = t % 2
                ps = C.psb[b][:, 0:512]
                pk = ("psb", b, 0)
                s4 = sc4[b]
                MSKIP = os.environ.get("MSKIP", "")
                for wi, (src, sc) in enumerate(((0, eq[0]), (0, eq[1]), (1, ek[0]), (1, ek[1]))):
                    if "a" in MSKIP:
                        break
                    P.dve(lambda e, ps=ps, s4=s4, wi=wi, src=src, sc=sc, t=t, h=h: e.tensor_scalar(
                        out=s4[:, wi, :], in0=ps[:, src * 128:(src + 1) * 128], scalar1=sc[:, t, h:h + 1], scalar2=None, op0=ALU.mult),
                        r=[pk, ("c_eq", 0), ("c_eq", 1), ("c_ek", 0), ("c_ek", 1)], w=[("c_sc4", b)])
                for d in range(2):
                    if "b" in MSKIP:
                        break
                    P.dve(lambda e, ps=ps, d=d, t=t, h=h: e.tensor_scalar(
                        out=KD[:, t, d, :], in0=ps[:, 128:256], scalar1=ekd[d][:, t, h:h + 1], scalar2=None, op0=ALU.mult),
                        r=[pk, ("c_ekd", d)], w=["c_KD"])
                if "c" not in MSKIP:
                    P.act(lambda e, ps=ps, t=t: e.copy(out=V[:, t, 0:128], in_=ps[:, 256:384]), r=[pk], w=["c_V", pk])
                if "d" not in MSKIP:
                    P.act(lambda e, ps=ps, t=t: e.activation(out=SO[:, t, :], in_=ps[:, 384:512], func=AF.Sigmoid), r=[pk], w=["c_SO", pk])
                pt = C.pst[b]
                if "e" in MSKIP:
                    return
                for wi in range(4):
                    P.pe(lambda e, pt=pt, wi=wi, s4=s4: e.transpose(pt[:, wi * 128:(wi + 1) * 128], s4[:, wi, :], C.ident[:]),
                         r=[("c_sc4", b)], w=[("pst", b)])
                P.act(lambda e, pt=pt, t=t: e.copy(out=T4[:, :, t * 128:(t + 1) * 128], in_=pt[:, 0:512].rearrange("p (j c) -> p j c", j=4)),
                      r=[("pst", b)], w=["c_T4"])
            ml_a(0)
            for t in range(NT):
                if t + 1 < NT:
                    ml_a(t + 1)
                ml_b(t)
            if MSTOP == 2:
                P.barrier(C.bar[:])
                return
            for d in range(2):
                P.dve(lambda e, d=d: e.memset(Cst[d][:], 0.0), w=[("c_Cst", d)])
                P.dve(lambda e, d=d: e.memset(Cbf[d][:], 0.0), w=[("c_Cbf", d)])
            written = set()
            def chain_vars(step, d):
                c = step if d == 0 else NT - 1 - step
                return dict(c=c, mask=(C.maskL if d == 0 else C.maskU),
                            qsT=T4[:, d, c * 128:(c + 1) * 128], ksT=T4[:, 2 + d, c * 128:(c + 1) * 128],
                            psA=C.psb[0][:, d * 512:d * 512 + 128], psB=C.psb[1][:, d * 512:d * 512 + 129],
                            psC=C.psb[2][:, d * 512:d * 512 + 129], kA=("psb", 0, d), kB=("psb", 1, d), kC=("psb", 2, d))

            def chain_pe1(step, h=h):
                for d in range(2):
                    v = chain_vars(step, d)
                    P.pe(lambda e, v=v: e.matmul(v["psA"], lhsT=v["ksT"], rhs=v["qsT"], start=True, stop=True), r=["c_T4"], w=[v["kA"]])
                    P.pe(lambda e, v=v, d=d: e.matmul(v["psC"], lhsT=KD[:, v["c"], d, :], rhs=V[:, v["c"], :], start=True, stop=True),
                         r=["c_KD", "c_V"], w=[v["kC"]])

            def chain_rest(step, h=h):
                for d in range(2):
                    v = chain_vars(step, d)
                    P.dve(lambda e, v=v, d=d: e.tensor_tensor(out=STm[d][:], in0=v["psA"], in1=v["mask"][:], op=ALU.mult),
                          r=[v["kA"]], w=[("c_STm", d)])
                for d in range(2):
                    v = chain_vars(step, d)
                    P.pe(lambda e, v=v, d=d: e.matmul(v["psB"], lhsT=v["qsT"], rhs=Cbf[d][:], start=True, stop=False),
                         r=["c_T4", ("c_Cbf", d)], w=[v["kB"]])
                    P.pe(lambda e, v=v, d=d: e.matmul(v["psB"], lhsT=STm[d][:], rhs=V[:, v["c"], :], start=False, stop=True),
                         r=[("c_STm", d), "c_V"], w=[v["kB"]])
                for d in range(2):
                    v = chain_vars(step, d)
                    c = v["c"]
                    psB, psC, kB, kC = v["psB"], v["psC"], v["kB"], v["kC"]
                    P.dve(lambda e, psC=psC, d=d, c=c, h=h: e.scalar_tensor_tensor(out=Cst[d][:], in0=Cst[d][:], scalar=dec[d][:, c, h:h + 1],
                                                                               in1=psC, op0=ALU.mult, op1=ALU.add),
                          r=[kC, ("c_Cst", d), ("c_dec", d)], w=[("c_Cst", d)])
                    P.act(lambda e, d=d: e.copy(out=Cbf[d][:], in_=Cst[d][:]), r=[("c_Cst", d), kB], w=[("c_Cbf", d)])
                    P.dve(lambda e, psB=psB, d=d: e.tensor_scalar_max(out=den[d][:], in0=psB[:, 128:129], scalar1=1.0), r=[kB], w=[("c_den", d)])
                    P.dve(lambda e, psB=psB, d=d: e.scalar_tensor_tensor(out=den[d][:], in0=psB[:, 128:129], scalar=-1.0, in1=den[d][:],
                                                                         op0=ALU.mult, op1=ALU.max), r=[kB, ("c_den", d)], w=[("c_den", d)])
                    P.dve(lambda e, d=d: e.reciprocal(out=den[d][:], in_=den[d][:]), r=[("c_den", d)], w=[("c_den", d)])
                    if c in written:
                        P.dve(lambda e, psB=psB, d=d, c=c: e.scalar_tensor_tensor(out=HS[:, c, :], in0=psB[:, 0:128], scalar=den[d][:, 0:1],
                                                                                 in1=HS[:, c, :], op0=ALU.mult, op1=ALU.add),
                              r=[kB, ("c_den", d), "c_HS"], w=["c_HS"])
                    else:
                        written.add(c)
                        P.dve(lambda e, psB=psB, d=d, c=c: e.tensor_scalar(out=HS[:, c, :], in0=psB[:, 0:128], scalar1=den[d][:, 0:1],
                                                                          scalar2=None, op0=ALU.mult), r=[kB, ("c_den", d)], w=["c_HS"])

            chain_pe1(0)
            for step in range(NT):
                chain_rest(step)
                if step + 1 < NT:
                    chain_pe1(step + 1)
            if MSTOP == 3:
                P.barrier(C.bar[:])
                return
            for t in range(NT):
                P.act(lambda e, t=t: e.activation(out=jk[:], in_=HS[:, t, :], func=AF.Square, accum_out=ssb[:, t:t + 1]),
                      r=["c_HS"], w=["c_jk", "c_ss"])
            P.act(lambda e: e.activation(out=ssb[:], in_=ssb[:], func=AF.Sqrt, scale=1.0 / 128, bias=C.epsb[:]), r=["c_ss"], w=["c_ss"])
            P.dve(lambda e: e.reciprocal(out=ssb[:], in_=ssb[:]), r=["c_ss"], w=["c_ss"])
            P.dve(lambda e: e.tensor_tensor(out=HS[:], in0=HS[:], in1=ssb[:].unsqueeze(2).to_broadcast([128, NT, 128]), op=ALU.mult),
                  r=["c_HS", "c_ss"], w=["c_HS"])
            P.dve(lambda e, h=h: e.tensor_tensor(out=HS[:], in0=HS[:], in1=gC[:, h * 128:(h + 1) * 128].unsqueeze(1).to_broadcast([128, NT, 128]),
                                                 op=ALU.mult), r=["c_HS", "c_gC"], w=["c_HS"])
            P.dve(lambda e: e.tensor_tensor(out=ytk[:], in0=HS[:], in1=SO[:], op=ALU.mult), r=["c_HS", "c_SO"], w=["c_ytk"])
            stgT = SO[:].rearrange("p t d -> p (t d)")
            for t in range(NT):
                b = (t // 8) % 2
                pt = C.pst[b]
                P.pe(lambda e, pt=pt, t=t: e.transpose(pt[:, (t % 8) * 128:(t % 8 + 1) * 128], ytk[:, t, :], C.ident[:]),
                     r=["c_ytk"], w=[("pst", b)])
                if t % 8 == 7 or t == NT - 1:
                    n8 = t % 8 + 1
                    t0 = (t // 8) * 8
                    P.act(lambda e, pt=pt, n8=n8, t0=t0: e.copy(out=stgT[:, t0 * 128:(t0 + n8) * 128], in_=pt[:, 0:n8 * 128]),
                          r=[("pst", b)], w=["c_SO"])
            P.dma("pool", C.ynT[2][h * 128:(h + 1) * 128, :], stgT, r=["c_SO"])
        P.barrier(C.bar[:])


def phase_merge(C, layer, xsrc):
    P, nc, S, NT, NQC, I = C.P, C.nc, C.S, C.NT, C.NQC, C.I
    with contextlib.ExitStack() as es:
        sb = lambda n, s, d: es.enter_context(nc.sbuf_tensor(_uniq(n), list(s), d))
        Wg = sb("m_Wg", [128, 8, 4096], BF16)
        Wbr = sb("m_Wbr", [128, 16, 1024], BF16)
        Wo = sb("m_Wo", [128, 8, 1024], BF16)
        uTc = [sb("m_uT%d" % i, [128, 8, 512], BF16) for i in range(1)] * 2
        yc = [sb("m_y%d" % i, [128, 16, 512], BF16) for i in range(1)] * 2
        mT = sb("m_mT", [128, 8, 512], BF16)
        sig = [sb("m_sig%d" % i, [128, 512], F32) for i in range(2)]
        acc = sb("m_acc", [128, 512], F32)
        tmp = sb("m_tmp", [128, 512], F32)
        xt = [sb("m_xt%d" % i, [128, 1024], F32) for i in range(2)]
        for n in range(4):
            P.dma("pool", Wg[:, :, n * 1024:(n + 1) * 1024],
                  I["w_in"][layer, :, COL["g"] + n * 1024:COL["g"] + (n + 1) * 1024].rearrange("(j p) n -> p j n", p=128), w=[("m_Wg", n)])
            P.dma("pool", Wbr[:, n * 4:(n + 1) * 4, :], I["w_branch"][layer, n].rearrange("(j p) n -> p j n", p=128), w=[("m_Wbr", n)])
        P.dma("pool", Wo[:], I["w_out"][layer].rearrange("(j p) n -> p j n", p=128), w=["m_Wo"])
        k = 0
        for tc in range(NQC):
            b = 0
            P.dma("sp", uTc[b][:], C.uT_d[:, tc * 512:(tc + 1) * 512].rearrange("(j p) t -> p j t", p=128), w=[("m_uT", b)])
            for n in range(4):
                P.dma("sp", yc[b][:, n * 4:(n + 1) * 4, :], C.ynT[n][:, tc * 512:(tc + 1) * 512].rearrange("(j p) t -> p j t", p=128),
                      w=[("m_y", b, n)])
            for j in range(8):
                for n in range(4):
                    pi = k % 2
                    k += 1
                    psG = C.psb[pi][:, 0:512]
                    psR = C.psb[pi][:, 512:1024]
                    kG, kR = ("psb", pi, 0), ("psb", pi, 1)
                    for dj in range(8):
                        P.pe(lambda e, psG=psG, dj=dj, n=n, j=j, b=b: e.matmul(
                            psG, lhsT=Wg[:, dj, n * 1024 + j * 128:n * 1024 + (j + 1) * 128], rhs=uTc[b][:, dj, :],
                            start=(dj == 0), stop=(dj == 7)), r=[("m_Wg", n), ("m_uT", b)], w=[kG])
                    for cc in range(4):
                        P.pe(lambda e, psR=psR, cc=cc, n=n, j=j, b=b: e.matmul(
                            psR, lhsT=Wbr[:, n * 4 + cc, j * 128:(j + 1) * 128], rhs=yc[b][:, n * 4 + cc, :],
                            start=(cc == 0), stop=(cc == 3)), r=[("m_Wbr", n), ("m_y", b, n)], w=[kR])
                    sg = sig[pi]
                    P.act(lambda e, sg=sg, psG=psG: e.activation(out=sg[:], in_=psG, func=AF.Sigmoid), r=[kG], w=[("m_sig", pi)])
                    if n == 0:
                        P.dve(lambda e, sg=sg, psR=psR: e.tensor_tensor(out=acc[:], in0=sg[:], in1=psR, op=ALU.mult),
                              r=[("m_sig", pi), kR], w=["m_acc"])
                    elif n < 3:
                        P.dve(lambda e, sg=sg, psR=psR: e.tensor_tensor(out=tmp[:], in0=sg[:], in1=psR, op=ALU.mult),
                              r=[("m_sig", pi), kR], w=["m_tmp"])
                        P.dve(lambda e: e.tensor_tensor(out=acc[:], in0=acc[:], in1=tmp[:], op=ALU.add), r=["m_acc", "m_tmp"], w=["m_acc"])
                    else:
                        P.dve(lambda e, sg=sg, psR=psR: e.tensor_tensor(out=tmp[:], in0=sg[:], in1=psR, op=ALU.mult),
                              r=[("m_sig", pi), kR], w=["m_tmp"])
                        P.dve(lambda e, j=j: e.tensor_tensor(out=mT[:, j, :], in0=acc[:], in1=tmp[:], op=ALU.add),
                              r=["m_acc", "m_tmp"], w=["m_mT"])
            for tt in range(4):
                t = tc * 4 + tt
                xb = t % 2
                P.dma("sp", xt[xb][:], xsrc[t * 128:(t + 1) * 128, :], w=[("m_xt", xb)])
                ps = C.psb[2]
                for half in range(2):
                    for dj in range(8):
                        P.pe(lambda e, ps=ps, half=half, dj=dj, tt=tt: e.matmul(
                            ps[:, half * 512:(half + 1) * 512], lhsT=mT[:, dj, tt * 128:(tt + 1) * 128], rhs=Wo[:, dj, half * 512:(half + 1) * 512],
                            start=(dj == 0), stop=(dj == 7)), r=["m_mT", "m_Wo"], w=[("psb", 2, half)])
                P.dve(lambda e, ps=ps, xb=xb: e.tensor_tensor(out=xt[xb][:], in0=xt[xb][:], in1=ps[:], op=ALU.add),
                      r=[("m_xt", xb), ("psb", 2, 0), ("psb", 2, 1)], w=[("m_xt", xb)])
                P.dma("pool", C.xres[t * 128:(t + 1) * 128, :], xt[xb][:], r=[("m_xt", xb)])
        P.barrier(C.bar[:])


def phase_ffn(C, layer):
    P, nc, S, NT, NQC, I = C.P, C.nc, C.S, C.NT, C.NQC, C.I
    TB = min(1024, S)
    NCC = DFF // 128
    NH = NCC // 2
    NB = S // TB
    with contextlib.ExitStack() as es:
        sb = lambda n, s, d: es.enter_context(nc.sbuf_tensor(_uniq(n), list(s), d))
        WuA = sb("f_WuA", [128, 8, NH * 128], BF16)
        WuL = sb("f_WuL", [128, 8, NH * 128], BF16)
        Wd = sb("f_Wd", [128, NH, 1024], BF16)
        cw = sb("f_cw", [128, NCC, 3], F32)
        cb = sb("f_cb", [128, NCC], F32)
        hT = sb("f_hT", [128, NH, TB], BF16)
        vTc = [sb("f_vT%d" % i, [128, 8, TB + 2], BF16) for i in range(2)]
        aS = [sb("f_aS%d" % i, [128, TB + 2], F32) for i in range(2)]
        t1 = [sb("f_t1%d" % i, [128, TB], F32) for i in range(2)]
        z2 = [sb("f_z2%d" % i, [128, TB], F32) for i in range(2)]
        sg = [sb("f_sg%d" % i, [128, TB], F32) for i in range(2)]
        xt = [sb("f_xt%d" % i, [128, 1024], F32) for i in range(2)]
        for j in range(3):
            P.dma("sp", cw[:, :, j:j + 1], I["conv_w"][layer, j:j + 1, :].rearrange("o (c p) -> p c o", p=128), w=["f_cw"],
                  allow_slow_non_contiguous=True)
        P.dma("sp", cb[:].unsqueeze(2), I["conv_b"][layer:layer + 1, :].rearrange("o (c p) -> p c o", p=128), w=["f_cb"],
              allow_slow_non_contiguous=True)
        k = 0
        kv = 0
        for hp in range(2):
            c_lo = hp * NH
            for q4 in range(0, NH * 128, 512):
                n = min(512, NH * 128 - q4)
                P.dma("pool", WuA[:, :, q4:q4 + n],
                      I["w_up"][layer, :, c_lo * 128 + q4:c_lo * 128 + q4 + n].rearrange("(j p) n -> p j n", p=128), w=[("f_WuA", q4 // 512)])
                P.dma("pool", WuL[:, :, q4:q4 + n],
                      I["w_up"][layer, :, DFF + c_lo * 128 + q4:DFF + c_lo * 128 + q4 + n].rearrange("(j p) n -> p j n", p=128), w=[("f_WuL", q4 // 512)])
            P.dma("pool", Wd[:], I["w_down"][layer, c_lo * 128:(c_lo + NH) * 128, :].rearrange("(j p) n -> p j n", p=128), w=["f_Wd"])
            for blk in range(NB):
                t0 = blk * TB
                vb = kv % 2
                kv += 1
                vt = vTc[vb]
                lo = max(t0 - 1, 0)
                hi = min(t0 + TB + 1, S)
                c0 = lo - (t0 - 1)
                if t0 == 0:
                    P.dve(lambda e, vt=vt: e.memset(vt[:, :, 0:1], 0.0), w=[("f_vT", vb)])
                if t0 + TB == S:
                    P.dve(lambda e, vt=vt: e.memset(vt[:, :, TB + 1:TB + 2], 0.0), w=[("f_vT", vb)])
                P.dma("sp", vt[:, :, c0:c0 + (hi - lo)], C.uT_d[:, lo:hi].rearrange("(j p) t -> p j t", p=128), w=[("f_vT", vb)])
                pieces = [(0, 512), (512, 512), (1024, 2)] if TB == 1024 else [(0, 512), (512, 2)]
                def stage1(ci, b):
                    cc = c_lo + ci
                    a = aS[b]
                    for pi, (p0, n) in enumerate(pieces):
                        bi = pi % 3
                        ps = C.psb[bi // 2][:, (bi % 2) * 512:(bi % 2) * 512 + n]
                        pk = ("psb", bi // 2, bi % 2)
                        for j in range(8):
                            P.pe(lambda e, ps=ps, j=j, p0=p0, n=n, ci=ci, vt=vt: e.matmul(
                                ps, lhsT=WuA[:, j, ci * 128:(ci + 1) * 128], rhs=vt[:, j, p0:p0 + n],
                                start=(j == 0), stop=(j == 7)), r=[("f_WuA", ci // 4), ("f_vT", vb)], w=[pk])
                        P.act(lambda e, ps=ps, a=a, p0=p0, n=n: e.copy(out=a[:, p0:p0 + n], in_=ps), r=[pk], w=[("f_aS", b)])
                    T1, Z2 = t1[b], z2[b]
                    P.dve(lambda e, a=a, cc=cc, T1=T1: e.tensor_scalar(out=T1[:], in0=a[:, 1:TB + 1], scalar1=cw[:, cc, 1:2], scalar2=cb[:, cc:cc + 1],
                                                                       op0=ALU.mult, op1=ALU.add), r=[("f_aS", b), "f_cw", "f_cb"], w=[("f_t1", b)])
                    P.dve(lambda e, a=a, cc=cc, T1=T1: e.scalar_tensor_tensor(out=T1[:], in0=a[:, 0:TB], scalar=cw[:, cc, 0:1], in1=T1[:],
                                                                              op0=ALU.mult, op1=ALU.add), r=[("f_aS", b), "f_cw", ("f_t1", b)], w=[("f_t1", b)])
                    P.dve(lambda e, a=a, cc=cc, T1=T1: e.scalar_tensor_tensor(out=T1[:], in0=a[:, 2:TB + 2], scalar=cw[:, cc, 2:3], in1=T1[:],
                                                                              op0=ALU.mult, op1=ALU.add), r=[("f_aS", b), "f_cw", ("f_t1", b)], w=[("f_t1", b)])
                    P.pool(lambda e, T1=T1, Z2=Z2: e.tensor_tensor(out=Z2[:], in0=T1[:], in1=T1[:], op=ALU.mult), r=[("f_t1", b)], w=[("f_z2", b)])
                    P.pool(lambda e, Z2=Z2: e.tensor_scalar(out=Z2[:], in0=Z2[:], scalar1=0.044715, scalar2=1.0, op0=ALU.mult, op1=ALU.add),
                           r=[("f_z2", b)], w=[("f_z2", b)])
                    P.pool(lambda e, T1=T1, Z2=Z2: e.tensor_tensor(out=Z2[:], in0=Z2[:], in1=T1[:], op=ALU.mult), r=[("f_z2", b), ("f_t1", b)], w=[("f_z2", b)])

                def stage2(ci, b):
                    T1, Z2, SG = t1[b], z2[b], sg[b]
                    P.act(lambda e, Z2=Z2, SG=SG: e.activation(out=SG[:], in_=Z2[:], func=AF.Sigmoid, scale=1.5957691216057308),
                          r=[("f_z2", b)], w=[("f_sg", b)])
                    P.dve(lambda e, T1=T1, SG=SG: e.tensor_tensor(out=SG[:], in0=SG[:], in1=T1[:], op=ALU.mult), r=[("f_sg", b), ("f_t1", b)], w=[("f_sg", b)])
                    for li in range(TB // 512):
                        bi = 3 + (li % 2)
                        ps = C.psb[bi // 2][:, (bi % 2) * 512:(bi % 2) * 512 + 512]
                        pk = ("psb", bi // 2, bi % 2)
                        for j in range(8):
                            P.pe(lambda e, ps=ps, j=j, li=li, ci=ci, vt=vt: e.matmul(
                                ps, lhsT=WuL[:, j, ci * 128:(ci + 1) * 128], rhs=vt[:, j, 1 + li * 512:1 + (li + 1) * 512],
                                start=(j == 0), stop=(j == 7)), r=[("f_WuL", ci // 4), ("f_vT", vb)], w=[pk])
                        P.dve(lambda e, ps=ps, li=li, ci=ci, SG=SG: e.tensor_tensor(out=hT[:, ci, li * 512:(li + 1) * 512], in0=SG[:, li * 512:(li + 1) * 512],
                                                                                    in1=ps, op=ALU.mult), r=[("f_sg", b), pk], w=["f_hT"])

                bs = []
                for ci in range(NH):
                    bs.append(k % 2)
                    k += 1
                stage1(0, bs[0])
                for ci in range(NH):
                    if ci + 1 < NH:
                        stage1(ci + 1, bs[ci + 1])
                    stage2(ci, bs[ci])
                for tt in range(TB // 128):
                    t = t0 // 128 + tt
                    xb = t % 2
                    P.dma("sp", xt[xb][:], C.xres[t * 128:(t + 1) * 128, :], w=[("f_xt", xb)])
                    ps = C.psb[2]
                    for half in range(2):
                        for ci in range(NH):
                            P.pe(lambda e, ps=ps, half=half, ci=ci, tt=tt: e.matmul(
                                ps[:, half * 512:(half + 1) * 512], lhsT=hT[:, ci, tt * 128:(tt + 1) * 128], rhs=Wd[:, ci, half * 512:(half + 1) * 512],
                                start=(ci == 0), stop=(ci == NH - 1)), r=["f_hT", "f_Wd"], w=[("psb", 2, half)])
                    P.dve(lambda e, ps=ps, xb=xb: e.tensor_tensor(out=xt[xb][:], in0=xt[xb][:], in1=ps[:], op=ALU.add),
                          r=[("f_xt", xb), ("psb", 2, 0), ("psb", 2, 1)], w=[("f_xt", xb)])
                    P.dma("pool", C.xres[t * 128:(t + 1) * 128, :], xt[xb][:], r=[("f_xt", xb)])
        P.barrier(C.bar[:])


def phase_out(C, dst):
    P, nc, S, NT, I = C.P, C.nc, C.S, C.NT, C.I
    last = []
    with contextlib.ExitStack() as es:
        sb = lambda n, s, d: es.enter_context(nc.sbuf_tensor(_uniq(n), list(s), d))
        gB = sb("o_gB", [128, D], F32)
        xt = [sb("o_xt%d" % i, [128, D], F32) for i in range(2)]
        yo = [sb("o_y%d" % i, [128, D], F32) for i in range(2)]
        junk = sb("o_junk", [128, D], F32)
        ssq = [sb("o_ssq%d" % i, [128, 1], F32) for i in range(2)]
        P.dma("sp", gB[:], I["final_norm_g"].partition_broadcast(128), w=["o_gB"])
        for t in range(NT):
            b = t % 2
            P.dma("sp", xt[b][:], C.xres[t * 128:(t + 1) * 128, :], w=[("o_xt", b)])
            P.act(lambda e, b=b: e.activation(out=junk[:], in_=xt[b][:], func=AF.Square, accum_out=ssq[b][:]),
                  r=[("o_xt", b)], w=["o_junk", ("o_ssq", b)])
            P.act(lambda e, b=b: e.activation(out=ssq[b][:], in_=ssq[b][:], func=AF.Sqrt, scale=1.0 / D, bias=C.epsb[:]),
                  r=[("o_ssq", b)], w=[("o_ssq", b)])
            P.dve(lambda e, b=b: e.reciprocal(out=ssq[b][:], in_=ssq[b][:]), r=[("o_ssq", b)], w=[("o_ssq", b)])
            P.dve(lambda e, b=b: e.scalar_tensor_tensor(out=yo[b][:], in0=xt[b][:], scalar=ssq[b][:, 0:1], in1=gB[:],
                                                        op0=ALU.mult, op1=ALU.mult),
                  r=[("o_xt", b), ("o_ssq", b), "o_gB"], w=[("o_y", b)])
            last.append(P.dma("pool", dst[t * 128:(t + 1) * 128, :], yo[b][:], r=[("o_y", b)]))
        P.barrier(C.bar[:])
    return last


_CACHE = {}


def kernel(**inputs):
    S = SEQ
    inp = {k: np.asarray(v) for k, v in inputs.items()}
    nb = inp["x"].shape[0]
    nc, st = build(S, DEPTH, stage="full")
    consts = host_consts(S)
    in_maps = [make_in_map(inp, inp["x"][b], S, consts) for b in range(nb)]
    res = run_bass_kernel_spmd(nc, in_maps, core_ids=list(range(nb)))
    out = np.stack([np.asarray(r["out"], dtype=np.float32) for r in res.results], axis=0)
    return out


def make_in_map(inp, x, S, consts=None):
    c = consts if consts is not None else host_consts(S)
    f = lambda a: np.ascontiguousarray(np.asarray(a, dtype=np.float32))
    m = {
        "x": f(x),
        "norm_mix_g": f(inp["norm_mix_g"]),
        "w_in": f(inp["w_in"]),
        "mlstm_gate_bias": f(inp["mlstm_gate_bias"]).reshape(DEPTH, 16),
        "qk_norm_g": f(inp["qk_norm_g"]).reshape(DEPTH, 128),
        "mlstm_norm_g": f(inp["mlstm_norm_g"]),
        "diff_lambda": f(inp["diff_lambda"]).reshape(DEPTH, 256),
        "diff_norm_g": f(inp["diff_norm_g"]),
        "rel_bias": f(inp["rel_bias"]).reshape(1, 128),
        "w_branch": f(inp["w_branch"]),
        "w_out": f(inp["w_out"]),
        "norm_ffn_g": f(inp["norm_ffn_g"]),
        "w_up": f(inp["w_up"]),
        "conv_w": f(inp["conv_w"]),
        "conv_b": f(inp["conv_b"]),
        "w_down": f(inp["w_down"]),
        "final_norm_g": f(inp["final_norm_g"]).reshape(1, D),
    }
    m.update(c)
    return m
```
